# Optimizing a Trainium2 kernel written in Bass

```python
import math
import jax
import jax.numpy as jnp
from jax import lax
import numpy as np

D_MODEL = 1024
BATCH = 1
SEQ = 16384
DEPTH = 4

GRID_W = 64
CTX_LEN = 256
N_MIXERS = 4
NORM_EPS = 1e-6
ROPE_THETA = 10000.0
HEAD_DIM = 64
BLOCK = 128
NEG_INF = -1e30

DIFF_HEADS = D_MODEL // (2 * HEAD_DIM)
DIFF_V_DIM = 2 * HEAD_DIM
DIFF_QK_WIDTH = 2 * DIFF_HEADS * HEAD_DIM
DIFF_V_WIDTH = DIFF_HEADS * DIFF_V_DIM
SHORT_CONV_W = 3
WIN_Q_HEADS = D_MODEL // HEAD_DIM
WIN_KV_HEADS = 4
WIN_GROUP = WIN_Q_HEADS // WIN_KV_HEADS
WINDOW = 128
SPAN = BLOCK + 2 * WINDOW
CONF_CONV_W = 31
D_FF = -(-(8 * D_MODEL) // (3 * 256)) * 256

kernel_name = "hybrid_interleaved_diffusion_block"


def _n_layers_of(kind):
    return (DEPTH - kind + N_MIXERS - 1) // N_MIXERS


def rmsnorm(x, g):
    xf = x.astype(jnp.float32)
    y = xf * lax.rsqrt(jnp.mean(xf * xf, axis=-1, keepdims=True) + NORM_EPS)
    return (y * g.astype(jnp.float32)).astype(x.dtype)


def layernorm(x, g, b):
    xf = x.astype(jnp.float32)
    mu = jnp.mean(xf, axis=-1, keepdims=True)
    var = jnp.mean(jnp.square(xf - mu), axis=-1, keepdims=True)
    y = (xf - mu) * lax.rsqrt(var + NORM_EPS)
    return (y * g.astype(jnp.float32) + b.astype(jnp.float32)).astype(x.dtype)


def rope_tables(n):
    n_rows = n // GRID_W
    rows = jnp.broadcast_to(jnp.arange(n_rows, dtype=jnp.float32)[:, None], (n_rows, GRID_W)).reshape(-1)
    cols = jnp.broadcast_to(jnp.arange(GRID_W, dtype=jnp.float32)[None, :], (n_rows, GRID_W)).reshape(-1)
    n_freq = HEAD_DIM // 4
    inv_freq = ROPE_THETA ** (-jnp.arange(n_freq, dtype=jnp.float32) / n_freq)
    ang_r = rows[:, None] * inv_freq[None, :]
    ang_c = cols[:, None] * inv_freq[None, :]
    ang = jnp.concatenate([ang_r, ang_r, ang_c, ang_c], axis=-1)
    return jnp.cos(ang), jnp.sin(ang)


def apply_rope(x, cos, sin):
    x1, x2, x3, x4 = jnp.split(x, 4, axis=-1)
    rot = jnp.concatenate([-x2, x1, -x4, x3], axis=-1)
    shape = (1, x.shape[1]) + (1,) * (x.ndim - 3) + (x.shape[-1],)
    y = x.astype(jnp.float32) * cos.reshape(shape) + rot.astype(jnp.float32) * sin.reshape(shape)
    return y.astype(x.dtype)


def dwconv(x, w):
    k = w.shape[0]
    return lax.conv_general_dilated(
        x, w[:, None, :].astype(x.dtype), window_strides=(1,), padding=[(k // 2, k // 2)],
        dimension_numbers=("NWC", "WIO", "NWC"), feature_group_count=x.shape[-1])


def _diff_attend(q, k, v, lam):
    s = jnp.einsum("bqmhd,bkmhd->bmhqk", q, k, preferred_element_type=jnp.float32) * (HEAD_DIM ** -0.5)
    p = jax.nn.softmax(s, axis=-1)
    a = (p[:, 0] - lam * p[:, 1]).astype(v.dtype)
    return jnp.einsum("bhqk,bkhd->bqhd", a, v, preferred_element_type=jnp.float32)


def diff_attention(h_lat, h_ctx, w_qkv, w_o, q_norm, k_norm, lam_q1, lam_k1, lam_q2, lam_k2,
                   subln, lam_init, cos, sin, with_ctx):
    bsz, n, _ = h_lat.shape

    def project(h, rotary):
        q, k, v = jnp.split(h @ w_qkv, [DIFF_QK_WIDTH, 2 * DIFF_QK_WIDTH], axis=-1)
        q = rmsnorm(q.reshape(bsz, -1, 2, DIFF_HEADS, HEAD_DIM), q_norm)
        k = rmsnorm(k.reshape(bsz, -1, 2, DIFF_HEADS, HEAD_DIM), k_norm)
        if rotary:
            q, k = apply_rope(q, cos, sin), apply_rope(k, cos, sin)
        return q, k, v.reshape(bsz, -1, DIFF_HEADS, DIFF_V_DIM)

    lam = (jnp.exp(jnp.sum(lam_q1.astype(jnp.float32) * lam_k1.astype(jnp.float32)))
           - jnp.exp(jnp.sum(lam_q2.astype(jnp.float32) * lam_k2.astype(jnp.float32))) + lam_init)
    q_l, k_l, v_l = project(h_lat, True)
    q_c, k_c, v_c = project(h_ctx, False)
    k_all = jnp.concatenate([k_l, k_c], axis=1)
    v_all = jnp.concatenate([v_l, v_c], axis=1)
    nb = n // BLOCK
    q_blocks = jnp.moveaxis(q_l.reshape(bsz, nb, BLOCK, 2, DIFF_HEADS, HEAD_DIM), 1, 0)
    o_blocks = lax.map(lambda qb: _diff_attend(qb, k_all, v_all, lam), q_blocks)
    o_l = jnp.moveaxis(o_blocks, 0, 1).reshape(bsz, n, DIFF_HEADS, DIFF_V_DIM)

    def out(o):
        o = rmsnorm(o, subln) * (1.0 - lam_init)
        return o.reshape(bsz, -1, DIFF_V_WIDTH).astype(h_lat.dtype) @ w_o

    y_c = out(_diff_attend(q_c, k_c, v_c, lam)) if with_ctx else None
    return out(o_l), y_c


def short_conv(h, w_in, conv_w, w_out):
    b_gate, c_gate, u = jnp.split(h @ w_in, 3, axis=-1)
    return (b_gate * dwconv(c_gate * u, conv_w)) @ w_out


def _sink_attend(scores, values, sink):
    b, kv, g, q, _ = scores[0].shape
    sink_col = jnp.broadcast_to(sink.astype(jnp.float32)[None, :, :, None, None], (b, kv, g, q, 1))
    p = jax.nn.softmax(jnp.concatenate(scores + [sink_col], axis=-1), axis=-1)
    out = 0.0
    start = 0
    for s, v in zip(scores, values):
        kn = s.shape[-1]
        out = out + jnp.einsum("bkgqj,bjkd->bqkgd", p[..., start:start + kn].astype(v.dtype), v,
                               preferred_element_type=jnp.float32)
        start += kn
    return out


def window_gqa(h_lat, h_ctx, w_qkv, w_o, q_norm, k_norm, sink, cos, sin, with_ctx):
    bsz, n, _ = h_lat.shape
    qw = WIN_Q_HEADS * HEAD_DIM
    kw = WIN_KV_HEADS * HEAD_DIM
    scale = HEAD_DIM ** -0.5

    def project(h, rotary):
        q, k, v = jnp.split(h @ w_qkv, [qw, qw + kw], axis=-1)
        q = rmsnorm(q.reshape(bsz, -1, WIN_KV_HEADS, WIN_GROUP, HEAD_DIM), q_norm)
        k = rmsnorm(k.reshape(bsz, -1, WIN_KV_HEADS, HEAD_DIM), k_norm)
        if rotary:
            q, k = apply_rope(q, cos, sin), apply_rope(k, cos, sin)
        return q, k, v.reshape(bsz, -1, WIN_KV_HEADS, HEAD_DIM)

    sink_kg = sink.reshape(WIN_KV_HEADS, WIN_GROUP)
    q_l, k_l, v_l = project(h_lat, True)
    q_c, k_c, v_c = project(h_ctx, False)
    pad = ((0, 0), (WINDOW, WINDOW), (0, 0), (0, 0))
    k_pad, v_pad = jnp.pad(k_l, pad), jnp.pad(v_l, pad)
    nb = n // BLOCK

    def block(args):
        qb, bi = args
        start = bi * BLOCK
        kb = lax.dynamic_slice_in_dim(k_pad, start, SPAN, axis=1)
        vb = lax.dynamic_slice_in_dim(v_pad, start, SPAN, axis=1)
        s_win = jnp.einsum("bqkgd,bjkd->bkgqj", qb, kb, preferred_element_type=jnp.float32) * scale
        kpos = start - WINDOW + jnp.arange(SPAN)
        qpos = start + jnp.arange(BLOCK)
        valid = ((jnp.abs(kpos[None, :] - qpos[:, None]) <= WINDOW)
                 & (kpos[None, :] >= 0) & (kpos[None, :] < n))
        s_win = jnp.where(valid, s_win, NEG_INF)
        s_ctx = jnp.einsum("bqkgd,bjkd->bkgqj", qb, k_c, preferred_element_type=jnp.float32) * scale
        return _sink_attend([s_win, s_ctx], [vb, v_c], sink_kg)

    q_blocks = jnp.moveaxis(q_l.reshape(bsz, nb, BLOCK, WIN_KV_HEADS, WIN_GROUP, HEAD_DIM), 1, 0)
    o_blocks = lax.map(block, (q_blocks, jnp.arange(nb)))
    o_l = jnp.moveaxis(o_blocks, 0, 1)

    def out(o):
        return o.reshape(bsz, -1, D_MODEL).astype(h_lat.dtype) @ w_o

    if with_ctx:
        s_cc = jnp.einsum("bqkgd,bjkd->bkgqj", q_c, k_c, preferred_element_type=jnp.float32) * scale
        y_c = out(_sink_attend([s_cc], [v_c], sink_kg))
    else:
        y_c = None
    return out(o_l), y_c


def conformer_conv(h, w_pw1, b_pw1, dw_w, dw_b, ln_g, ln_b, w_pw2, b_pw2):
    a, g = jnp.split(h @ w_pw1 + b_pw1, 2, axis=-1)
    u = dwconv(a * jax.nn.sigmoid(g), dw_w) + dw_b
    u = jax.nn.silu(layernorm(u, ln_g, ln_b))
    return u @ w_pw2 + b_pw2


def swiglu(h, w_gate_up, w_down):
    g, u = jnp.split(h @ w_gate_up, 2, axis=-1)
    return (jax.nn.silu(g) * u) @ w_down


def setup_inputs(seed: int = 0) -> dict:
    key = jax.random.key(seed)
    keys = iter(jax.random.split(key, 48))

    def nrm(shape, std):
        return jax.random.normal(next(keys), shape, jnp.float32) * std

    def gain(shape):
        return 1.0 + nrm(shape, 0.05)

    d = D_MODEL
    n_a, n_b, n_c, n_d = (_n_layers_of(k) for k in range(N_MIXERS))
    return {
        "x": nrm((BATCH, SEQ, d), 1.0),
        "c": nrm((BATCH, d), 1.0),
        "ctx": nrm((BATCH, CTX_LEN, d), 1.0),
        "c_ctx": nrm((d,), 1.0),
        "ada_w": nrm((DEPTH, d, 6 * d), 0.5 * d ** -0.5),
        "ada_b": nrm((DEPTH, 6 * d), 0.02),
        "norm1": gain((DEPTH, d)),
        "norm2": gain((DEPTH, d)),
        "ffn_w_gate_up": nrm((DEPTH, d, 2 * D_FF), d ** -0.5),
        "ffn_w_down": nrm((DEPTH, D_FF, d), D_FF ** -0.5),
        "diff_w_qkv": nrm((n_a, d, 2 * DIFF_QK_WIDTH + DIFF_V_WIDTH), d ** -0.5),
        "diff_w_o": nrm((n_a, DIFF_V_WIDTH, d), DIFF_V_WIDTH ** -0.5),
        "diff_q_norm": gain((n_a, HEAD_DIM)),
        "diff_k_norm": gain((n_a, HEAD_DIM)),
        "diff_lam_q1": nrm((n_a, HEAD_DIM), 0.1),
        "diff_lam_k1": nrm((n_a, HEAD_DIM), 0.1),
        "diff_lam_q2": nrm((n_a, HEAD_DIM), 0.1),
        "diff_lam_k2": nrm((n_a, HEAD_DIM), 0.1),
        "diff_subln": gain((n_a, DIFF_V_DIM)),
        "sc_w_in": nrm((n_b, d, 3 * d), d ** -0.5),
        "sc_conv_w": nrm((n_b, SHORT_CONV_W, d), SHORT_CONV_W ** -0.5),
        "sc_w_out": nrm((n_b, d, d), d ** -0.5),
        "win_w_qkv": nrm((n_c, d, (WIN_Q_HEADS + 2 * WIN_KV_HEADS) * HEAD_DIM), d ** -0.5),
        "win_w_o": nrm((n_c, d, d), d ** -0.5),
        "win_q_norm": gain((n_c, HEAD_DIM)),
        "win_k_norm": gain((n_c, HEAD_DIM)),
        "win_sink": nrm((n_c, WIN_Q_HEADS), 0.5),
        "cf_w_pw1": nrm((n_d, d, 2 * d), d ** -0.5),
        "cf_b_pw1": nrm((n_d, 2 * d), 0.01),
        "cf_dw_w": nrm((n_d, CONF_CONV_W, d), CONF_CONV_W ** -0.5),
        "cf_dw_b": nrm((n_d, d), 0.01),
        "cf_ln_g": gain((n_d, d)),
        "cf_ln_b": nrm((n_d, d), 0.01),
        "cf_w_pw2": nrm((n_d, d, d), d ** -0.5),
        "cf_b_pw2": nrm((n_d, d), 0.01),
    }


def reference(x, c, ctx, c_ctx, ada_w, ada_b, norm1, norm2, ffn_w_gate_up, ffn_w_down,
              diff_w_qkv, diff_w_o, diff_q_norm, diff_k_norm, diff_lam_q1, diff_lam_k1,
              diff_lam_q2, diff_lam_k2, diff_subln,
              sc_w_in, sc_conv_w, sc_w_out,
              win_w_qkv, win_w_o, win_q_norm, win_k_norm, win_sink,
              cf_w_pw1, cf_b_pw1, cf_dw_w, cf_dw_b, cf_ln_g, cf_ln_b, cf_w_pw2, cf_b_pw2):
    n = x.shape[1]
    cos, sin = rope_tables(n)
    c_act = jax.nn.silu(c)
    cc_act = jax.nn.silu(c_ctx)
    h, hc = x, ctx
    for i in range(DEPTH):
        kind, j = i % N_MIXERS, i // N_MIXERS
        with_ctx = i < DEPTH - 1
        mod = (c_act @ ada_w[i] + ada_b[i])[:, None, :]
        mod_c = (cc_act @ ada_w[i] + ada_b[i])[None, None, :]
        sh1, sc1, g1, sh2, sc2, g2 = jnp.split(mod, 6, axis=-1)
        csh1, csc1, cg1, csh2, csc2, cg2 = jnp.split(mod_c, 6, axis=-1)
        a = rmsnorm(h, norm1[i]) * (1.0 + sc1) + sh1
        if with_ctx or kind in (0, 2):
            ac = rmsnorm(hc, norm1[i]) * (1.0 + csc1) + csh1
        if kind == 0:
            lam_init = 0.8 - 0.6 * math.exp(-0.3 * i)
            y, yc = diff_attention(a, ac, diff_w_qkv[j], diff_w_o[j], diff_q_norm[j], diff_k_norm[j],
                                   diff_lam_q1[j], diff_lam_k1[j], diff_lam_q2[j], diff_lam_k2[j],
                                   diff_subln[j], lam_init, cos, sin, with_ctx)
        elif kind == 1:
            y = short_conv(a, sc_w_in[j], sc_conv_w[j], sc_w_out[j])
            yc = short_conv(ac, sc_w_in[j], sc_conv_w[j], sc_w_out[j]) if with_ctx else None
        elif kind == 2:
            y, yc = window_gqa(a, ac, win_w_qkv[j], win_w_o[j], win_q_norm[j], win_k_norm[j],
                               win_sink[j], cos, sin, with_ctx)
        else:
            cf = (cf_w_pw1[j], cf_b_pw1[j], cf_dw_w[j], cf_dw_b[j], cf_ln_g[j], cf_ln_b[j],
                  cf_w_pw2[j], cf_b_pw2[j])
            y = conformer_conv(a, *cf)
            yc = conformer_conv(ac, *cf) if with_ctx else None
        h = h + g1 * y
        h = h + g2 * swiglu(rmsnorm(h, norm2[i]) * (1.0 + sc2) + sh2, ffn_w_gate_up[i], ffn_w_down[i])
        if with_ctx:
            hc = hc + cg1 * yc
            hc = hc + cg2 * swiglu(rmsnorm(hc, norm2[i]) * (1.0 + csc2) + csh2,
                                   ffn_w_gate_up[i], ffn_w_down[i])
    return h
```

```python
import math
import numpy as np
import concourse.bass as bass
import concourse.mybir as mybir
from concourse.bass_utils import run_bass_kernel_spmd

F32 = mybir.dt.float32
BF16 = mybir.dt.bfloat16
AF = mybir.ActivationFunctionType
ALU = mybir.AluOpType

NCORES = 8
D = 1024
SEQ = 16384
OWN = SEQ // NCORES
HALO = 144
NW = OWN + 2 * HALO
NCTX = 256
NT = NW + NCTX
DFF = 2816
EPS = 1e-6
NLAYERS_DEFAULT = 4

ENG_NAMES = ("pe", "act", "dve", "pool", "sp")
EPOCH = 12000
NDSEM = 6


class Op:
    __slots__ = ("eng", "fn", "reads", "writes", "dma", "sig", "idx", "waits", "cnt", "dslot", "cc")

    def __init__(self, eng, fn, reads, writes, dma=False, sig=True, cc=False):
        self.eng = eng; self.fn = fn; self.reads = tuple(reads); self.writes = tuple(writes)
        self.dma = dma; self.sig = sig; self.waits = []; self.cnt = None; self.dslot = None; self.cc = cc


class Prog:
    def __init__(self, nc, selfsync=True):
        self.nc = nc
        self.ops = []
        self.selfsync = selfsync
        self.ncc = 0

    def add(self, eng, fn, reads=(), writes=(), dma=False, sig=True, cc=False):
        op = Op(eng, fn, list(reads) + ["__phase__"], writes, dma, sig, cc)
        op.idx = len(self.ops)
        self.ops.append(op)
        return op

    def barrier(self):
        nc = self.nc
        op = Op("dve", lambda e: e.memset(self._bar[:, :], 0.0), [], ["__phase__"])
        op.idx = len(self.ops)
        self.ops.append(op)

    def finalize(self, final_wait_ops=()):
        nc = self.nc
        ops = self.ops
        cnt = {e: 0 for e in ENG_NAMES}
        dcount = {}
        dn = {e: 0 for e in ENG_NAMES}
        for op in ops:
            if op.cc:
                self.ncc += 1
                op.dslot = (("cc", self.ncc), 1)
            elif op.dma:
                slot = dn[op.eng] % NDSEM
                dn[op.eng] += 1
                k = (op.eng, slot)
                dcount[k] = dcount.get(k, 0) + 1
                op.dslot = (k, dcount[k])
            elif op.sig:
                cnt[op.eng] += 1
                op.cnt = cnt[op.eng]
        nxt = {e: None for e in ENG_NAMES}
        for op in reversed(ops):
            if op.dma or op.cc:
                continue
            if op.sig:
                nxt[op.eng] = op
            else:
                op.cnt = ("fwd", nxt[op.eng])
        last_writer = {}
        readers = {}
        seen = {e: {} for e in ENG_NAMES}
        n_waits = 0
        for op in ops:
            deps = set()
            for r in op.reads:
                w = last_writer.get(r)
                if w is not None: deps.add(w)
            for wr in op.writes:
                w = last_writer.get(wr)
                if w is not None: deps.add(w)
                rl = readers.get(wr)
                if rl:
                    deps.update(rl.values() if isinstance(rl, dict) else rl)
            waits = {}
            if op.dma and not op.cc:
                k, c = op.dslot
                if c > 1:
                    waits[("d",) + k] = 16 * (c - 1)
            for d in deps:
                if d is op: continue
                if d.cc:
                    key = ("d",) + d.dslot[0]; val = 1
                elif d.dma:
                    k, c = d.dslot
                    key = ("d",) + k; val = 16 * c
                else:
                    tgt = d
                    if not d.sig:
                        tgt = d.cnt[1]
                        assert tgt is not None, f"no signalling op after {d.idx}"
                    if tgt.eng == op.eng and not op.dma and not op.cc:
                        if op.eng == "pe" or not self.selfsync:
                            continue
                        if tgt.idx >= op.idx:
                            continue
                    assert tgt.idx < op.idx, f"signal op {tgt.idx} after waiter {op.idx} (dep {d.idx})"
                    ep, v = divmod(tgt.cnt - 1, EPOCH)
                    key = ("c", tgt.eng, ep); val = v + 1
                if waits.get(key, 0) < val: waits[key] = val
            for key, val in waits.items():
                if seen[op.eng].get(key, 0) >= val: continue
                seen[op.eng][key] = val
                op.waits.append((key, val))
                n_waits += 1
            for r in op.reads:
                if r == "__phase__":
                    readers.setdefault(r, {})
                    rk = op.dslot[0] if (op.dma or op.cc) else op.eng
                    if op.sig or op.dma or op.cc:
                        readers[r][rk] = op
                else:
                    readers.setdefault(r, []).append(op)
            for wr in op.writes:
                last_writer[wr] = op
                readers[wr] = {} if wr == "__phase__" else []
        self.n_waits = n_waits
        sems = {}

        def sem(key):
            if key not in sems:
                sems[key] = nc.alloc_semaphore("s_" + "_".join(str(k) for k in key))
            return sems[key]

        fin = []
        for op in final_wait_ops:
            k, c = op.dslot
            fin.append((("d",) + k, 16 * c))
        per_eng = {e: [op for op in ops if op.eng == e] for e in ENG_NAMES}
        with nc.cleanup_on_exit():
            for op in ops:
                for key, val in op.waits:
                    sem(key)
                if op.cc or op.dma:
                    sem(("d",) + op.dslot[0])
                elif op.sig:
                    sem(("c", op.eng, (op.cnt - 1) // EPOCH))
            for key, val in fin:
                sem(key)
            for h_ in sems.values():
                nc.gpsimd.sem_clear(h_)
            nc.all_engine_barrier()
            with nc.Block() as block:
                def emit(ename):
                    def body(eng):
                        for op in per_eng[ename]:
                            for key, val in op.waits:
                                eng.wait_ge(sem(key), val)
                            ins = op.fn(eng)
                            if op.cc:
                                ins.then_inc(sem(("d",) + op.dslot[0]))
                            elif op.dma:
                                k, c = op.dslot
                                ins.then_inc(sem(("d",) + k), 16)
                            elif op.sig:
                                ep = (op.cnt - 1) // EPOCH
                                ins.then_inc(sem(("c", ename, ep)), 1)
                        if ename == "sp":
                            for key, val in fin:
                                eng.wait_ge(sem(key), val)
                    return body
                block.tensor(emit("pe"))
                block.scalar(emit("act"))
                block.vector(emit("dve"))
                block.gpsimd(emit("pool"))
                block.sync(emit("sp"))
            nc.all_engine_barrier()
        return len(sems)


def _fix_phase_readers(readers_val):
    return readers_val.values() if isinstance(readers_val, dict) else readers_val


class Arena:
    def __init__(self, ap32, nbytes):
        self.ap32 = ap32
        self.nbytes = nbytes
        self.off = 0

    def reset(self, off=0):
        self.off = off

    def alloc(self, free_shape, dtype):
        n = int(np.prod(free_shape))
        esz = 4 if dtype == F32 else 2
        nb = (n * esz + 31) // 32 * 32
        assert self.off + nb <= self.nbytes, f"arena overflow {self.off}+{nb}>{self.nbytes}"
        a = self.ap32[:, self.off // 4:(self.off + nb) // 4]
        self.off += nb
        if dtype != F32:
            a = a.bitcast(dtype)
        a = a[:, 0:n]
        if len(free_shape) == 2:
            a = a.rearrange("p (a b) -> p a b", a=free_shape[0])
        elif len(free_shape) == 3:
            a = a.rearrange("p (a b c) -> p a b c", a=free_shape[0], b=free_shape[1])
        return a


WSPEC_A = [("l0_qkv", 3072, 1), ("l0_wo", 1024, 1), ("ffn_gu0", 5632, 1), ("ffn_d0", 1024, 3)]
WSPEC_B = [("l1_win", 3072, 1), ("l1_wout", 1024, 1), ("ffn_gu1", 5632, 1), ("ffn_d1", 1024, 3),
           ("l2_qkv", 1792, 1), ("l2_wo", 1024, 1), ("ffn_gu2", 5632, 1), ("ffn_d2", 1024, 3),
           ("l3_pw1", 2048, 1), ("l3_pw2", 1024, 1), ("ffn_gu3", 5632, 1), ("ffn_d3", 1024, 3)]


def _wlayout(spec):
    offs = {}
    off = 0
    for name, n, cpr in spec:
        offs[name] = (off, n, cpr)
        off += cpr * 128 * n
    assert off % 512 == 0
    return offs, off


WOFF_A, WSIZE_A = _wlayout(WSPEC_A)
WOFF_B, WSIZE_B = _wlayout(WSPEC_B)

CH_LAT = [(0, 512), (512, 512), (1024, 512), (1536, 512), (2048, 288)]
CH_CTX = (NW, NCTX)
CH_B = [(128, 512), (640, 512), (1152, 512), (1664, 512), (2176, 32)]


def build_program(nlayers=NLAYERS_DEFAULT, debug_h=False):
    nc = bass.Bass("TRN2", target_bir_lowering=False)
    P = Prog(nc)

    def din(name, shape, dt=F32):
        return nc.dram_tensor(name, list(shape), dt, kind="ExternalInput").ap()

    xw = din("xw", [NW, D]); ctxa = din("ctxa", [128, D]); ctxb = din("ctxb", [128, D])
    cvec = din("cvec", [128, 8, 2])
    adaw = din("adaw", [4, D, 768]); adab = din("adab", [128, 24])
    normd = din("normd", [128, 4, 2, 8])
    cosd = din("cosd", [128, NW]); sind = din("sind", [128, NW])
    validd = din("validd", [128, NW])
    constf = din("constf", [128, 256])
    constb = din("constb", [128, 5 * 128])
    maskda = din("maskda", [128, 3 * 512]); maskdb = din("maskdb", [128, 3 * 512])
    wA = din("wA", [WSIZE_A // 512, 512]); wB = din("wB", [WSIZE_B // 512, 512])
    l0_vec = din("l0_vec", [128, 8]); l1_conv = din("l1_conv", [128, 8, 3])
    l2_vec = din("l2_vec", [128, 16])
    hspill = nc.dram_tensor("hspill", [128, 8 * NT], F32).ap(); l3_vec = din("l3_vec", [128, 8, 40])
    wAb = nc.dram_tensor("wAb", [WSIZE_A // 512, 512], F32).ap()
    wBb = nc.dram_tensor("wBb", [WSIZE_B // 512, 512], F32).ap()
    gA = nc.dram_tensor("gA", [NCORES * WSIZE_A // 512, 512], F32).ap()
    gB = nc.dram_tensor("gB", [NCORES * WSIZE_B // 512, 512], F32).ap()
    gAf = gA.rearrange("a b -> (a b)"); gBf = gB.rearrange("a b -> (a b)")

    def wsrc(name, i):
        if name in WOFF_A:
            (off, n, cpr), flat, rs = WOFF_A[name], gAf, WSIZE_A
        else:
            (off, n, cpr), flat, rs = WOFF_B[name], gBf, WSIZE_B
        base = (i // cpr) * rs + off + (i % cpr) * 128 * n
        return flat[base:base + 128 * n].rearrange("(p n) -> p n", n=n), ("gA" if name in WOFF_A else "gB")
    outd = nc.dram_tensor("out", [OWN, D], F32, kind="ExternalOutput").ap()
    kvloc = nc.dram_tensor("kvloc", [2048, 2048], BF16).ap()
    kvall = nc.dram_tensor("kvall", [NCORES * 2048, 2048], BF16).ap()
    modloc = nc.dram_tensor("modloc", [128, 48], F32).ap()
    modall = nc.dram_tensor("modall", [NCORES * 128, 48], F32).ap()

    ident_ones = nc.alloc_sbuf_tensor("ident_ones", [128, 256], F32)
    ident = ident_ones[:, 0:128]; onesf = ident_ones[:, 128:256]
    cb = nc.alloc_sbuf_tensor("cb", [128, 5 * 128], BF16)
    onesb = cb[:, 0:128]; blk = cb[:, 128:256]; rotm = cb[:, 256:384]; E0 = cb[:, 384:512]; E1 = cb[:, 512:640]
    validb = nc.alloc_sbuf_tensor("validb", [128, NW], BF16)
    modsb = nc.alloc_sbuf_tensor("modsb", [128, 4, 48, 2], F32)
    normsb = nc.alloc_sbuf_tensor("normsb", [128, 4, 2, 8], F32)
    Gt = nc.alloc_sbuf_tensor("Gt", [128, 4, 2, 2, 8], F32)
    bar = nc.alloc_sbuf_tensor("bar", [128, 8], F32)
    P._bar = bar
    small = nc.alloc_sbuf_tensor("small", [128, 64], F32)
    l0v = nc.alloc_sbuf_tensor("l0v", [128, 8], F32)
    l1cv = nc.alloc_sbuf_tensor("l1cv", [128, 8, 3], F32)
    l2v = nc.alloc_sbuf_tensor("l2v", [128, 16], F32)
    l3v = nc.alloc_sbuf_tensor("l3v", [128, 8, 40], F32)
    H_BYTES = NT * 8 * 4
    hreg = nc.alloc_sbuf_tensor("hreg", [128, H_BYTES // 4], F32)
    T_BYTES = 115 * 1024
    treg = nc.alloc_sbuf_tensor("treg", [128, T_BYTES // 4], F32)
    hT = hreg[:, :].rearrange("p (c t) -> p c t", c=8)
    HA = Arena(hreg[:, :], H_BYTES)
    TA = Arena(treg[:, :], T_BYTES)
    ps = nc.alloc_psum_tensor("ps", [128, 8 * 512], F32)
    bankctr = [0]

    def bank():
        i = bankctr[0] % 8
        bankctr[0] += 1
        return i

    def bk(i, n=512):
        return ps[:, i * 512:i * 512 + n]

    def MM(out, lhsT, rhs, st, sp, R, W, sig=True, tp=None):
        if tp is None:
            P.add("pe", lambda e: e.matmul(out, lhsT=lhsT, rhs=rhs, start=st, stop=sp), R, W, sig=sig)
        else:
            P.add("pe", lambda e: e.matmul(out, lhsT=lhsT, rhs=rhs, start=st, stop=sp, tile_position=tp), R, W, sig=sig)

    def ACT(out, in_, func, R, W, bias=None, scale=None):
        kw = {}
        if bias is not None: kw["bias"] = bias
        if scale is not None: kw["scale"] = scale
        P.add("act", lambda e: e.activation(out=out, in_=in_, func=func, **kw), R, W)

    def TT(eng, out, in0, in1, op, R, W):
        P.add(eng, lambda e: e.tensor_tensor(out=out, in0=in0, in1=in1, op=op), R, W)

    def STT(eng, out, in0, scalar, in1, op0, op1, R, W):
        P.add(eng, lambda e: e.scalar_tensor_tensor(out=out, in0=in0, scalar=scalar, in1=in1, op0=op0, op1=op1), R, W)

    def TS(eng, out, in0, s1, s2, op0, op1, R, W):
        if s2 is None:
            P.add(eng, lambda e: e.tensor_scalar(out=out, in0=in0, scalar1=s1, scalar2=None, op0=op0), R, W)
        else:
            P.add(eng, lambda e: e.tensor_scalar(out=out, in0=in0, scalar1=s1, scalar2=s2, op0=op0, op1=op1), R, W)

    def CP(eng, out, in_, R, W):
        if eng == "act":
            P.add("act", lambda e: e.copy(out=out, in_=in_), R, W)
        else:
            P.add(eng, lambda e: e.tensor_copy(out=out, in_=in_), R, W)

    def RECIP(out, in_, R, W):
        P.add("dve", lambda e: e.reciprocal(out=out, in_=in_), R, W)

    def DMA(out, in_, R, W, q="sp"):
        return P.add(q, lambda e: e.dma_start(out=out, in_=in_), R, W, dma=True)

    def MEMSET(eng, ap, val, W):
        P.add(eng, lambda e: e.memset(ap, val), [], W)

    def modap(l, q, s):
        return modsb[:, l, q, s:s + 1]

    def load_w(dst, wname, name, kc_n=8, c0=0, ncol=None):
        for kc in range(kc_n):
            src, g = wsrc(wname, kc)
            if ncol is not None:
                src = src[:, c0:c0 + ncol]
            DMA(dst[:, kc, :], src, [g], [f"{name}"], q="pool")

    evac_ctr = [0]

    def evac_eng():
        evac_ctr[0] += 1
        return "act" if evac_ctr[0] % 2 else "dve"

    DMA(ident_ones[:, :], constf[:, :], [], ["constf"])
    DMA(cb[:, :], constb[:, :], [], ["cb"], q="pool")
    DMA(validb[:, :], validd[:, :], [], ["validb"], q="pool")
    DMA(normsb[:, :, :, :], normd[:, :, :, :], [], ["normsb"])
    DMA(l0v[:, :], l0_vec[:, :], [], ["l0v"])
    DMA(l1cv[:, :, :], l1_conv[:, :, :], [], ["l1cv"])
    DMA(l2v[:, :], l2_vec[:, :], [], ["l2v"])
    DMA(l3v[:, :, :], l3_vec[:, :, :], [], ["l3v"])
    DMA(wAb[:, :], wA[:, :], [], ["wAb"])
    P.add("pool", lambda e: e.collective_compute("AllGather", ALU.bypass, replica_groups=[list(range(NCORES))],
                                                 ins=[wAb[:, :]], outs=[gA[:, :]]), ["wAb"], ["gA", "ccchain"], cc=True)
    DMA(wBb[:, :], wB[:, :], [], ["wBb"])

    TA.reset()
    cact = TA.alloc((8, 2), F32)
    csig = TA.alloc((8, 2), F32)
    adabs = TA.alloc((24,), F32)
    modl = TA.alloc((24, 2), F32)
    DMA(cact, cvec[:, :, :], [], ["cact"])
    DMA(adabs, adab[:, :], [], ["adabs"])
    ACT(csig, cact, AF.Silu, ["cact"], ["csig"])
    awb = [TA.alloc((8, 768), F32) for _ in range(2)]
    for l in range(4):
        wb_ = awb[l % 2]
        for kc in range(8):
            DMA(wb_[:, kc, :], adaw[l, kc * 128:(kc + 1) * 128, :], [], [f"awb{l % 2}"])
        for cc in range(6):
            b = bank()
            for kc in range(8):
                MM(ps[:, b * 512:b * 512 + 2], wb_[:, kc, cc * 128:(cc + 1) * 128], csig[:, kc, :], kc == 0, kc == 7,
                   [f"awb{l % 2}", "csig"], [f"B{b}"], sig=(kc == 7))
            TS("dve", modl[:, l * 6 + cc, :], ps[:, b * 512:b * 512 + 2], adabs[:, l * 6 + cc:l * 6 + cc + 1], None, ALU.add, None,
               [f"B{b}", "adabs"], ["modl"])
    DMA(modloc[:, :], modl.rearrange("p a b -> p (a b)"), ["modl"], ["modloc"])
    P.add("pool", lambda e: e.collective_compute("AllGather", ALU.bypass, replica_groups=[list(range(NCORES))],
                                                 ins=[modloc[:, :]], outs=[modall[:, :]]),
          ["modloc"], ["modall", "ccchain"], cc=True)
    for l in range(4):
        P.add("sp", lambda e, l=l: e.dma_start(
            out=modsb[:, l, :, :].rearrange("p (r c) s -> p r (c s)", r=8),
            in_=modall[:, l * 12:(l + 1) * 12].rearrange("(r p) cs -> p r cs", p=128)),
            ["modall"], ["modsb"], dma=True)
    for l in range(4):
        for sub in range(2):
            for s in range(2):
                q0 = 8 if sub == 0 else 32
                STT("dve", Gt[:, l, sub, s, :], modsb[:, l, q0:q0 + 8, s], 1.0, normsb[:, l, sub, :], ALU.add, ALU.mult,
                    ["modsb", "normsb"], ["Gt"])
    P.barrier()

    def load_xT(arena):
        stg = arena.alloc((4, 1024), F32)
        groups = [(xw, g * 512, min(512, NW - g * 512), g * 512) for g in range(5)] + [(None, 0, 256, NW)]
        for gi, (src, r0, n, t0) in enumerate(groups):
            ntile = (n + 127) // 128
            for i in range(ntile):
                rows = min(128, n - i * 128)
                if src is None:
                    sap = (ctxa, ctxb)[i][:, :]
                else:
                    sap = src[r0 + i * 128:r0 + i * 128 + rows, :]
                DMA(stg[0:rows, i, :], sap, [], ["xstg"])
            for fc in range(8):
                b = bank()
                for i in range(ntile):
                    rows = min(128, n - i * 128)
                    MM(ps[:, b * 512 + i * 128:b * 512 + i * 128 + rows], stg[0:rows, i, fc * 128:(fc + 1) * 128],
                       ident[0:rows, 0:rows], True, True, ["xstg", "constf"], [f"B{b}"], sig=(i == ntile - 1))
                CP(evac_eng(), hT[:, fc, t0:t0 + n], bk(b, n), [f"B{b}"], [f"h{gi}"])

    def chunk_id(t0):
        if t0 >= NW: return 5
        return t0 // 512

    def make_xn(xn, l, sub, chunks, tmpA):
        sq = tmpA.alloc((8, 512), BF16)
        rs = tmpA.alloc((512,), F32)
        tm = [tmpA.alloc((512,), F32) for _ in range(2)]
        shq = 0 if sub == 0 else 24
        for (t0, n) in chunks:
            s = 1 if t0 >= NW else 0
            hid = f"h{chunk_id(t0)}"
            ACT(sq[:, :, 0:n], hT[:, :, t0:t0 + n], AF.Square, [hid], ["sq"])
            b = bank()
            for fc in range(8):
                MM(bk(b, n), onesb, sq[:, fc, 0:n], fc == 0, fc == 7, ["sq", "cb"], [f"B{b}"], sig=(fc == 7))
            ACT(rs[:, 0:n], bk(b, n), AF.Ln, [f"B{b}"], ["rs"], bias=EPS, scale=1.0 / D)
            ACT(rs[:, 0:n], rs[:, 0:n], AF.Exp, ["rs"], ["rs"], scale=-0.5)
            for fc in range(8):
                t_ = tm[fc % 2]
                STT("dve", t_[:, 0:n], hT[:, fc, t0:t0 + n], Gt[:, l, sub, s, fc:fc + 1], rs[:, 0:n], ALU.mult, ALU.mult,
                    [hid, "Gt", "rs"], [f"tm{fc % 2}"])
                ACT(xn[:, fc, t0:t0 + n], t_[:, 0:n], AF.Identity, [f"tm{fc % 2}", "modsb"], [f"xn{chunk_id(t0)}"],
                    bias=modap(l, shq + fc, s))

    def proj_residual(w_sb, wname, inT, in_name, l, gq, chunks, kc_n=8, bias=None):
        for (t0, n) in chunks:
            s = 1 if t0 >= NW else 0
            cid = chunk_id(t0)
            for dc in range(8):
                b = bank()
                for kc in range(kc_n):
                    MM(bk(b, n), w_sb[:, kc, dc * 128:(dc + 1) * 128], inT[:, kc, t0:t0 + n], kc == 0, kc == kc_n - 1,
                       [wname, f"{in_name}{cid}"], [f"B{b}"], sig=(kc == kc_n - 1))
                if bias is None:
                    STT("dve", hT[:, dc, t0:t0 + n], bk(b, n), modap(l, gq + dc, s), hT[:, dc, t0:t0 + n], ALU.mult, ALU.add,
                        [f"B{b}", "modsb", f"h{cid}"], [f"h{cid}"])
                else:
                    raise NotImplementedError

    def ffn(l, chunks):
        TA.reset()
        xn2 = TA.alloc((8, NT), BF16)
        actb = TA.alloc((2, NT), BF16)
        gub = [TA.alloc((8, 2, 256), BF16) for _ in range(3)]
        wdb = [TA.alloc((2, 1024), BF16) for _ in range(3)]
        sgt = [TA.alloc((512,), BF16) for _ in range(2)]
        make_xn(xn2, l, 1, chunks, TA)
        for u in range(11):
            g_ = gub[u % 3]; wd_ = wdb[u % 3]
            c0 = u * 256
            for kc in range(8):
                src, g = wsrc(f"ffn_gu{l}", kc)
                DMA(g_[:, kc, 0, :], src[:, c0:c0 + 256], [g], [f"gub{u % 3}"], q="pool")
                DMA(g_[:, kc, 1, :], src[:, DFF + c0:DFF + c0 + 256], [g], [f"gub{u % 3}"], q="pool")
            for jj in range(2):
                src, g = wsrc(f"ffn_d{l}", 2 * u + jj)
                DMA(wd_[:, jj, :], src, [g], [f"wdb{u % 3}"], q="pool")
            for jj in range(2):
                for ci, (t0, n) in enumerate(chunks):
                    cid = chunk_id(t0)
                    bg = bank()
                    for kc in range(8):
                        MM(bk(bg, n), g_[:, kc, 0, jj * 128:(jj + 1) * 128], xn2[:, kc, t0:t0 + n], kc == 0, kc == 7,
                           [f"gub{u % 3}", f"xn{cid}"], [f"B{bg}"], sig=(kc == 7))
                    bu = bank()
                    for kc in range(8):
                        MM(bk(bu, n), g_[:, kc, 1, jj * 128:(jj + 1) * 128], xn2[:, kc, t0:t0 + n], kc == 0, kc == 7,
                           [f"gub{u % 3}", f"xn{cid}"], [f"B{bu}"], sig=(kc == 7))
                    st_ = sgt[ci % 2]
                    ACT(st_[:, 0:n], bk(bg, n), AF.Silu, [f"B{bg}"], [f"sgt{ci % 2}"])
                    TT("dve", actb[:, jj, t0:t0 + n], st_[:, 0:n], bk(bu, n), ALU.mult, [f"sgt{ci % 2}", f"B{bu}"], [f"act{jj}_{cid}"])
            for (t0, n) in chunks:
                s = 1 if t0 >= NW else 0
                cid = chunk_id(t0)
                for dc in range(8):
                    b = bank()
                    for jj in range(2):
                        MM(bk(b, n), wd_[:, jj, dc * 128:(dc + 1) * 128], actb[:, jj, t0:t0 + n], jj == 0, jj == 1,
                           [f"wdb{u % 3}", f"act{jj}_{cid}"], [f"B{b}"], sig=(jj == 1))
                    STT("dve", hT[:, dc, t0:t0 + n], bk(b, n), modap(l, 40 + dc, s), hT[:, dc, t0:t0 + n], ALU.mult, ALU.add,
                        [f"B{b}", "modsb", f"h{cid}"], [f"h{cid}"])
        P.barrier()

    def layer0():
        l = 0
        chunks = CH_LAT + [CH_CTX]
        TA.reset()
        xn = TA.alloc((8, NT), BF16)
        QO = TA.alloc((8, NT), BF16)
        ctxK = TA.alloc((8, NCTX), BF16)
        ctxV = TA.alloc((2, 1024), BF16)
        tsave = TA.off
        make_xn(xn, l, 0, chunks, TA)
        P.barrier()
        HA.reset()
        wq = HA.alloc((8, 3072), BF16)
        cos_ = HA.alloc((NW,), F32); sin_ = HA.alloc((NW,), F32)
        load_w(wq, "l0_qkv", "wq")
        DMA(cos_, cosd[:, :], [], ["cos"]); DMA(sin_, sind[:, :], [], ["sin"])
        TA.reset(tsave)
        sqb = [TA.alloc((512,), BF16) for _ in range(2)]
        rr = [TA.alloc((512,), F32) for _ in range(2)]
        qn = [TA.alloc((512,), BF16) for _ in range(2)]
        t1 = [TA.alloc((512,), F32) for _ in range(2)]
        t2 = [TA.alloc((512,), F32) for _ in range(2)]
        kst = [TA.alloc((512,), BF16) for _ in range(2)]
        vst = [TA.alloc((512,), BF16) for _ in range(2)]
        it = 0
        for (t0, n) in chunks:
            isctx = t0 >= NW
            cid = chunk_id(t0)
            for oc in range(16):
                isk = oc >= 8
                hh = oc % 8
                i2 = it % 2; it += 1
                b = bank()
                for kc in range(8):
                    MM(bk(b, n), wq[:, kc, oc * 128:(oc + 1) * 128], xn[:, kc, t0:t0 + n], kc == 0, kc == 7,
                       ["wq", f"xn{cid}"], [f"B{b}"], sig=(kc == 7))
                ACT(sqb[i2][:, 0:n], bk(b, n), AF.Square, [f"B{b}"], [f"sqb{i2}"])
                b2 = bank()
                MM(bk(b2, n), blk, sqb[i2][:, 0:n], True, True, [f"sqb{i2}", "cb"], [f"B{b2}"])
                ACT(rr[i2][:, 0:n], bk(b2, n), AF.Ln, [f"B{b2}"], [f"rr{i2}"], bias=EPS, scale=1.0 / 64)
                ACT(rr[i2][:, 0:n], rr[i2][:, 0:n], AF.Exp, [f"rr{i2}"], [f"rr{i2}"], scale=-0.5)
                gcol = l0v[:, 1:2] if isk else l0v[:, 0:1]
                if isctx:
                    dst = ctxK[:, hh, :] if isk else QO[:, hh, t0:t0 + n]
                    dname = "ctxK" if isk else f"QO{hh}_{cid}"
                    STT("dve", dst, bk(b, n), gcol, rr[i2][:, 0:n], ALU.mult, ALU.mult, [f"B{b}", f"rr{i2}", "l0v"], [dname])
                    continue
                STT("dve", qn[i2][:, 0:n], bk(b, n), gcol, rr[i2][:, 0:n], ALU.mult, ALU.mult, [f"B{b}", f"rr{i2}", "l0v"], [f"qn{i2}"])
                b3 = bank()
                MM(bk(b3, n), rotm, qn[i2][:, 0:n], True, True, [f"qn{i2}", "cb"], [f"B{b3}"])
                TT("dve", t1[i2][:, 0:n], qn[i2][:, 0:n], cos_[:, t0:t0 + n], ALU.mult, [f"qn{i2}", "cos"], [f"t1{i2}"])
                TT("dve", t2[i2][:, 0:n], bk(b3, n), sin_[:, t0:t0 + n], ALU.mult, [f"B{b3}", "sin"], [f"t2{i2}"])
                if not isk:
                    TT("pool", QO[:, hh, t0:t0 + n], t1[i2][:, 0:n], t2[i2][:, 0:n], ALU.add, [f"t1{i2}", f"t2{i2}"], [f"QO{hh}_{cid}"])
                else:
                    TT("pool", kst[i2][:, 0:n], t1[i2][:, 0:n], t2[i2][:, 0:n], ALU.add, [f"t1{i2}", f"t2{i2}"], [f"kst{i2}"])
                    a = max(t0, HALO); e_ = min(t0 + n, HALO + OWN)
                    DMA(kvloc[hh * 128:(hh + 1) * 128, a - HALO:e_ - HALO], kst[i2][:, a - t0:e_ - t0], [f"kst{i2}"], ["kvloc"])
        vt = 0
        for i in range(16 + 2):
            isctx = i >= 16
            tok0 = (HALO + 128 * i) if not isctx else (NW + 128 * (i - 16))
            xr = [f"xn{c}" for c in sorted({chunk_id(tok0), chunk_id(tok0 + 127)})]
            for hb in range(2):
                b = bank()
                for kc in range(8):
                    MM(bk(b), xn[:, kc, tok0:tok0 + 128], wq[:, kc, 2048 + hb * 512:2048 + (hb + 1) * 512], kc == 0, kc == 7,
                       ["wq"] + xr, [f"B{b}"], sig=(kc == 7))
                if isctx:
                    CP(evac_eng(), ctxV[:, i - 16, hb * 512:(hb + 1) * 512], bk(b), [f"B{b}"], ["ctxV"])
                else:
                    i2 = vt % 2; vt += 1
                    CP(evac_eng(), vst[i2][:, :], bk(b), [f"B{b}"], [f"vst{i2}"])
                    r0 = (8 + hb * 4) * 128
                    DMA(kvloc[r0:r0 + 512, i * 128:(i + 1) * 128].rearrange("(h p) d -> p h d", p=128),
                        vst[i2][:, :].rearrange("p (h d) -> p h d", h=4), [f"vst{i2}"], ["kvloc"])
        P.add("pool", lambda e: e.collective_compute("AllGather", ALU.bypass, replica_groups=[list(range(NCORES))],
                                                     ins=[kvloc[:, :]], outs=[kvall[:, :]]),
              ["kvloc"], ["kvall", "ccchain"], cc=True)
        P.add("pool", lambda e: e.collective_compute("AllGather", ALU.bypass, replica_groups=[list(range(NCORES))],
                                                     ins=[wBb[:, :]], outs=[gB[:, :]]), ["wBb"], ["gB", "ccchain"], cc=True)
        lamv = small[:, 0:4]
        TT("dve", small[0:64, 8:9], l0v[0:64, 3:4], l0v[0:64, 4:5], ALU.mult, ["l0v"], ["lamp"])
        TT("dve", small[0:64, 9:10], l0v[0:64, 5:6], l0v[0:64, 6:7], ALU.mult, ["l0v", "lamp"], ["lamp"])
        b = bank()
        MM(ps[:, b * 512:b * 512 + 2], onesf[0:64, :], small[0:64, 8:10], True, True, ["lamp", "constf"], [f"B{b}"])
        ACT(small[:, 10:12], ps[:, b * 512:b * 512 + 2], AF.Exp, [f"B{b}"], ["lame"])
        TT("dve", small[:, 12:13], small[:, 10:11], small[:, 11:12], ALU.subtract, ["lame"], ["lam"])
        lam_init = 0.8 - 0.6 * math.exp(-0.3 * 0)
        TS("dve", small[:, 13:14], small[:, 12:13], lam_init, -1.0, ALU.add, ALU.mult, ["lam"], ["neglam"])
        TS("dve", small[:, 14:15], l0v[:, 2:3], 1.0 - lam_init, None, ALU.mult, None, ["l0v"], ["sgain"])
        neglam = small[:, 13:14]; sgain = small[:, 14:15]
        P.barrier()
        HA.reset()
        Kh = HA.alloc((SEQ,), BF16)
        Vh = HA.alloc((128, 128), BF16)
        r0_ = HA.alloc((512,), F32); r1_ = HA.alloc((512,), F32)
        u0_ = HA.alloc((512,), F32); u1_ = HA.alloc((512,), F32)
        o_ = HA.alloc((512,), F32); lr_ = HA.alloc((512,), F32)
        osq = HA.alloc((512,), BF16)
        TA.reset(tsave)
        pt = [TA.alloc((1024,), BF16) for _ in range(3)]
        pctr = 0
        sctr = 0
        scale = 1.0 / 8.0
        for h in range(8):
            for r in range(NCORES):
                DMA(Kh[:, r * 2048:(r + 1) * 2048], kvall[r * 2048 + h * 128:r * 2048 + (h + 1) * 128, :], ["kvall"], ["Kh"])
                DMA(Vh[:, r * 16:(r + 1) * 16, :].rearrange("p a b -> p (a b)"),
                    kvall[r * 2048 + (8 + h) * 128:r * 2048 + (9 + h) * 128, :], ["kvall"], ["Vh"])
            for (t0, n) in chunks:
                isctx = t0 >= NW
                cid = chunk_id(t0)
                qname = f"QO{h}_{cid}"
                keys = ([] if isctx else [("g", kt) for kt in range(128)]) + [("c", 0), ("c", 1)]
                nk = len(keys)
                for ki, (kind, kt) in enumerate(keys):
                    sb = (sctr % 2) * 2; sctr += 1
                    if kind == "g":
                        kl0 = Kh[0:64, kt * 128:(kt + 1) * 128]; kl1 = Kh[64:128, kt * 128:(kt + 1) * 128]
                        vl = Vh[:, kt, :]; kr = ["Kh"]; vr = ["Vh"]
                    else:
                        kl0 = ctxK[0:64, h, kt * 128:(kt + 1) * 128]; kl1 = ctxK[64:128, h, kt * 128:(kt + 1) * 128]
                        vl = ctxV[:, kt, h * 128:(h + 1) * 128]; kr = ["ctxK"]; vr = ["ctxV"]
                    MM(bk(sb, n), kl0, QO[0:64, h, t0:t0 + n], True, True, kr + [qname], [f"B{sb}"], sig=False, tp=(0, 0))
                    MM(bk(sb + 1, n), kl1, QO[64:128, h, t0:t0 + n], True, True, kr + [qname], [f"B{sb + 1}"], tp=(64, 0))
                    p_ = pt[pctr % 3]; pn = f"pt{pctr % 3}"; pctr += 1
                    if n == 512:
                        ACT(p_[:, 0:1024], ps[:, sb * 512:sb * 512 + 1024], AF.Exp, [f"B{sb}", f"B{sb + 1}"], [pn], scale=scale)
                    else:
                        ACT(p_[:, :].rearrange("p (a b) -> p a b", a=2)[:, :, 0:n],
                            ps[:, sb * 512:sb * 512 + 1024].rearrange("p (a b) -> p a b", a=2)[:, :, 0:n],
                            AF.Exp, [f"B{sb}", f"B{sb + 1}"], [pn], scale=scale)
                    first = ki == 0; last = ki == nk - 1
                    for m in range(2):
                        MM(bk(4 + m, n), vl, p_[:, m * 512:m * 512 + n], first, last, vr + [pn], [f"B{4 + m}"], sig=False)
                    for m in range(2):
                        MM(bk(6 + m, n), onesb, p_[:, m * 512:m * 512 + n], first, last, [pn, "cb"], [f"B{6 + m}"], sig=(m == 1))
                RECIP(r0_[:, 0:n], bk(6, n), ["B6"], ["r0"])
                RECIP(r1_[:, 0:n], bk(7, n), ["B7"], ["r1"])
                TT("dve", u0_[:, 0:n], bk(4, n), r0_[:, 0:n], ALU.mult, ["B4", "r0"], ["u0"])
                TT("dve", u1_[:, 0:n], bk(5, n), r1_[:, 0:n], ALU.mult, ["B5", "r1"], ["u1"])
                STT("dve", o_[:, 0:n], u1_[:, 0:n], neglam, u0_[:, 0:n], ALU.mult, ALU.add, ["u0", "u1", "neglam"], ["o_"])
                ACT(osq[:, 0:n], o_[:, 0:n], AF.Square, ["o_"], ["osq"])
                sb = (sctr % 2) * 2; sctr += 1
                MM(bk(sb, n), onesb, osq[:, 0:n], True, True, ["osq", "cb"], [f"B{sb}"])
                ACT(lr_[:, 0:n], bk(sb, n), AF.Ln, [f"B{sb}"], ["lr"], bias=EPS, scale=1.0 / 128)
                ACT(lr_[:, 0:n], lr_[:, 0:n], AF.Exp, ["lr"], ["lr"], scale=-0.5)
                STT("dve", QO[:, h, t0:t0 + n], o_[:, 0:n], sgain, lr_[:, 0:n], ALU.mult, ALU.mult, ["o_", "lr", "sgain"], [qname])
        P.barrier()
        HA.reset()
        TA.reset(tsave)
        load_xT(TA)
        P.barrier()
        TA.reset(tsave)
        wo = TA.alloc((8, 1024), BF16)
        load_w(wo, "l0_wo", "wo")
        for h in range(8):
            pass
        for (t0, n) in chunks:
            s = 1 if t0 >= NW else 0
            cid = chunk_id(t0)
            for dc in range(8):
                b = bank()
                for kc in range(8):
                    MM(bk(b, n), wo[:, kc, dc * 128:(dc + 1) * 128], QO[:, kc, t0:t0 + n], kc == 0, kc == 7,
                       ["wo", f"QO{kc}_{cid}"], [f"B{b}"], sig=(kc == 7))
                STT("dve", hT[:, dc, t0:t0 + n], bk(b, n), modap(l, 16 + dc, s), hT[:, dc, t0:t0 + n], ALU.mult, ALU.add,
                    [f"B{b}", "modsb", f"h{cid}"], [f"h{cid}"])
        P.barrier()
        ffn(l, chunks)


    def layer1():
        l = 1
        chunks = CH_LAT + [CH_CTX]
        TA.reset()
        xn = TA.alloc((8, NT), BF16)
        z = TA.alloc((8, NT), BF16)
        tsave = TA.off
        make_xn(xn, l, 0, chunks, TA)
        P.barrier()
        TA.reset(tsave)
        cu = TA.alloc((NT,), F32)
        bS = TA.alloc((NT,), BF16)
        wst = [TA.alloc((8, 3, 128), BF16) for _ in range(2)]
        cS = TA.alloc((512,), F32)
        yt = TA.alloc((512,), F32)
        segs = [(0, NW), (NW, NT)]
        for fc in range(8):
            w_ = wst[fc % 2]; wn = f"wst{fc % 2}"
            for kc in range(8):
                src, g = wsrc("l1_win", kc)
                for j in range(3):
                    DMA(w_[:, kc, j, :], src[:, j * 1024 + fc * 128:j * 1024 + (fc + 1) * 128], [g], [wn], q="pool")
            for (t0, n) in chunks:
                cid = chunk_id(t0)
                bb = [bank(), bank(), bank()]
                for j in range(3):
                    for kc in range(8):
                        MM(bk(bb[j], n), w_[:, kc, j, :], xn[:, kc, t0:t0 + n], kc == 0, kc == 7, [wn, f"xn{cid}"], [f"B{bb[j]}"], sig=(kc == 7))
                CP("act", cS[:, 0:n], bk(bb[1], n), [f"B{bb[1]}"], ["cS"])
                TT("dve", cu[:, t0:t0 + n], cS[:, 0:n], bk(bb[2], n), ALU.mult, ["cS", f"B{bb[2]}"], ["cu"])
                CP("act", bS[:, t0:t0 + n], bk(bb[0], n), [f"B{bb[0]}"], ["bS"])
            TT("dve", cu[:, 0:NW], cu[:, 0:NW], validb[:, :], ALU.mult, ["cu", "validb"], ["cu"])
            for (t0, n) in chunks:
                cid = chunk_id(t0)
                s0, s1 = segs[1] if t0 >= NW else segs[0]
                ACT(yt[:, 0:n], cu[:, t0:t0 + n], AF.Identity, ["cu"], ["yt"], scale=l1cv[:, fc, 1:2])
                a = max(t0, s0 + 1)
                STT("dve", yt[:, a - t0:n], cu[:, a - 1:t0 + n - 1], l1cv[:, fc, 0:1], yt[:, a - t0:n], ALU.mult, ALU.add, ["cu", "yt", "l1cv"], ["yt"])
                e_ = min(t0 + n, s1 - 1)
                STT("dve", yt[:, 0:e_ - t0], cu[:, t0 + 1:e_ + 1], l1cv[:, fc, 2:3], yt[:, 0:e_ - t0], ALU.mult, ALU.add, ["cu", "yt", "l1cv"], ["yt"])
                TT("dve", z[:, fc, t0:t0 + n], yt[:, 0:n], bS[:, t0:t0 + n], ALU.mult, ["yt", "bS"], [f"z{cid}"])
        P.barrier()
        TA.reset(tsave)
        wo = TA.alloc((8, 1024), BF16)
        load_w(wo, "l1_wout", "wo1")
        proj_residual(wo, "wo1", z, "z", l, 16, chunks)
        P.barrier()
        ffn(l, chunks)


    def layer2():
        l = 2
        chunks_kv = CH_LAT + [CH_CTX]
        TA.reset()
        xn = TA.alloc((8, NT), BF16)
        QO = TA.alloc((8, NW), BF16)
        K2 = TA.alloc((4, NT), BF16)
        tsave = TA.off
        make_xn(xn, l, 0, chunks_kv, TA)
        for fc in range(8):
            DMA(hspill[:, fc * NT:(fc + 1) * NT], hreg[:, fc * NT:(fc + 1) * NT], [f"h{c}" for c in range(6)], ["hspill"])
        P.barrier()
        HA.reset()
        VP = HA.alloc((21, 576), BF16)
        vsave = HA.off
        wq = HA.alloc((8, 1792), BF16)
        cos_ = HA.alloc((NW,), F32); sin_ = HA.alloc((NW,), F32)
        load_w(wq, "l2_qkv", "wq2")
        DMA(cos_, cosd[:, :], [], ["cos"]); DMA(sin_, sind[:, :], [], ["sin"])
        MEMSET("pool", VP[:, :, :], 0.0, ["VP"])
        TA.reset(tsave)
        sqb = [TA.alloc((512,), BF16) for _ in range(2)]
        rr = [TA.alloc((512,), F32) for _ in range(2)]
        qn = [TA.alloc((512,), BF16) for _ in range(2)]
        t1 = [TA.alloc((512,), F32) for _ in range(2)]
        t2 = [TA.alloc((512,), F32) for _ in range(2)]
        it = 0
        for (t0, n) in chunks_kv:
            isctx = t0 >= NW
            cid = chunk_id(t0)
            for oc in range(12):
                isk = oc >= 8
                if isctx and not isk:
                    continue
                i2 = it % 2; it += 1
                b = bank()
                for kc in range(8):
                    MM(bk(b, n), wq[:, kc, oc * 128:(oc + 1) * 128], xn[:, kc, t0:t0 + n], kc == 0, kc == 7,
                       ["wq2", f"xn{cid}"], [f"B{b}"], sig=(kc == 7))
                ACT(sqb[i2][:, 0:n], bk(b, n), AF.Square, [f"B{b}"], [f"sqb{i2}"])
                b2 = bank()
                MM(bk(b2, n), blk, sqb[i2][:, 0:n], True, True, [f"sqb{i2}", "cb"], [f"B{b2}"])
                ACT(rr[i2][:, 0:n], bk(b2, n), AF.Ln, [f"B{b2}"], [f"rr{i2}"], bias=EPS, scale=1.0 / 64)
                ACT(rr[i2][:, 0:n], rr[i2][:, 0:n], AF.Exp, [f"rr{i2}"], [f"rr{i2}"], scale=-0.5)
                gcol = l2v[:, 1:2] if isk else l2v[:, 0:1]
                dst = K2[:, oc - 8, t0:t0 + n] if isk else QO[:, oc, t0:t0 + n]
                dname = f"K2_{cid}" if isk else f"QO{oc}_{cid}"
                if isctx:
                    STT("dve", dst, bk(b, n), gcol, rr[i2][:, 0:n], ALU.mult, ALU.mult, [f"B{b}", f"rr{i2}", "l2v"], [dname])
                    continue
                STT("dve", qn[i2][:, 0:n], bk(b, n), gcol, rr[i2][:, 0:n], ALU.mult, ALU.mult, [f"B{b}", f"rr{i2}", "l2v"], [f"qn{i2}"])
                b3 = bank()
                MM(bk(b3, n), rotm, qn[i2][:, 0:n], True, True, [f"qn{i2}", "cb"], [f"B{b3}"])
                TT("dve", t1[i2][:, 0:n], qn[i2][:, 0:n], cos_[:, t0:t0 + n], ALU.mult, [f"qn{i2}", "cos"], [f"t1{i2}"])
                TT("dve", t2[i2][:, 0:n], bk(b3, n), sin_[:, t0:t0 + n], ALU.mult, [f"B{b3}", "sin"], [f"t2{i2}"])
                TT("pool", dst, t1[i2][:, 0:n], t2[i2][:, 0:n], ALU.add, [f"t1{i2}", f"t2{i2}"], [dname])
        for i in range(21):
            isctx = i >= 19
            tok0 = 128 * i if not isctx else NW + 128 * (i - 19)
            rows = min(128, NW - tok0) if not isctx else 128
            xr = [f"xn{c}" for c in sorted({chunk_id(tok0), chunk_id(tok0 + rows - 1)})]
            b = bank()
            for kc in range(8):
                MM(ps[0:rows, b * 512:b * 512 + 256], xn[:, kc, tok0:tok0 + rows], wq[:, kc, 1536:1792], kc == 0, kc == 7,
                   ["wq2"] + xr, [f"B{b}"], sig=(kc == 7))
            CP(evac_eng(), VP[0:rows, i, 64:576].rearrange("p (k c) -> p k c", c=128)[:, :, 0:64],
               ps[0:rows, b * 512:b * 512 + 256].rearrange("p (k c) -> p k c", c=64), [f"B{b}"], ["VP"])
        P.barrier()
        HA.reset(vsave)
        mk = HA.alloc((6, 512), BF16)
        esk = HA.alloc((8,), F32)
        vcol = HA.alloc((19,), F32)
        den = HA.alloc((512,), F32)
        pt = [HA.alloc((1024,), BF16) for _ in range(3)]
        DMA(mk[:, 0:3, :].rearrange("p a b -> p (a b)"), maskda[:, :], [], ["mk"], q="pool")
        DMA(mk[:, 3:6, :].rearrange("p a b -> p (a b)"), maskdb[:, :], [], ["mk"], q="pool")
        ACT(esk[:, :], l2v[:, 2:10], AF.Exp, ["l2v"], ["esk"])
        for i in range(19):
            rows = min(128, NW - 128 * i)
            DMA(vcol[0:rows, i:i + 1], validd[0:1, 128 * i:128 * i + rows].rearrange("o (p x) -> (o p) x", x=1), [], ["vcol"])
        pctr = 0; sctr = 0
        scale = 1.0 / 8.0
        for c in range(8):
            kvh = c // 2
            for (t0, n) in CH_B:
                cid = chunk_id(t0)
                qname = f"QO{c}_{cid}"
                k0 = t0 - 128
                keys = []
                for j in range(6):
                    ks = k0 + 128 * j
                    if ks >= NW or 128 * (j - 1) - (n - 1) > 128:
                        continue
                    keys.append(("w", ks // 128, j))
                keys += [("c", 19, None), ("c", 20, None)]
                nk = len(keys)
                for ki, (kind, ti, j) in enumerate(keys):
                    sb = (sctr % 2) * 2; sctr += 1
                    if kind == "w":
                        rows = min(128, NW - 128 * ti)
                        kcol = 128 * ti
                    else:
                        rows = 128
                        kcol = NW + 128 * (ti - 19)
                    kn = f"K2_{chunk_id(kcol)}"
                    MM(ps[0:rows, sb * 512:sb * 512 + n], K2[0:64, kvh, kcol:kcol + rows], QO[0:64, c, t0:t0 + n], True, True,
                       [kn, qname], [f"B{sb}"], sig=False, tp=(0, 0))
                    MM(ps[0:rows, (sb + 1) * 512:(sb + 1) * 512 + n], K2[64:128, kvh, kcol:kcol + rows], QO[64:128, c, t0:t0 + n], True, True,
                       [kn, qname], [f"B{sb + 1}"], tp=(64, 0))
                    p_ = pt[pctr % 3]; pn = f"pt{pctr % 3}"; pctr += 1
                    pv = p_[0:rows, :].rearrange("p (a b) -> p a b", a=2)[:, :, 0:n]
                    ACT(pv, ps[0:rows, sb * 512:sb * 512 + 1024].rearrange("p (a b) -> p a b", a=2)[:, :, 0:n],
                        AF.Exp, [f"B{sb}", f"B{sb + 1}"], [pn], scale=scale)
                    if kind == "w":
                        for m in range(2):
                            STT("dve", p_[0:rows, m * 512:m * 512 + n], p_[0:rows, m * 512:m * 512 + n], vcol[0:rows, ti:ti + 1],
                                mk[0:rows, j, 0:n], ALU.mult, ALU.mult, [pn, "vcol", "mk"], [pn])
                    first = ki == 0; last = ki == nk - 1
                    va = VP[0:rows, ti, 64 + 128 * kvh:64 + 128 * kvh + 128]
                    vb = VP[0:rows, ti, 128 * kvh:128 * kvh + 128]
                    MM(bk(4, n), va, p_[0:rows, 0:n], first, False, ["VP", pn], ["B4"], sig=False)
                    MM(bk(4, n), vb, p_[0:rows, 512:512 + n], False, last, ["VP", pn], ["B4"], sig=False)
                    MM(bk(6, n), E0[0:rows, :], p_[0:rows, 0:n], first, False, [pn, "cb"], ["B6"], sig=False)
                    MM(bk(6, n), E1[0:rows, :], p_[0:rows, 512:512 + n], False, last, [pn, "cb"], ["B6"])
                TS("dve", den[:, 0:n], bk(6, n), esk[:, c:c + 1], None, ALU.add, None, ["B6", "esk"], ["den"])
                RECIP(den[:, 0:n], den[:, 0:n], ["den"], ["den"])
                TT("dve", QO[:, c, t0:t0 + n], bk(4, n), den[:, 0:n], ALU.mult, ["B4", "den"], [qname])
        P.barrier()
        for fc in range(8):
            DMA(hreg[:, fc * NT:(fc + 1) * NT], hspill[:, fc * NT:(fc + 1) * NT], ["hspill"], [f"h{c}" for c in range(6)])
        TA.reset(tsave)
        wo = TA.alloc((8, 1024), BF16)
        load_w(wo, "l2_wo", "wo2")
        P.barrier()
        for (t0, n) in CH_B:
            cid = chunk_id(t0)
            for dc in range(8):
                b = bank()
                for kc in range(8):
                    MM(bk(b, n), wo[:, kc, dc * 128:(dc + 1) * 128], QO[:, kc, t0:t0 + n], kc == 0, kc == 7,
                       ["wo2", f"QO{kc}_{cid}"], [f"B{b}"], sig=(kc == 7))
                STT("dve", hT[:, dc, t0:t0 + n], bk(b, n), modap(l, 16 + dc, 0), hT[:, dc, t0:t0 + n], ALU.mult, ALU.add,
                    [f"B{b}", "modsb", f"h{cid}"], [f"h{cid}"])
        P.barrier()
        ffn(l, CH_B)


    CH_OWN = [(HALO + 512 * i, 512) for i in range(4)]

    def layer3():
        l = 3
        TA.reset()
        xn = TA.alloc((8, NW), BF16)
        U = TA.alloc((8, NW), BF16)
        tsave = TA.off
        make_xn(xn, l, 0, CH_B, TA)
        P.barrier()
        TA.reset(tsave)
        glu = TA.alloc((NW,), F32)
        accA = TA.alloc((NW,), F32)
        accB = TA.alloc((NW,), F32)
        wst = [TA.alloc((8, 2, 128), BF16) for _ in range(2)]
        sg = TA.alloc((512,), F32)
        o0, o1 = HALO, HALO + OWN
        for fc in range(8):
            w_ = wst[fc % 2]; wn = f"wst{fc % 2}"
            for kc in range(8):
                src, g = wsrc("l3_pw1", kc)
                for j in range(2):
                    DMA(w_[:, kc, j, :], src[:, j * 1024 + fc * 128:j * 1024 + (fc + 1) * 128], [g], [wn], q="pool")
            for (t0, n) in CH_B:
                cid = chunk_id(t0)
                ba = bank(); bg = bank()
                for j, b in ((0, ba), (1, bg)):
                    for kc in range(8):
                        MM(bk(b, n), w_[:, kc, j, :], xn[:, kc, t0:t0 + n], kc == 0, kc == 7, [wn, f"xn{cid}"], [f"B{b}"], sig=(kc == 7))
                ACT(sg[:, 0:n], bk(bg, n), AF.Sigmoid, [f"B{bg}", "l3v"], ["sg"], bias=l3v[:, fc, 1:2])
                STT("dve", glu[:, t0:t0 + n], bk(ba, n), l3v[:, fc, 0:1], sg[:, 0:n], ALU.add, ALU.mult, [f"B{ba}", "sg", "l3v"], ["glu"])
            TT("dve", glu[:, 128:2208], glu[:, 128:2208], validb[:, 128:2208], ALU.mult, ["glu", "validb"], ["glu"])
            for k in range(31):
                src_ = glu[:, o0 + k - 15:o1 + k - 15]
                wk = l3v[:, fc, 8 + k:9 + k]
                if k == 0:
                    TS("dve", accA[:, o0:o1], src_, wk, None, ALU.mult, None, ["glu", "l3v"], ["accA"])
                else:
                    STT("dve", accA[:, o0:o1], src_, wk, accA[:, o0:o1], ALU.mult, ALU.add, ["glu", "l3v", "accA"], ["accA"])
            TS("dve", U[:, fc, o0:o1], accA[:, o0:o1], l3v[:, fc, 2:3], None, ALU.add, None, ["accA", "l3v"], ["U"])
        P.barrier()
        TA.reset(tsave)
        usq = TA.alloc((8, 512), BF16)
        mean = TA.alloc((512,), F32); msq = TA.alloc((512,), F32); rstd = TA.alloc((512,), F32)
        tmp = [TA.alloc((512,), F32) for _ in range(2)]
        tmpb = [TA.alloc((512,), F32) for _ in range(2)]
        wo = TA.alloc((8, 1024), BF16)
        load_w(wo, "l3_pw2", "wo3")
        for (t0, n) in CH_OWN:
            cid = chunk_id(t0)
            ACT(usq[:, :, 0:n], U[:, :, t0:t0 + n], AF.Square, ["U"], ["usq"])
            b1 = bank()
            for fc in range(8):
                MM(bk(b1, n), onesb, U[:, fc, t0:t0 + n], fc == 0, fc == 7, ["U", "cb"], [f"B{b1}"], sig=(fc == 7))
            b2 = bank()
            for fc in range(8):
                MM(bk(b2, n), onesb, usq[:, fc, 0:n], fc == 0, fc == 7, ["usq", "cb"], [f"B{b2}"], sig=(fc == 7))
            ACT(mean[:, 0:n], bk(b1, n), AF.Identity, [f"B{b1}"], ["mean"], scale=1.0 / D)
            TT("dve", msq[:, 0:n], mean[:, 0:n], mean[:, 0:n], ALU.mult, ["mean"], ["msq"])
            STT("dve", rstd[:, 0:n], bk(b2, n), 1.0 / D, msq[:, 0:n], ALU.mult, ALU.subtract, [f"B{b2}", "msq"], ["rstd"])
            ACT(rstd[:, 0:n], rstd[:, 0:n], AF.Ln, ["rstd"], ["rstd"], bias=EPS)
            ACT(rstd[:, 0:n], rstd[:, 0:n], AF.Exp, ["rstd"], ["rstd"], scale=-0.5)
            for fc in range(8):
                t_ = tmp[fc % 2]; tn = f"tmp{fc % 2}"
                TT("dve", t_[:, 0:n], U[:, fc, t0:t0 + n], mean[:, 0:n], ALU.subtract, ["U", "mean"], [tn])
                TT("dve", t_[:, 0:n], t_[:, 0:n], rstd[:, 0:n], ALU.mult, [tn, "rstd"], [tn])
                ACT(U[:, fc, t0:t0 + n], t_[:, 0:n], AF.Silu, [tn, "l3v"], [f"S{cid}"], bias=l3v[:, fc, 4:5], scale=l3v[:, fc, 3:4])
        for (t0, n) in CH_OWN:
            cid = chunk_id(t0)
            for dc in range(8):
                b = bank()
                for kc in range(8):
                    MM(bk(b, n), wo[:, kc, dc * 128:(dc + 1) * 128], U[:, kc, t0:t0 + n], kc == 0, kc == 7,
                       ["wo3", f"S{cid}", "U"], [f"B{b}"], sig=(kc == 7))
                tb = tmpb[dc % 2]; tbn = f"tmpb{dc % 2}"
                ACT(tb[:, 0:n], bk(b, n), AF.Identity, [f"B{b}", "l3v"], [tbn], bias=l3v[:, dc, 5:6])
                STT("dve", hT[:, dc, t0:t0 + n], tb[:, 0:n], modap(l, 16 + dc, 0), hT[:, dc, t0:t0 + n], ALU.mult, ALU.add,
                    [tbn, "modsb", f"h{cid}"], [f"h{cid}"])
        P.barrier()
        ffn(l, CH_OWN)

    TA.reset()
    load_xT(TA)
    P.barrier()
    if nlayers >= 1:
        layer0()
    if nlayers >= 2:
        layer1()
    if nlayers >= 3:
        layer2()
    if nlayers >= 4:
        layer3()

    P.barrier()
    TA.reset()
    ot = [TA.alloc((1024,), F32) for _ in range(2)]
    outs = []
    for i in range(16):
        tok0 = HALO + 128 * i
        o2 = ot[i % 2]
        hr = [f"h{c}" for c in sorted({chunk_id(tok0), chunk_id(tok0 + 127)})]
        for half in range(2):
            b = bank()
            for f4 in range(4):
                fc = half * 4 + f4
                MM(ps[:, b * 512 + f4 * 128:b * 512 + (f4 + 1) * 128], hT[:, fc, tok0:tok0 + 128], ident, True, True,
                   hr + ["constf"], [f"B{b}"], sig=(f4 == 3))
            CP(evac_eng(), o2[:, half * 512:(half + 1) * 512], bk(b), [f"B{b}"], [f"ot{i % 2}"])
        outs.append(DMA(outd[i * 128:(i + 1) * 128, :], o2[:, :], [f"ot{i % 2}"], ["out"]))
    nsem = P.finalize(final_wait_ops=outs)
    return nc, P, nsem


def _rope_tables():
    n_freq = 16
    inv = (10000.0 ** (-np.arange(n_freq, dtype=np.float32) / np.float32(n_freq))).astype(np.float32)
    t = np.arange(SEQ)
    rows = (t // 64).astype(np.float32); cols = (t % 64).astype(np.float32)
    ang_r = rows[:, None] * inv[None, :]; ang_c = cols[:, None] * inv[None, :]
    ang = np.concatenate([ang_r, ang_r, ang_c, ang_c], axis=-1).astype(np.float32)
    return np.cos(ang).astype(np.float32), np.sin(ang).astype(np.float32)


def _consts():
    ident = np.eye(128, dtype=np.float32)
    ones = np.ones((128, 128), np.float32)
    blk = np.zeros((128, 128), np.float32); blk[:64, :64] = 1; blk[64:, 64:] = 1
    R = np.zeros((128, 128), np.float32)
    for base in (0, 64):
        for j in range(16):
            R[base + 16 + j, base + j] = -1.0
            R[base + j, base + 16 + j] = 1.0
            R[base + 48 + j, base + 32 + j] = -1.0
            R[base + 32 + j, base + 48 + j] = 1.0
    E0 = np.zeros((128, 128), np.float32); E0[:, :64] = 1
    E1 = np.zeros((128, 128), np.float32); E1[:, 64:] = 1
    constf = np.concatenate([ident, ones], 1)
    constb = np.concatenate([ones, blk, R, E0, E1], 1)
    p = np.arange(128)[:, None]; f = np.arange(512)[None, :]
    masks = [(np.abs(128 * (j - 1) + p - f) <= 128).astype(np.float32) for j in range(6)]
    return constf, constb, np.concatenate(masks, 1)


_CACHE = {}


def kernel(**inp):
    nlayers = int(inp.pop("_nlayers", NLAYERS_DEFAULT))
    f32 = lambda a: np.ascontiguousarray(np.asarray(a, dtype=np.float32))
    x = f32(inp["x"])[0]; ctx = f32(inp["ctx"])[0]
    xpad = np.zeros((SEQ + 2 * HALO, D), np.float32); xpad[HALO:HALO + SEQ] = x
    cosf, sinf = _rope_tables()
    cpad = np.ones((SEQ + 2 * HALO, 64), np.float32); cpad[HALO:HALO + SEQ] = cosf
    spad = np.zeros((SEQ + 2 * HALO, 64), np.float32); spad[HALO:HALO + SEQ] = sinf
    vpad = np.zeros((SEQ + 2 * HALO,), np.float32); vpad[HALO:HALO + SEQ] = 1.0
    constf, constb, maskd = _consts()
    cvec = np.stack([f32(inp["c"])[0].reshape(8, 128).T, f32(inp["c_ctx"]).reshape(8, 128).T], -1)
    ada_w = f32(inp["ada_w"]); ada_b = f32(inp["ada_b"])
    normd = np.stack([f32(inp["norm1"]).reshape(4, 8, 128), f32(inp["norm2"]).reshape(4, 8, 128)], 1)
    normd = np.ascontiguousarray(normd.transpose(3, 0, 1, 2))
    w = f32(inp["diff_w_qkv"])[0]
    wq = w[:, :1024].reshape(D, 2, 8, 64).transpose(0, 2, 1, 3).reshape(D, 1024)
    wk = w[:, 1024:2048].reshape(D, 2, 8, 64).transpose(0, 2, 1, 3).reshape(D, 1024)
    l0_qkv = np.ascontiguousarray(np.concatenate([wq, wk, w[:, 2048:]], 1))
    l0_vec = np.zeros((128, 8), np.float32)
    l0_vec[:, 0] = np.tile(f32(inp["diff_q_norm"])[0], 2); l0_vec[:, 1] = np.tile(f32(inp["diff_k_norm"])[0], 2)
    l0_vec[:, 2] = f32(inp["diff_subln"])[0]
    for j, nm in enumerate(["diff_lam_q1", "diff_lam_k1", "diff_lam_q2", "diff_lam_k2"]):
        l0_vec[:64, 3 + j] = f32(inp[nm])[0]
    w2 = f32(inp["win_w_qkv"])[0]
    k2 = w2[:, 1024:1280].reshape(D, 4, 1, 64).repeat(2, axis=2).reshape(D, 512)
    l2_qkv = np.ascontiguousarray(np.concatenate([w2[:, :1024], k2, w2[:, 1280:]], 1))
    l2_vec = np.zeros((128, 16), np.float32)
    l2_vec[:, 0] = np.tile(f32(inp["win_q_norm"])[0], 2); l2_vec[:, 1] = np.tile(f32(inp["win_k_norm"])[0], 2)
    sink = f32(inp["win_sink"])[0]
    for c in range(8):
        l2_vec[:64, 2 + c] = sink[2 * c]; l2_vec[64:, 2 + c] = sink[2 * c + 1]
    l1_conv = np.ascontiguousarray(f32(inp["sc_conv_w"])[0].reshape(3, 8, 128).transpose(2, 1, 0))
    l3_vec = np.zeros((128, 8, 40), np.float32)
    pb = f32(inp["cf_b_pw1"])[0]
    l3_vec[:, :, 0] = pb[:1024].reshape(8, 128).T; l3_vec[:, :, 1] = pb[1024:].reshape(8, 128).T
    l3_vec[:, :, 2] = f32(inp["cf_dw_b"])[0].reshape(8, 128).T
    l3_vec[:, :, 3] = f32(inp["cf_ln_g"])[0].reshape(8, 128).T
    l3_vec[:, :, 4] = f32(inp["cf_ln_b"])[0].reshape(8, 128).T
    l3_vec[:, :, 5] = f32(inp["cf_b_pw2"])[0].reshape(8, 128).T
    l3_vec[:, :, 8:39] = f32(inp["cf_dw_w"])[0].reshape(31, 8, 128).transpose(2, 1, 0)
    wts = {"l0_qkv": l0_qkv, "l0_wo": f32(inp["diff_w_o"])[0],
           "l1_win": f32(inp["sc_w_in"])[0], "l1_wout": f32(inp["sc_w_out"])[0],
           "l2_qkv": l2_qkv, "l2_wo": f32(inp["win_w_o"])[0],
           "l3_pw1": f32(inp["cf_w_pw1"])[0], "l3_pw2": f32(inp["cf_w_pw2"])[0]}
    gu = f32(inp["ffn_w_gate_up"]); dn = f32(inp["ffn_w_down"])
    for l in range(4):
        wts[f"ffn_gu{l}"] = gu[l]
        dpad = np.zeros((24 * 128, D), np.float32); dpad[:DFF] = dn[l]
        wts[f"ffn_d{l}"] = dpad

    def pack(spec, r):
        parts = []
        for name, n, cpr in spec:
            parts.append(wts[name][r * cpr * 128:(r + 1) * cpr * 128, :].reshape(-1))
        return np.ascontiguousarray(np.concatenate(parts).reshape(-1, 512))

    shared = dict(ctxa=np.ascontiguousarray(ctx[:128]), ctxb=np.ascontiguousarray(ctx[128:]), cvec=f32(cvec), normd=normd,
                  constf=constf, constb=constb, maskda=np.ascontiguousarray(maskd[:, :1536]), maskdb=np.ascontiguousarray(maskd[:, 1536:]),
                  l0_vec=l0_vec, l1_conv=l1_conv, l2_vec=l2_vec, l3_vec=l3_vec)
    in_maps = []
    for i in range(NCORES):
        s0 = OWN * i
        m = dict(shared)
        m["xw"] = np.ascontiguousarray(xpad[s0:s0 + NW])
        m["cosd"] = np.ascontiguousarray(np.tile(cpad[s0:s0 + NW].T, (2, 1)))
        m["sind"] = np.ascontiguousarray(np.tile(spad[s0:s0 + NW].T, (2, 1)))
        m["validd"] = np.ascontiguousarray(np.broadcast_to(vpad[s0:s0 + NW][None, :], (128, NW)))
        m["wA"] = pack(WSPEC_A, i); m["wB"] = pack(WSPEC_B, i)
        m["adaw"] = np.ascontiguousarray(ada_w[:, :, i * 768:(i + 1) * 768])
        m["adab"] = np.ascontiguousarray(ada_b[:, i * 768:(i + 1) * 768].reshape(4, 6, 128).transpose(2, 0, 1).reshape(128, 24))
        in_maps.append(m)
    key = nlayers
    if key not in _CACHE:
        _CACHE[key] = build_program(nlayers)[0]
    nc = _CACHE[key]
    res = run_bass_kernel_spmd(nc, in_maps, core_ids=list(range(NCORES)))
    out = np.concatenate([np.asarray(r["out"], dtype=np.float32) for r in res.results], 0)
    return out.reshape(1, SEQ, D)
```

```python
import math
import numpy as np
import concourse.bass as bass
import concourse.mybir as mybir
from concourse.bass_utils import run_bass_kernel_spmd

F32 = mybir.dt.float32
BF16 = mybir.dt.bfloat16
AF = mybir.ActivationFunctionType
ALU = mybir.AluOpType

NCORES = 8
D = 1024
SEQ = 16384
OWN = SEQ // NCORES
HALO = 144
NW = OWN + 2 * HALO
NCTX = 256
NT = NW + NCTX
DFF = 2816
EPS = 1e-6
NLAYERS_DEFAULT = 4

ENG_NAMES = ("pe", "act", "dve", "pool", "sp")
EPOCH = 12000
NDSEM = 6


class Op:
    __slots__ = ("eng", "fn", "reads", "writes", "dma", "sig", "idx", "waits", "cnt", "dslot", "cc")

    def __init__(self, eng, fn, reads, writes, dma=False, sig=True, cc=False):
        self.eng = eng; self.fn = fn; self.reads = tuple(reads); self.writes = tuple(writes)
        self.dma = dma; self.sig = sig; self.waits = []; self.cnt = None; self.dslot = None; self.cc = cc


class Prog:
    def __init__(self, nc, selfsync=True):
        self.nc = nc
        self.ops = []
        self.selfsync = selfsync
        self.ncc = 0

    def add(self, eng, fn, reads=(), writes=(), dma=False, sig=True, cc=False):
        op = Op(eng, fn, list(reads) + ["__phase__"], writes, dma, sig, cc)
        op.idx = len(self.ops)
        self.ops.append(op)
        return op

    def barrier(self):
        nc = self.nc
        op = Op("dve", lambda e: e.memset(self._bar[:, :], 0.0), [], ["__phase__"])
        op.idx = len(self.ops)
        self.ops.append(op)

    def finalize(self, final_wait_ops=()):
        nc = self.nc
        ops = self.ops
        cnt = {e: 0 for e in ENG_NAMES}
        dcount = {}
        dn = {e: 0 for e in ENG_NAMES}
        for op in ops:
            if op.cc:
                self.ncc += 1
                op.dslot = (("cc", self.ncc), 1)
            elif op.dma:
                slot = dn[op.eng] % NDSEM
                dn[op.eng] += 1
                k = (op.eng, slot)
                dcount[k] = dcount.get(k, 0) + 1
                op.dslot = (k, dcount[k])
            elif op.sig:
                cnt[op.eng] += 1
                op.cnt = cnt[op.eng]
        nxt = {e: None for e in ENG_NAMES}
        for op in reversed(ops):
            if op.dma or op.cc:
                continue
            if op.sig:
                nxt[op.eng] = op
            else:
                op.cnt = ("fwd", nxt[op.eng])
        last_writer = {}
        readers = {}
        seen = {e: {} for e in ENG_NAMES}
        n_waits = 0
        for op in ops:
            deps = set()
            for r in op.reads:
                w = last_writer.get(r)
                if w is not None: deps.add(w)
            for wr in op.writes:
                w = last_writer.get(wr)
                if w is not None: deps.add(w)
                rl = readers.get(wr)
                if rl:
                    deps.update(rl.values() if isinstance(rl, dict) else rl)
            waits = {}
            if op.dma and not op.cc:
                k, c = op.dslot
                if c > 1:
                    waits[("d",) + k] = 16 * (c - 1)
            for d in deps:
                if d is op: continue
                if d.cc:
                    key = ("d",) + d.dslot[0]; val = 1
                elif d.dma:
                    k, c = d.dslot
                    key = ("d",) + k; val = 16 * c
                else:
                    tgt = d
                    if not d.sig:
                        tgt = d.cnt[1]
                        assert tgt is not None, f"no signalling op after {d.idx}"
                    if tgt.eng == op.eng and not op.dma and not op.cc:
                        if op.eng == "pe" or not self.selfsync:
                            continue
                        if tgt.idx >= op.idx:
                            continue
                    assert tgt.idx < op.idx, f"signal op {tgt.idx} after waiter {op.idx} (dep {d.idx})"
                    ep, v = divmod(tgt.cnt - 1, EPOCH)
                    key = ("c", tgt.eng, ep); val = v + 1
                if waits.get(key, 0) < val: waits[key] = val
            for key, val in waits.items():
                if seen[op.eng].get(key, 0) >= val: continue
                seen[op.eng][key] = val
                op.waits.append((key, val))
                n_waits += 1
            for r in op.reads:
                if r == "__phase__":
                    readers.setdefault(r, {})
                    rk = op.dslot[0] if (op.dma or op.cc) else op.eng
                    if op.sig or op.dma or op.cc:
                        readers[r][rk] = op
                else:
                    readers.setdefault(r, []).append(op)
            for wr in op.writes:
                last_writer[wr] = op
                readers[wr] = {} if wr == "__phase__" else []
        self.n_waits = n_waits
        sems = {}

        def sem(key):
            if key not in sems:
                sems[key] = nc.alloc_semaphore("s_" + "_".join(str(k) for k in key))
            return sems[key]

        fin = []
        for op in final_wait_ops:
            k, c = op.dslot
            fin.append((("d",) + k, 16 * c))
        per_eng = {e: [op for op in ops if op.eng == e] for e in ENG_NAMES}
        with nc.cleanup_on_exit():
            for op in ops:
                for key, val in op.waits:
                    sem(key)
                if op.cc or op.dma:
                    sem(("d",) + op.dslot[0])
                elif op.sig:
                    sem(("c", op.eng, (op.cnt - 1) // EPOCH))
            for key, val in fin:
                sem(key)
            for h_ in sems.values():
                nc.gpsimd.sem_clear(h_)
            nc.all_engine_barrier()
            with nc.Block() as block:
                def emit(ename):
                    def body(eng):
                        for op in per_eng[ename]:
                            for key, val in op.waits:
                                eng.wait_ge(sem(key), val)
                            ins = op.fn(eng)
                            if op.cc:
                                ins.then_inc(sem(("d",) + op.dslot[0]))
                            elif op.dma:
                                k, c = op.dslot
                                ins.then_inc(sem(("d",) + k), 16)
                            elif op.sig:
                                ep = (op.cnt - 1) // EPOCH
                                ins.then_inc(sem(("c", ename, ep)), 1)
                        if ename == "sp":
                            for key, val in fin:
                                eng.wait_ge(sem(key), val)
                    return body
                block.tensor(emit("pe"))
                block.scalar(emit("act"))
                block.vector(emit("dve"))
                block.gpsimd(emit("pool"))
                block.sync(emit("sp"))
            nc.all_engine_barrier()
        return len(sems)


def _fix_phase_readers(readers_val):
    return readers_val.values() if isinstance(readers_val, dict) else readers_val


class Arena:
    def __init__(self, ap32, nbytes):
        self.ap32 = ap32
        self.nbytes = nbytes
        self.off = 0

    def reset(self, off=0):
        self.off = off

    def alloc(self, free_shape, dtype):
        n = int(np.prod(free_shape))
        esz = 4 if dtype == F32 else 2
        nb = (n * esz + 31) // 32 * 32
        assert self.off + nb <= self.nbytes, f"arena overflow {self.off}+{nb}>{self.nbytes}"
        a = self.ap32[:, self.off // 4:(self.off + nb) // 4]
        self.off += nb
        if dtype != F32:
            a = a.bitcast(dtype)
        a = a[:, 0:n]
        if len(free_shape) == 2:
            a = a.rearrange("p (a b) -> p a b", a=free_shape[0])
        elif len(free_shape) == 3:
            a = a.rearrange("p (a b c) -> p a b c", a=free_shape[0], b=free_shape[1])
        return a


WSPEC_A = [("l0_qkv", 3072, 1), ("l0_wo", 1024, 1), ("ffn_gu0", 5632, 1), ("ffn_d0", 1024, 3)]
WSPEC_B = [("l1_win", 3072, 1), ("l1_wout", 1024, 1), ("ffn_gu1", 5632, 1), ("ffn_d1", 1024, 3),
           ("l2_qkv", 1792, 1), ("l2_wo", 1024, 1), ("ffn_gu2", 5632, 1), ("ffn_d2", 1024, 3),
           ("l3_pw1", 2048, 1), ("l3_pw2", 1024, 1), ("ffn_gu3", 5632, 1), ("ffn_d3", 1024, 3)]


def _wlayout(spec):
    offs = {}
    off = 0
    for name, n, cpr in spec:
        offs[name] = (off, n, cpr)
        off += cpr * 128 * n
    assert off % 512 == 0
    return offs, off


WOFF_A, WSIZE_A = _wlayout(WSPEC_A)
WOFF_B, WSIZE_B = _wlayout(WSPEC_B)

CH_LAT = [(0, 512), (512, 512), (1024, 512), (1536, 512), (2048, 288)]
CH_CTX = (NW, NCTX)
CH_B = [(128, 512), (640, 512), (1152, 512), (1664, 512), (2176, 32)]


def build_program(nlayers=NLAYERS_DEFAULT, debug_h=False):
    nc = bass.Bass("TRN2", target_bir_lowering=False)
    P = Prog(nc)

    def din(name, shape, dt=F32):
        return nc.dram_tensor(name, list(shape), dt, kind="ExternalInput").ap()

    xw = din("xw", [NW, D]); ctxa = din("ctxa", [128, D]); ctxb = din("ctxb", [128, D])
    cvec = din("cvec", [128, 8, 2])
    adaw = din("adaw", [4, D, 768]); adab = din("adab", [128, 24])
    normd = din("normd", [128, 4, 2, 8])
    cosd = din("cosd", [128, NW]); sind = din("sind", [128, NW])
    validd = din("validd", [128, NW])
    constf = din("constf", [128, 256])
    constb = din("constb", [128, 5 * 128])
    maskda = din("maskda", [128, 3 * 512]); maskdb = din("maskdb", [128, 3 * 512])
    wA = din("wA", [WSIZE_A // 512, 512]); wB = din("wB", [WSIZE_B // 512, 512])
    l0_vec = din("l0_vec", [128, 8]); l1_conv = din("l1_conv", [128, 8, 3])
    l2_vec = din("l2_vec", [128, 16])
    hspill = nc.dram_tensor("hspill", [128, 8 * NT], F32).ap(); l3_vec = din("l3_vec", [128, 8, 40])
    wAb = nc.dram_tensor("wAb", [WSIZE_A // 512, 512], F32).ap()
    wBb = nc.dram_tensor("wBb", [WSIZE_B // 512, 512], F32).ap()
    gA = nc.dram_tensor("gA", [NCORES * WSIZE_A // 512, 512], F32).ap()
    gB = nc.dram_tensor("gB", [NCORES * WSIZE_B // 512, 512], F32).ap()
    gAf = gA.rearrange("a b -> (a b)"); gBf = gB.rearrange("a b -> (a b)")

    def wsrc(name, i):
        if name in WOFF_A:
            (off, n, cpr), flat, rs = WOFF_A[name], gAf, WSIZE_A
        else:
            (off, n, cpr), flat, rs = WOFF_B[name], gBf, WSIZE_B
        base = (i // cpr) * rs + off + (i % cpr) * 128 * n
        return flat[base:base + 128 * n].rearrange("(p n) -> p n", n=n), ("gA" if name in WOFF_A else "gB")
    outd = [nc.dram_tensor(f"out{i}", [128, D], F32, kind="ExternalOutput").ap() for i in range(16)]
    kvloc = nc.dram_tensor("kvloc", [2048, 2048], BF16).ap()
    kvall = nc.dram_tensor("kvall", [NCORES * 2048, 2048], BF16).ap()
    modloc = nc.dram_tensor("modloc", [128, 48], F32).ap()
    modall = nc.dram_tensor("modall", [NCORES * 128, 48], F32).ap()

    ident_ones = nc.alloc_sbuf_tensor("ident_ones", [128, 256], F32)
    ident = ident_ones[:, 0:128]; onesf = ident_ones[:, 128:256]
    cb = nc.alloc_sbuf_tensor("cb", [128, 5 * 128], BF16)
    onesb = cb[:, 0:128]; blk = cb[:, 128:256]; rotm = cb[:, 256:384]; E0 = cb[:, 384:512]; E1 = cb[:, 512:640]
    validb = nc.alloc_sbuf_tensor("validb", [128, NW], BF16)
    modsb = nc.alloc_sbuf_tensor("modsb", [128, 4, 48, 2], F32)
    normsb = nc.alloc_sbuf_tensor("normsb", [128, 4, 2, 8], F32)
    Gt = nc.alloc_sbuf_tensor("Gt", [128, 4, 2, 2, 8], F32)
    bar = nc.alloc_sbuf_tensor("bar", [128, 8], F32)
    P._bar = bar
    small = nc.alloc_sbuf_tensor("small", [128, 64], F32)
    l0v = nc.alloc_sbuf_tensor("l0v", [128, 8], F32)
    l1cv = nc.alloc_sbuf_tensor("l1cv", [128, 8, 3], F32)
    l2v = nc.alloc_sbuf_tensor("l2v", [128, 16], F32)
    l3v = nc.alloc_sbuf_tensor("l3v", [128, 8, 40], F32)
    H_BYTES = NT * 8 * 4
    hreg = nc.alloc_sbuf_tensor("hreg", [128, H_BYTES // 4], F32)
    T_BYTES = 115 * 1024
    treg = nc.alloc_sbuf_tensor("treg", [128, T_BYTES // 4], F32)
    hT = hreg[:, :].rearrange("p (c t) -> p c t", c=8)
    HA = Arena(hreg[:, :], H_BYTES)
    TA = Arena(treg[:, :], T_BYTES)
    ps = nc.alloc_psum_tensor("ps", [128, 8 * 512], F32)
    bankctr = [0]

    def bank():
        i = bankctr[0] % 8
        bankctr[0] += 1
        return i

    def bk(i, n=512):
        return ps[:, i * 512:i * 512 + n]

    def MM(out, lhsT, rhs, st, sp, R, W, sig=True, tp=None):
        if tp is None:
            P.add("pe", lambda e: e.matmul(out, lhsT=lhsT, rhs=rhs, start=st, stop=sp), R, W, sig=sig)
        else:
            P.add("pe", lambda e: e.matmul(out, lhsT=lhsT, rhs=rhs, start=st, stop=sp, tile_position=tp), R, W, sig=sig)

    def ACT(out, in_, func, R, W, bias=None, scale=None):
        kw = {}
        if bias is not None: kw["bias"] = bias
        if scale is not None: kw["scale"] = scale
        P.add("act", lambda e: e.activation(out=out, in_=in_, func=func, **kw), R, W)

    def TT(eng, out, in0, in1, op, R, W):
        P.add(eng, lambda e: e.tensor_tensor(out=out, in0=in0, in1=in1, op=op), R, W)

    def STT(eng, out, in0, scalar, in1, op0, op1, R, W):
        P.add(eng, lambda e: e.scalar_tensor_tensor(out=out, in0=in0, scalar=scalar, in1=in1, op0=op0, op1=op1), R, W)

    def TS(eng, out, in0, s1, s2, op0, op1, R, W):
        if s2 is None:
            P.add(eng, lambda e: e.tensor_scalar(out=out, in0=in0, scalar1=s1, scalar2=None, op0=op0), R, W)
        else:
            P.add(eng, lambda e: e.tensor_scalar(out=out, in0=in0, scalar1=s1, scalar2=s2, op0=op0, op1=op1), R, W)

    def CP(eng, out, in_, R, W):
        if eng == "act":
            P.add("act", lambda e: e.copy(out=out, in_=in_), R, W)
        else:
            P.add(eng, lambda e: e.tensor_copy(out=out, in_=in_), R, W)

    def RECIP(out, in_, R, W):
        P.add("dve", lambda e: e.reciprocal(out=out, in_=in_), R, W)

    def DMA(out, in_, R, W, q="sp"):
        return P.add(q, lambda e: e.dma_start(out=out, in_=in_), R, W, dma=True)

    def MEMSET(eng, ap, val, W):
        P.add(eng, lambda e: e.memset(ap, val), [], W)

    def modap(l, q, s):
        return modsb[:, l, q, s:s + 1]

    def load_w(dst, wname, name, kc_n=8, c0=0, ncol=None):
        for kc in range(kc_n):
            src, g = wsrc(wname, kc)
            if ncol is not None:
                src = src[:, c0:c0 + ncol]
            DMA(dst[:, kc, :], src, [g], [f"{name}"], q="pool")

    evac_ctr = [0]

    def evac_eng():
        evac_ctr[0] += 1
        return "act" if evac_ctr[0] % 2 else "dve"

    DMA(ident_ones[:, :], constf[:, :], [], ["constf"])
    DMA(cb[:, :], constb[:, :], [], ["cb"], q="pool")
    DMA(validb[:, :], validd[:, :], [], ["validb"], q="pool")
    DMA(normsb[:, :, :, :], normd[:, :, :, :], [], ["normsb"])
    DMA(l0v[:, :], l0_vec[:, :], [], ["l0v"])
    DMA(l1cv[:, :, :], l1_conv[:, :, :], [], ["l1cv"])
    DMA(l2v[:, :], l2_vec[:, :], [], ["l2v"])
    DMA(l3v[:, :, :], l3_vec[:, :, :], [], ["l3v"])
    DMA(wAb[:, :], wA[:, :], [], ["wAb"])
    P.add("pool", lambda e: e.collective_compute("AllGather", ALU.bypass, replica_groups=[list(range(NCORES))],
                                                 ins=[wAb[:, :]], outs=[gA[:, :]]), ["wAb"], ["gA", "ccchain"], cc=True)
    DMA(wBb[:, :], wB[:, :], [], ["wBb"])

    TA.reset()
    cact = TA.alloc((8, 2), F32)
    csig = TA.alloc((8, 2), F32)
    adabs = TA.alloc((24,), F32)
    modl = TA.alloc((24, 2), F32)
    DMA(cact, cvec[:, :, :], [], ["cact"])
    DMA(adabs, adab[:, :], [], ["adabs"])
    ACT(csig, cact, AF.Silu, ["cact"], ["csig"])
    awb = [TA.alloc((8, 768), F32) for _ in range(2)]
    for l in range(4):
        wb_ = awb[l % 2]
        for kc in range(8):
            DMA(wb_[:, kc, :], adaw[l, kc * 128:(kc + 1) * 128, :], [], [f"awb{l % 2}"])
        for cc in range(6):
            b = bank()
            for kc in range(8):
                MM(ps[:, b * 512:b * 512 + 2], wb_[:, kc, cc * 128:(cc + 1) * 128], csig[:, kc, :], kc == 0, kc == 7,
                   [f"awb{l % 2}", "csig"], [f"B{b}"], sig=(kc == 7))
            TS("dve", modl[:, l * 6 + cc, :], ps[:, b * 512:b * 512 + 2], adabs[:, l * 6 + cc:l * 6 + cc + 1], None, ALU.add, None,
               [f"B{b}", "adabs"], ["modl"])
    DMA(modloc[:, :], modl.rearrange("p a b -> p (a b)"), ["modl"], ["modloc"])
    P.add("pool", lambda e: e.collective_compute("AllGather", ALU.bypass, replica_groups=[list(range(NCORES))],
                                                 ins=[modloc[:, :]], outs=[modall[:, :]]),
          ["modloc"], ["modall", "ccchain"], cc=True)
    for l in range(4):
        P.add("sp", lambda e, l=l: e.dma_start(
            out=modsb[:, l, :, :].rearrange("p (r c) s -> p r (c s)", r=8),
            in_=modall[:, l * 12:(l + 1) * 12].rearrange("(r p) cs -> p r cs", p=128)),
            ["modall"], ["modsb"], dma=True)
    for l in range(4):
        for sub in range(2):
            for s in range(2):
                q0 = 8 if sub == 0 else 32
                STT("dve", Gt[:, l, sub, s, :], modsb[:, l, q0:q0 + 8, s], 1.0, normsb[:, l, sub, :], ALU.add, ALU.mult,
                    ["modsb", "normsb"], ["Gt"])
    P.barrier()

    def load_xT(arena):
        stg = arena.alloc((4, 1024), F32)
        groups = [(xw, g * 512, min(512, NW - g * 512), g * 512) for g in range(5)] + [(None, 0, 256, NW)]
        for gi, (src, r0, n, t0) in enumerate(groups):
            ntile = (n + 127) // 128
            for i in range(ntile):
                rows = min(128, n - i * 128)
                if src is None:
                    sap = (ctxa, ctxb)[i][:, :]
                else:
                    sap = src[r0 + i * 128:r0 + i * 128 + rows, :]
                DMA(stg[0:rows, i, :], sap, [], ["xstg"])
            for fc in range(8):
                b = bank()
                for i in range(ntile):
                    rows = min(128, n - i * 128)
                    MM(ps[:, b * 512 + i * 128:b * 512 + i * 128 + rows], stg[0:rows, i, fc * 128:(fc + 1) * 128],
                       ident[0:rows, 0:rows], True, True, ["xstg", "constf"], [f"B{b}"], sig=(i == ntile - 1))
                CP(evac_eng(), hT[:, fc, t0:t0 + n], bk(b, n), [f"B{b}"], [f"h{gi}"])

    def chunk_id(t0):
        if t0 >= NW: return 5
        return t0 // 512

    def make_xn(xn, l, sub, chunks, tmpA):
        sq = tmpA.alloc((8, 512), BF16)
        rs = tmpA.alloc((512,), F32)
        tm = [tmpA.alloc((512,), F32) for _ in range(2)]
        shq = 0 if sub == 0 else 24
        for (t0, n) in chunks:
            s = 1 if t0 >= NW else 0
            hid = f"h{chunk_id(t0)}"
            ACT(sq[:, :, 0:n], hT[:, :, t0:t0 + n], AF.Square, [hid], ["sq"])
            b = bank()
            for fc in range(8):
                MM(bk(b, n), onesb, sq[:, fc, 0:n], fc == 0, fc == 7, ["sq", "cb"], [f"B{b}"], sig=(fc == 7))
            ACT(rs[:, 0:n], bk(b, n), AF.Ln, [f"B{b}"], ["rs"], bias=EPS, scale=1.0 / D)
            ACT(rs[:, 0:n], rs[:, 0:n], AF.Exp, ["rs"], ["rs"], scale=-0.5)
            for fc in range(8):
                t_ = tm[fc % 2]
                STT("dve", t_[:, 0:n], hT[:, fc, t0:t0 + n], Gt[:, l, sub, s, fc:fc + 1], rs[:, 0:n], ALU.mult, ALU.mult,
                    [hid, "Gt", "rs"], [f"tm{fc % 2}"])
                ACT(xn[:, fc, t0:t0 + n], t_[:, 0:n], AF.Identity, [f"tm{fc % 2}", "modsb"], [f"xn{chunk_id(t0)}"],
                    bias=modap(l, shq + fc, s))

    def proj_residual(w_sb, wname, inT, in_name, l, gq, chunks, kc_n=8, bias=None):
        for (t0, n) in chunks:
            s = 1 if t0 >= NW else 0
            cid = chunk_id(t0)
            for dc in range(8):
                b = bank()
                for kc in range(kc_n):
                    MM(bk(b, n), w_sb[:, kc, dc * 128:(dc + 1) * 128], inT[:, kc, t0:t0 + n], kc == 0, kc == kc_n - 1,
                       [wname, f"{in_name}{cid}"], [f"B{b}"], sig=(kc == kc_n - 1))
                if bias is None:
                    STT("dve", hT[:, dc, t0:t0 + n], bk(b, n), modap(l, gq + dc, s), hT[:, dc, t0:t0 + n], ALU.mult, ALU.add,
                        [f"B{b}", "modsb", f"h{cid}"], [f"h{cid}"])
                else:
                    raise NotImplementedError

    def ffn(l, chunks):
        TA.reset()
        xn2 = TA.alloc((8, NT), BF16)
        actb = TA.alloc((2, NT), BF16)
        gub = [TA.alloc((8, 2, 256), BF16) for _ in range(3)]
        wdb = [TA.alloc((2, 1024), BF16) for _ in range(3)]
        sgt = [TA.alloc((512,), BF16) for _ in range(2)]
        make_xn(xn2, l, 1, chunks, TA)
        for u in range(11):
            g_ = gub[u % 3]; wd_ = wdb[u % 3]
            c0 = u * 256
            for kc in range(8):
                src, g = wsrc(f"ffn_gu{l}", kc)
                DMA(g_[:, kc, 0, :], src[:, c0:c0 + 256], [g], [f"gub{u % 3}"], q="pool")
                DMA(g_[:, kc, 1, :], src[:, DFF + c0:DFF + c0 + 256], [g], [f"gub{u % 3}"], q="pool")
            for jj in range(2):
                src, g = wsrc(f"ffn_d{l}", 2 * u + jj)
                DMA(wd_[:, jj, :], src, [g], [f"wdb{u % 3}"], q="pool")
            for jj in range(2):
                for ci, (t0, n) in enumerate(chunks):
                    cid = chunk_id(t0)
                    bg = bank()
                    for kc in range(8):
                        MM(bk(bg, n), g_[:, kc, 0, jj * 128:(jj + 1) * 128], xn2[:, kc, t0:t0 + n], kc == 0, kc == 7,
                           [f"gub{u % 3}", f"xn{cid}"], [f"B{bg}"], sig=(kc == 7))
                    bu = bank()
                    for kc in range(8):
                        MM(bk(bu, n), g_[:, kc, 1, jj * 128:(jj + 1) * 128], xn2[:, kc, t0:t0 + n], kc == 0, kc == 7,
                           [f"gub{u % 3}", f"xn{cid}"], [f"B{bu}"], sig=(kc == 7))
                    st_ = sgt[ci % 2]
                    ACT(st_[:, 0:n], bk(bg, n), AF.Silu, [f"B{bg}"], [f"sgt{ci % 2}"])
                    TT("dve", actb[:, jj, t0:t0 + n], st_[:, 0:n], bk(bu, n), ALU.mult, [f"sgt{ci % 2}", f"B{bu}"], [f"act{jj}_{cid}"])
            for (t0, n) in chunks:
                s = 1 if t0 >= NW else 0
                cid = chunk_id(t0)
                for dc in range(8):
                    b = bank()
                    for jj in range(2):
                        MM(bk(b, n), wd_[:, jj, dc * 128:(dc + 1) * 128], actb[:, jj, t0:t0 + n], jj == 0, jj == 1,
                           [f"wdb{u % 3}", f"act{jj}_{cid}"], [f"B{b}"], sig=(jj == 1))
                    STT("dve", hT[:, dc, t0:t0 + n], bk(b, n), modap(l, 40 + dc, s), hT[:, dc, t0:t0 + n], ALU.mult, ALU.add,
                        [f"B{b}", "modsb", f"h{cid}"], [f"h{cid}"])
        P.barrier()

    def layer0():
        l = 0
        chunks = CH_LAT + [CH_CTX]
        TA.reset()
        xn = TA.alloc((8, NT), BF16)
        QO = TA.alloc((8, NT), BF16)
        ctxK = TA.alloc((8, NCTX), BF16)
        ctxV = TA.alloc((2, 1024), BF16)
        tsave = TA.off
        make_xn(xn, l, 0, chunks, TA)
        P.barrier()
        HA.reset()
        wq = HA.alloc((8, 3072), BF16)
        cos_ = HA.alloc((NW,), F32); sin_ = HA.alloc((NW,), F32)
        load_w(wq, "l0_qkv", "wq")
        DMA(cos_, cosd[:, :], [], ["cos"]); DMA(sin_, sind[:, :], [], ["sin"])
        TA.reset(tsave)
        sqb = [TA.alloc((512,), BF16) for _ in range(2)]
        rr = [TA.alloc((512,), F32) for _ in range(2)]
        qn = [TA.alloc((512,), BF16) for _ in range(2)]
        t1 = [TA.alloc((512,), F32) for _ in range(2)]
        t2 = [TA.alloc((512,), F32) for _ in range(2)]
        kst = [TA.alloc((512,), BF16) for _ in range(2)]
        vst = [TA.alloc((512,), BF16) for _ in range(2)]
        it = 0
        for (t0, n) in chunks:
            isctx = t0 >= NW
            cid = chunk_id(t0)
            for oc in range(16):
                isk = oc >= 8
                hh = oc % 8
                i2 = it % 2; it += 1
                b = bank()
                for kc in range(8):
                    MM(bk(b, n), wq[:, kc, oc * 128:(oc + 1) * 128], xn[:, kc, t0:t0 + n], kc == 0, kc == 7,
                       ["wq", f"xn{cid}"], [f"B{b}"], sig=(kc == 7))
                ACT(sqb[i2][:, 0:n], bk(b, n), AF.Square, [f"B{b}"], [f"sqb{i2}"])
                b2 = bank()
                MM(bk(b2, n), blk, sqb[i2][:, 0:n], True, True, [f"sqb{i2}", "cb"], [f"B{b2}"])
                ACT(rr[i2][:, 0:n], bk(b2, n), AF.Ln, [f"B{b2}"], [f"rr{i2}"], bias=EPS, scale=1.0 / 64)
                ACT(rr[i2][:, 0:n], rr[i2][:, 0:n], AF.Exp, [f"rr{i2}"], [f"rr{i2}"], scale=-0.5)
                gcol = l0v[:, 1:2] if isk else l0v[:, 0:1]
                if isctx:
                    dst = ctxK[:, hh, :] if isk else QO[:, hh, t0:t0 + n]
                    dname = "ctxK" if isk else f"QO{hh}_{cid}"
                    STT("dve", dst, bk(b, n), gcol, rr[i2][:, 0:n], ALU.mult, ALU.mult, [f"B{b}", f"rr{i2}", "l0v"], [dname])
                    continue
                STT("dve", qn[i2][:, 0:n], bk(b, n), gcol, rr[i2][:, 0:n], ALU.mult, ALU.mult, [f"B{b}", f"rr{i2}", "l0v"], [f"qn{i2}"])
                b3 = bank()
                MM(bk(b3, n), rotm, qn[i2][:, 0:n], True, True, [f"qn{i2}", "cb"], [f"B{b3}"])
                TT("dve", t1[i2][:, 0:n], qn[i2][:, 0:n], cos_[:, t0:t0 + n], ALU.mult, [f"qn{i2}", "cos"], [f"t1{i2}"])
                TT("dve", t2[i2][:, 0:n], bk(b3, n), sin_[:, t0:t0 + n], ALU.mult, [f"B{b3}", "sin"], [f"t2{i2}"])
                if not isk:
                    TT("pool", QO[:, hh, t0:t0 + n], t1[i2][:, 0:n], t2[i2][:, 0:n], ALU.add, [f"t1{i2}", f"t2{i2}"], [f"QO{hh}_{cid}"])
                else:
                    TT("pool", kst[i2][:, 0:n], t1[i2][:, 0:n], t2[i2][:, 0:n], ALU.add, [f"t1{i2}", f"t2{i2}"], [f"kst{i2}"])
                    a = max(t0, HALO); e_ = min(t0 + n, HALO + OWN)
                    DMA(kvloc[hh * 128:(hh + 1) * 128, a - HALO:e_ - HALO], kst[i2][:, a - t0:e_ - t0], [f"kst{i2}"], ["kvloc"])
        vt = 0
        for i in range(16 + 2):
            isctx = i >= 16
            tok0 = (HALO + 128 * i) if not isctx else (NW + 128 * (i - 16))
            xr = [f"xn{c}" for c in sorted({chunk_id(tok0), chunk_id(tok0 + 127)})]
            for hb in range(2):
                b = bank()
                for kc in range(8):
                    MM(bk(b), xn[:, kc, tok0:tok0 + 128], wq[:, kc, 2048 + hb * 512:2048 + (hb + 1) * 512], kc == 0, kc == 7,
                       ["wq"] + xr, [f"B{b}"], sig=(kc == 7))
                if isctx:
                    CP(evac_eng(), ctxV[:, i - 16, hb * 512:(hb + 1) * 512], bk(b), [f"B{b}"], ["ctxV"])
                else:
                    i2 = vt % 2; vt += 1
                    CP(evac_eng(), vst[i2][:, :], bk(b), [f"B{b}"], [f"vst{i2}"])
                    r0 = (8 + hb * 4) * 128
                    DMA(kvloc[r0:r0 + 512, i * 128:(i + 1) * 128].rearrange("(h p) d -> p h d", p=128),
                        vst[i2][:, :].rearrange("p (h d) -> p h d", h=4), [f"vst{i2}"], ["kvloc"])
        P.add("pool", lambda e: e.collective_compute("AllGather", ALU.bypass, replica_groups=[list(range(NCORES))],
                                                     ins=[kvloc[:, :]], outs=[kvall[:, :]]),
              ["kvloc"], ["kvall", "ccchain"], cc=True)
        P.add("pool", lambda e: e.collective_compute("AllGather", ALU.bypass, replica_groups=[list(range(NCORES))],
                                                     ins=[wBb[:, :]], outs=[gB[:, :]]), ["wBb"], ["gB", "ccchain"], cc=True)
        lamv = small[:, 0:4]
        TT("dve", small[0:64, 8:9], l0v[0:64, 3:4], l0v[0:64, 4:5], ALU.mult, ["l0v"], ["lamp"])
        TT("dve", small[0:64, 9:10], l0v[0:64, 5:6], l0v[0:64, 6:7], ALU.mult, ["l0v", "lamp"], ["lamp"])
        b = bank()
        MM(ps[:, b * 512:b * 512 + 2], onesf[0:64, :], small[0:64, 8:10], True, True, ["lamp", "constf"], [f"B{b}"])
        ACT(small[:, 10:12], ps[:, b * 512:b * 512 + 2], AF.Exp, [f"B{b}"], ["lame"])
        TT("dve", small[:, 12:13], small[:, 10:11], small[:, 11:12], ALU.subtract, ["lame"], ["lam"])
        lam_init = 0.8 - 0.6 * math.exp(-0.3 * 0)
        TS("dve", small[:, 13:14], small[:, 12:13], lam_init, -1.0, ALU.add, ALU.mult, ["lam"], ["neglam"])
        TS("dve", small[:, 14:15], l0v[:, 2:3], 1.0 - lam_init, None, ALU.mult, None, ["l0v"], ["sgain"])
        neglam = small[:, 13:14]; sgain = small[:, 14:15]
        P.barrier()
        HA.reset()
        Kh = HA.alloc((SEQ,), BF16)
        Vh = HA.alloc((128, 128), BF16)
        r0_ = HA.alloc((512,), F32); r1_ = HA.alloc((512,), F32)
        u0_ = HA.alloc((512,), F32); u1_ = HA.alloc((512,), F32)
        o_ = HA.alloc((512,), F32); lr_ = HA.alloc((512,), F32)
        osq = HA.alloc((512,), BF16)
        TA.reset(tsave)
        pt = [TA.alloc((1024,), BF16) for _ in range(3)]
        pctr = 0
        sctr = 0
        scale = 1.0 / 8.0
        for h in range(8):
            for r in range(NCORES):
                DMA(Kh[:, r * 2048:(r + 1) * 2048], kvall[r * 2048 + h * 128:r * 2048 + (h + 1) * 128, :], ["kvall"], [f"Kh{r}"])
                DMA(Vh[:, r * 16:(r + 1) * 16, :].rearrange("p a b -> p (a b)"),
                    kvall[r * 2048 + (8 + h) * 128:r * 2048 + (9 + h) * 128, :], ["kvall"], [f"Vh{r}"])
            for (t0, n) in chunks:
                isctx = t0 >= NW
                cid = chunk_id(t0)
                qname = f"QO{h}_{cid}"
                keys = ([] if isctx else [("g", kt) for kt in range(128)]) + [("c", 0), ("c", 1)]
                nk = len(keys)

                def emit_scores(ki):
                    kind, kt = keys[ki]
                    sb = (ki % 2) * 2
                    if kind == "g":
                        kl0 = Kh[0:64, kt * 128:(kt + 1) * 128]; kl1 = Kh[64:128, kt * 128:(kt + 1) * 128]
                        kr = [f"Kh{kt // 16}"]
                    else:
                        kl0 = ctxK[0:64, h, kt * 128:(kt + 1) * 128]; kl1 = ctxK[64:128, h, kt * 128:(kt + 1) * 128]
                        kr = ["ctxK"]
                    MM(bk(sb, n), kl0, QO[0:64, h, t0:t0 + n], True, True, kr + [qname], [f"B{sb}"], sig=False, tp=(0, 0))
                    MM(bk(sb + 1, n), kl1, QO[64:128, h, t0:t0 + n], True, True, kr + [qname], [f"B{sb + 1}"], tp=(64, 0))

                emit_scores(0)
                for ki, (kind, kt) in enumerate(keys):
                    sb = (ki % 2) * 2
                    if ki + 1 < nk:
                        emit_scores(ki + 1)
                    if kind == "g":
                        vl = Vh[:, kt, :]; vr = [f"Vh{kt // 16}"]
                    else:
                        vl = ctxV[:, kt, h * 128:(h + 1) * 128]; vr = ["ctxV"]
                    p_ = pt[pctr % 3]; pn = f"pt{pctr % 3}"; pctr += 1
                    if n == 512:
                        ACT(p_[:, 0:1024], ps[:, sb * 512:sb * 512 + 1024], AF.Exp, [f"B{sb}", f"B{sb + 1}"], [pn], scale=scale)
                    else:
                        ACT(p_[:, :].rearrange("p (a b) -> p a b", a=2)[:, :, 0:n],
                            ps[:, sb * 512:sb * 512 + 1024].rearrange("p (a b) -> p a b", a=2)[:, :, 0:n],
                            AF.Exp, [f"B{sb}", f"B{sb + 1}"], [pn], scale=scale)
                    first = ki == 0; last = ki == nk - 1
                    for m in range(2):
                        MM(bk(4 + m, n), vl, p_[:, m * 512:m * 512 + n], first, last, vr + [pn], [f"B{4 + m}"], sig=False)
                    for m in range(2):
                        MM(bk(6 + m, n), onesb, p_[:, m * 512:m * 512 + n], first, last, [pn, "cb"], [f"B{6 + m}"], sig=(m == 1))
                RECIP(r0_[:, 0:n], bk(6, n), ["B6"], ["r0"])
                RECIP(r1_[:, 0:n], bk(7, n), ["B7"], ["r1"])
                TT("dve", u0_[:, 0:n], bk(4, n), r0_[:, 0:n], ALU.mult, ["B4", "r0"], ["u0"])
                TT("dve", u1_[:, 0:n], bk(5, n), r1_[:, 0:n], ALU.mult, ["B5", "r1"], ["u1"])
                STT("dve", o_[:, 0:n], u1_[:, 0:n], neglam, u0_[:, 0:n], ALU.mult, ALU.add, ["u0", "u1", "neglam"], ["o_"])
                ACT(osq[:, 0:n], o_[:, 0:n], AF.Square, ["o_"], ["osq"])
                sb = 0
                MM(bk(sb, n), onesb, osq[:, 0:n], True, True, ["osq", "cb"], [f"B{sb}"])
                ACT(lr_[:, 0:n], bk(sb, n), AF.Ln, [f"B{sb}"], ["lr"], bias=EPS, scale=1.0 / 128)
                ACT(lr_[:, 0:n], lr_[:, 0:n], AF.Exp, ["lr"], ["lr"], scale=-0.5)
                STT("dve", QO[:, h, t0:t0 + n], o_[:, 0:n], sgain, lr_[:, 0:n], ALU.mult, ALU.mult, ["o_", "lr", "sgain"], [qname])
        P.barrier()
        HA.reset()
        TA.reset(tsave)
        load_xT(TA)
        P.barrier()
        TA.reset(tsave)
        wo = TA.alloc((8, 1024), BF16)
        load_w(wo, "l0_wo", "wo")
        for h in range(8):
            pass
        for (t0, n) in chunks:
            s = 1 if t0 >= NW else 0
            cid = chunk_id(t0)
            for dc in range(8):
                b = bank()
                for kc in range(8):
                    MM(bk(b, n), wo[:, kc, dc * 128:(dc + 1) * 128], QO[:, kc, t0:t0 + n], kc == 0, kc == 7,
                       ["wo", f"QO{kc}_{cid}"], [f"B{b}"], sig=(kc == 7))
                STT("dve", hT[:, dc, t0:t0 + n], bk(b, n), modap(l, 16 + dc, s), hT[:, dc, t0:t0 + n], ALU.mult, ALU.add,
                    [f"B{b}", "modsb", f"h{cid}"], [f"h{cid}"])
        P.barrier()
        ffn(l, chunks)


    def layer1():
        l = 1
        chunks = CH_LAT + [CH_CTX]
        TA.reset()
        xn = TA.alloc((8, NT), BF16)
        z = TA.alloc((8, NT), BF16)
        tsave = TA.off
        make_xn(xn, l, 0, chunks, TA)
        P.barrier()
        TA.reset(tsave)
        cu = TA.alloc((NT,), F32)
        bS = TA.alloc((NT,), BF16)
        wst = [TA.alloc((8, 3, 128), BF16) for _ in range(2)]
        cS = TA.alloc((512,), F32)
        yt = TA.alloc((512,), F32)
        segs = [(0, NW), (NW, NT)]
        for fc in range(8):
            w_ = wst[fc % 2]; wn = f"wst{fc % 2}"
            for kc in range(8):
                src, g = wsrc("l1_win", kc)
                for j in range(3):
                    DMA(w_[:, kc, j, :], src[:, j * 1024 + fc * 128:j * 1024 + (fc + 1) * 128], [g], [wn], q="pool")
            for (t0, n) in chunks:
                cid = chunk_id(t0)
                bb = [bank(), bank(), bank()]
                for j in range(3):
                    for kc in range(8):
                        MM(bk(bb[j], n), w_[:, kc, j, :], xn[:, kc, t0:t0 + n], kc == 0, kc == 7, [wn, f"xn{cid}"], [f"B{bb[j]}"], sig=(kc == 7))
                CP("act", cS[:, 0:n], bk(bb[1], n), [f"B{bb[1]}"], ["cS"])
                TT("dve", cu[:, t0:t0 + n], cS[:, 0:n], bk(bb[2], n), ALU.mult, ["cS", f"B{bb[2]}"], ["cu"])
                CP("act", bS[:, t0:t0 + n], bk(bb[0], n), [f"B{bb[0]}"], ["bS"])
            TT("dve", cu[:, 0:NW], cu[:, 0:NW], validb[:, :], ALU.mult, ["cu", "validb"], ["cu"])
            for (t0, n) in chunks:
                cid = chunk_id(t0)
                s0, s1 = segs[1] if t0 >= NW else segs[0]
                ACT(yt[:, 0:n], cu[:, t0:t0 + n], AF.Identity, ["cu"], ["yt"], scale=l1cv[:, fc, 1:2])
                a = max(t0, s0 + 1)
                STT("dve", yt[:, a - t0:n], cu[:, a - 1:t0 + n - 1], l1cv[:, fc, 0:1], yt[:, a - t0:n], ALU.mult, ALU.add, ["cu", "yt", "l1cv"], ["yt"])
                e_ = min(t0 + n, s1 - 1)
                STT("dve", yt[:, 0:e_ - t0], cu[:, t0 + 1:e_ + 1], l1cv[:, fc, 2:3], yt[:, 0:e_ - t0], ALU.mult, ALU.add, ["cu", "yt", "l1cv"], ["yt"])
                TT("dve", z[:, fc, t0:t0 + n], yt[:, 0:n], bS[:, t0:t0 + n], ALU.mult, ["yt", "bS"], [f"z{cid}"])
        P.barrier()
        TA.reset(tsave)
        wo = TA.alloc((8, 1024), BF16)
        load_w(wo, "l1_wout", "wo1")
        proj_residual(wo, "wo1", z, "z", l, 16, chunks)
        P.barrier()
        ffn(l, chunks)


    def layer2():
        l = 2
        chunks_kv = CH_LAT + [CH_CTX]
        TA.reset()
        xn = TA.alloc((8, NT), BF16)
        QO = TA.alloc((8, NW), BF16)
        K2 = TA.alloc((4, NT), BF16)
        tsave = TA.off
        make_xn(xn, l, 0, chunks_kv, TA)
        for fc in range(8):
            DMA(hspill[:, fc * NT:(fc + 1) * NT], hreg[:, fc * NT:(fc + 1) * NT], [f"h{c}" for c in range(6)], ["hspill"])
        P.barrier()
        HA.reset()
        VP = HA.alloc((21, 576), BF16)
        vsave = HA.off
        wq = HA.alloc((8, 1792), BF16)
        cos_ = HA.alloc((NW,), F32); sin_ = HA.alloc((NW,), F32)
        load_w(wq, "l2_qkv", "wq2")
        DMA(cos_, cosd[:, :], [], ["cos"]); DMA(sin_, sind[:, :], [], ["sin"])
        MEMSET("pool", VP[:, :, :], 0.0, ["VP"])
        TA.reset(tsave)
        sqb = [TA.alloc((512,), BF16) for _ in range(2)]
        rr = [TA.alloc((512,), F32) for _ in range(2)]
        qn = [TA.alloc((512,), BF16) for _ in range(2)]
        t1 = [TA.alloc((512,), F32) for _ in range(2)]
        t2 = [TA.alloc((512,), F32) for _ in range(2)]
        it = 0
        for (t0, n) in chunks_kv:
            isctx = t0 >= NW
            cid = chunk_id(t0)
            for oc in range(12):
                isk = oc >= 8
                if isctx and not isk:
                    continue
                i2 = it % 2; it += 1
                b = bank()
                for kc in range(8):
                    MM(bk(b, n), wq[:, kc, oc * 128:(oc + 1) * 128], xn[:, kc, t0:t0 + n], kc == 0, kc == 7,
                       ["wq2", f"xn{cid}"], [f"B{b}"], sig=(kc == 7))
                ACT(sqb[i2][:, 0:n], bk(b, n), AF.Square, [f"B{b}"], [f"sqb{i2}"])
                b2 = bank()
                MM(bk(b2, n), blk, sqb[i2][:, 0:n], True, True, [f"sqb{i2}", "cb"], [f"B{b2}"])
                ACT(rr[i2][:, 0:n], bk(b2, n), AF.Ln, [f"B{b2}"], [f"rr{i2}"], bias=EPS, scale=1.0 / 64)
                ACT(rr[i2][:, 0:n], rr[i2][:, 0:n], AF.Exp, [f"rr{i2}"], [f"rr{i2}"], scale=-0.5)
                gcol = l2v[:, 1:2] if isk else l2v[:, 0:1]
                dst = K2[:, oc - 8, t0:t0 + n] if isk else QO[:, oc, t0:t0 + n]
                dname = f"K2_{cid}" if isk else f"QO{oc}_{cid}"
                if isctx:
                    STT("dve", dst, bk(b, n), gcol, rr[i2][:, 0:n], ALU.mult, ALU.mult, [f"B{b}", f"rr{i2}", "l2v"], [dname])
                    continue
                STT("dve", qn[i2][:, 0:n], bk(b, n), gcol, rr[i2][:, 0:n], ALU.mult, ALU.mult, [f"B{b}", f"rr{i2}", "l2v"], [f"qn{i2}"])
                b3 = bank()
                MM(bk(b3, n), rotm, qn[i2][:, 0:n], True, True, [f"qn{i2}", "cb"], [f"B{b3}"])
                TT("dve", t1[i2][:, 0:n], qn[i2][:, 0:n], cos_[:, t0:t0 + n], ALU.mult, [f"qn{i2}", "cos"], [f"t1{i2}"])
                TT("dve", t2[i2][:, 0:n], bk(b3, n), sin_[:, t0:t0 + n], ALU.mult, [f"B{b3}", "sin"], [f"t2{i2}"])
                TT("pool", dst, t1[i2][:, 0:n], t2[i2][:, 0:n], ALU.add, [f"t1{i2}", f"t2{i2}"], [dname])
        for i in range(21):
            isctx = i >= 19
            tok0 = 128 * i if not isctx else NW + 128 * (i - 19)
            rows = min(128, NW - tok0) if not isctx else 128
            xr = [f"xn{c}" for c in sorted({chunk_id(tok0), chunk_id(tok0 + rows - 1)})]
            b = bank()
            for kc in range(8):
                MM(ps[0:rows, b * 512:b * 512 + 256], xn[:, kc, tok0:tok0 + rows], wq[:, kc, 1536:1792], kc == 0, kc == 7,
                   ["wq2"] + xr, [f"B{b}"], sig=(kc == 7))
            CP(evac_eng(), VP[0:rows, i, 64:576].rearrange("p (k c) -> p k c", c=128)[:, :, 0:64],
               ps[0:rows, b * 512:b * 512 + 256].rearrange("p (k c) -> p k c", c=64), [f"B{b}"], ["VP"])
        P.barrier()
        HA.reset(vsave)
        mk = HA.alloc((6, 512), BF16)
        esk = HA.alloc((8,), F32)
        vcol = HA.alloc((19,), F32)
        den = HA.alloc((512,), F32)
        pt = [HA.alloc((1024,), BF16) for _ in range(3)]
        DMA(mk[:, 0:3, :].rearrange("p a b -> p (a b)"), maskda[:, :], [], ["mk"], q="pool")
        DMA(mk[:, 3:6, :].rearrange("p a b -> p (a b)"), maskdb[:, :], [], ["mk"], q="pool")
        ACT(esk[:, :], l2v[:, 2:10], AF.Exp, ["l2v"], ["esk"])
        for i in range(19):
            rows = min(128, NW - 128 * i)
            DMA(vcol[0:rows, i:i + 1], validd[0:1, 128 * i:128 * i + rows].rearrange("o (p x) -> (o p) x", x=1), [], ["vcol"])
        pctr = 0; sctr = 0
        scale = 1.0 / 8.0
        for c in range(8):
            kvh = c // 2
            for (t0, n) in CH_B:
                cid = chunk_id(t0)
                qname = f"QO{c}_{cid}"
                k0 = t0 - 128
                keys = []
                for j in range(6):
                    ks = k0 + 128 * j
                    if ks >= NW or 128 * (j - 1) - (n - 1) > 128:
                        continue
                    keys.append(("w", ks // 128, j))
                keys += [("c", 19, None), ("c", 20, None)]
                nk = len(keys)
                for ki, (kind, ti, j) in enumerate(keys):
                    sb = (sctr % 2) * 2; sctr += 1
                    if kind == "w":
                        rows = min(128, NW - 128 * ti)
                        kcol = 128 * ti
                    else:
                        rows = 128
                        kcol = NW + 128 * (ti - 19)
                    kn = f"K2_{chunk_id(kcol)}"
                    MM(ps[0:rows, sb * 512:sb * 512 + n], K2[0:64, kvh, kcol:kcol + rows], QO[0:64, c, t0:t0 + n], True, True,
                       [kn, qname], [f"B{sb}"], sig=False, tp=(0, 0))
                    MM(ps[0:rows, (sb + 1) * 512:(sb + 1) * 512 + n], K2[64:128, kvh, kcol:kcol + rows], QO[64:128, c, t0:t0 + n], True, True,
                       [kn, qname], [f"B{sb + 1}"], tp=(64, 0))
                    p_ = pt[pctr % 3]; pn = f"pt{pctr % 3}"; pctr += 1
                    pv = p_[0:rows, :].rearrange("p (a b) -> p a b", a=2)[:, :, 0:n]
                    ACT(pv, ps[0:rows, sb * 512:sb * 512 + 1024].rearrange("p (a b) -> p a b", a=2)[:, :, 0:n],
                        AF.Exp, [f"B{sb}", f"B{sb + 1}"], [pn], scale=scale)
                    if kind == "w":
                        for m in range(2):
                            STT("dve", p_[0:rows, m * 512:m * 512 + n], p_[0:rows, m * 512:m * 512 + n], vcol[0:rows, ti:ti + 1],
                                mk[0:rows, j, 0:n], ALU.mult, ALU.mult, [pn, "vcol", "mk"], [pn])
                    first = ki == 0; last = ki == nk - 1
                    va = VP[0:rows, ti, 64 + 128 * kvh:64 + 128 * kvh + 128]
                    vb = VP[0:rows, ti, 128 * kvh:128 * kvh + 128]
                    MM(bk(4, n), va, p_[0:rows, 0:n], first, False, ["VP", pn], ["B4"], sig=False)
                    MM(bk(4, n), vb, p_[0:rows, 512:512 + n], False, last, ["VP", pn], ["B4"], sig=False)
                    MM(bk(6, n), E0[0:rows, :], p_[0:rows, 0:n], first, False, [pn, "cb"], ["B6"], sig=False)
                    MM(bk(6, n), E1[0:rows, :], p_[0:rows, 512:512 + n], False, last, [pn, "cb"], ["B6"])
                TS("dve", den[:, 0:n], bk(6, n), esk[:, c:c + 1], None, ALU.add, None, ["B6", "esk"], ["den"])
                RECIP(den[:, 0:n], den[:, 0:n], ["den"], ["den"])
                TT("dve", QO[:, c, t0:t0 + n], bk(4, n), den[:, 0:n], ALU.mult, ["B4", "den"], [qname])
        P.barrier()
        for fc in range(8):
            DMA(hreg[:, fc * NT:(fc + 1) * NT], hspill[:, fc * NT:(fc + 1) * NT], ["hspill"], [f"h{c}" for c in range(6)])
        TA.reset(tsave)
        wo = TA.alloc((8, 1024), BF16)
        load_w(wo, "l2_wo", "wo2")
        P.barrier()
        for (t0, n) in CH_B:
            cid = chunk_id(t0)
            for dc in range(8):
                b = bank()
                for kc in range(8):
                    MM(bk(b, n), wo[:, kc, dc * 128:(dc + 1) * 128], QO[:, kc, t0:t0 + n], kc == 0, kc == 7,
                       ["wo2", f"QO{kc}_{cid}"], [f"B{b}"], sig=(kc == 7))
                STT("dve", hT[:, dc, t0:t0 + n], bk(b, n), modap(l, 16 + dc, 0), hT[:, dc, t0:t0 + n], ALU.mult, ALU.add,
                    [f"B{b}", "modsb", f"h{cid}"], [f"h{cid}"])
        P.barrier()
        ffn(l, CH_B)


    CH_OWN = [(HALO + 512 * i, 512) for i in range(4)]

    def layer3():
        l = 3
        TA.reset()
        xn = TA.alloc((8, NW), BF16)
        U = TA.alloc((8, NW), BF16)
        tsave = TA.off
        make_xn(xn, l, 0, CH_B, TA)
        P.barrier()
        TA.reset(tsave)
        glu = TA.alloc((NW,), F32)
        accA = TA.alloc((NW,), F32)
        accB = TA.alloc((NW,), F32)
        wst = [TA.alloc((8, 2, 128), BF16) for _ in range(2)]
        sg = TA.alloc((512,), F32)
        o0, o1 = HALO, HALO + OWN
        for fc in range(8):
            w_ = wst[fc % 2]; wn = f"wst{fc % 2}"
            for kc in range(8):
                src, g = wsrc("l3_pw1", kc)
                for j in range(2):
                    DMA(w_[:, kc, j, :], src[:, j * 1024 + fc * 128:j * 1024 + (fc + 1) * 128], [g], [wn], q="pool")
            for (t0, n) in CH_B:
                cid = chunk_id(t0)
                ba = bank(); bg = bank()
                for j, b in ((0, ba), (1, bg)):
                    for kc in range(8):
                        MM(bk(b, n), w_[:, kc, j, :], xn[:, kc, t0:t0 + n], kc == 0, kc == 7, [wn, f"xn{cid}"], [f"B{b}"], sig=(kc == 7))
                ACT(sg[:, 0:n], bk(bg, n), AF.Sigmoid, [f"B{bg}", "l3v"], ["sg"], bias=l3v[:, fc, 1:2])
                STT("dve", glu[:, t0:t0 + n], bk(ba, n), l3v[:, fc, 0:1], sg[:, 0:n], ALU.add, ALU.mult, [f"B{ba}", "sg", "l3v"], ["glu"])
            TT("dve", glu[:, 128:2208], glu[:, 128:2208], validb[:, 128:2208], ALU.mult, ["glu", "validb"], ["glu"])
            for k in range(31):
                src_ = glu[:, o0 + k - 15:o1 + k - 15]
                wk = l3v[:, fc, 8 + k:9 + k]
                if k == 0:
                    TS("dve", accA[:, o0:o1], src_, wk, None, ALU.mult, None, ["glu", "l3v"], ["accA"])
                else:
                    STT("dve", accA[:, o0:o1], src_, wk, accA[:, o0:o1], ALU.mult, ALU.add, ["glu", "l3v", "accA"], ["accA"])
            TS("dve", U[:, fc, o0:o1], accA[:, o0:o1], l3v[:, fc, 2:3], None, ALU.add, None, ["accA", "l3v"], ["U"])
        P.barrier()
        TA.reset(tsave)
        usq = TA.alloc((8, 512), BF16)
        mean = TA.alloc((512,), F32); msq = TA.alloc((512,), F32); rstd = TA.alloc((512,), F32)
        tmp = [TA.alloc((512,), F32) for _ in range(2)]
        tmpb = [TA.alloc((512,), F32) for _ in range(2)]
        wo = TA.alloc((8, 1024), BF16)
        load_w(wo, "l3_pw2", "wo3")
        for (t0, n) in CH_OWN:
            cid = chunk_id(t0)
            ACT(usq[:, :, 0:n], U[:, :, t0:t0 + n], AF.Square, ["U"], ["usq"])
            b1 = bank()
            for fc in range(8):
                MM(bk(b1, n), onesb, U[:, fc, t0:t0 + n], fc == 0, fc == 7, ["U", "cb"], [f"B{b1}"], sig=(fc == 7))
            b2 = bank()
            for fc in range(8):
                MM(bk(b2, n), onesb, usq[:, fc, 0:n], fc == 0, fc == 7, ["usq", "cb"], [f"B{b2}"], sig=(fc == 7))
            ACT(mean[:, 0:n], bk(b1, n), AF.Identity, [f"B{b1}"], ["mean"], scale=1.0 / D)
            TT("dve", msq[:, 0:n], mean[:, 0:n], mean[:, 0:n], ALU.mult, ["mean"], ["msq"])
            STT("dve", rstd[:, 0:n], bk(b2, n), 1.0 / D, msq[:, 0:n], ALU.mult, ALU.subtract, [f"B{b2}", "msq"], ["rstd"])
            ACT(rstd[:, 0:n], rstd[:, 0:n], AF.Ln, ["rstd"], ["rstd"], bias=EPS)
            ACT(rstd[:, 0:n], rstd[:, 0:n], AF.Exp, ["rstd"], ["rstd"], scale=-0.5)
            for fc in range(8):
                t_ = tmp[fc % 2]; tn = f"tmp{fc % 2}"
                TT("dve", t_[:, 0:n], U[:, fc, t0:t0 + n], mean[:, 0:n], ALU.subtract, ["U", "mean"], [tn])
                TT("dve", t_[:, 0:n], t_[:, 0:n], rstd[:, 0:n], ALU.mult, [tn, "rstd"], [tn])
                ACT(U[:, fc, t0:t0 + n], t_[:, 0:n], AF.Silu, [tn, "l3v"], [f"S{cid}"], bias=l3v[:, fc, 4:5], scale=l3v[:, fc, 3:4])
        for (t0, n) in CH_OWN:
            cid = chunk_id(t0)
            for dc in range(8):
                b = bank()
                for kc in range(8):
                    MM(bk(b, n), wo[:, kc, dc * 128:(dc + 1) * 128], U[:, kc, t0:t0 + n], kc == 0, kc == 7,
                       ["wo3", f"S{cid}", "U"], [f"B{b}"], sig=(kc == 7))
                tb = tmpb[dc % 2]; tbn = f"tmpb{dc % 2}"
                ACT(tb[:, 0:n], bk(b, n), AF.Identity, [f"B{b}", "l3v"], [tbn], bias=l3v[:, dc, 5:6])
                STT("dve", hT[:, dc, t0:t0 + n], tb[:, 0:n], modap(l, 16 + dc, 0), hT[:, dc, t0:t0 + n], ALU.mult, ALU.add,
                    [tbn, "modsb", f"h{cid}"], [f"h{cid}"])
        P.barrier()
        ffn(l, CH_OWN)

    TA.reset()
    load_xT(TA)
    P.barrier()
    if nlayers >= 1:
        layer0()
    if nlayers >= 2:
        layer1()
    if nlayers >= 3:
        layer2()
    if nlayers >= 4:
        layer3()

    P.barrier()
    TA.reset()
    ot = [TA.alloc((1024,), F32) for _ in range(2)]
    outs = []
    for i in range(16):
        tok0 = HALO + 128 * i
        o2 = ot[i % 2]
        hr = [f"h{c}" for c in sorted({chunk_id(tok0), chunk_id(tok0 + 127)})]
        for half in range(2):
            b = bank()
            for f4 in range(4):
                fc = half * 4 + f4
                MM(ps[:, b * 512 + f4 * 128:b * 512 + (f4 + 1) * 128], hT[:, fc, tok0:tok0 + 128], ident, True, True,
                   hr + ["constf"], [f"B{b}"], sig=(f4 == 3))
            CP(evac_eng(), o2[:, half * 512:(half + 1) * 512], bk(b), [f"B{b}"], [f"ot{i % 2}"])
        outs.append(DMA(outd[i][:, :], o2[:, :], [f"ot{i % 2}"], ["out"]))
    nsem = P.finalize(final_wait_ops=outs)
    return nc, P, nsem


def _rope_tables():
    n_freq = 16
    inv = (10000.0 ** (-np.arange(n_freq, dtype=np.float32) / np.float32(n_freq))).astype(np.float32)
    t = np.arange(SEQ)
    rows = (t // 64).astype(np.float32); cols = (t % 64).astype(np.float32)
    ang_r = rows[:, None] * inv[None, :]; ang_c = cols[:, None] * inv[None, :]
    ang = np.concatenate([ang_r, ang_r, ang_c, ang_c], axis=-1).astype(np.float32)
    return np.cos(ang).astype(np.float32), np.sin(ang).astype(np.float32)


def _consts():
    ident = np.eye(128, dtype=np.float32)
    ones = np.ones((128, 128), np.float32)
    blk = np.zeros((128, 128), np.float32); blk[:64, :64] = 1; blk[64:, 64:] = 1
    R = np.zeros((128, 128), np.float32)
    for base in (0, 64):
        for j in range(16):
            R[base + 16 + j, base + j] = -1.0
            R[base + j, base + 16 + j] = 1.0
            R[base + 48 + j, base + 32 + j] = -1.0
            R[base + 32 + j, base + 48 + j] = 1.0
    E0 = np.zeros((128, 128), np.float32); E0[:, :64] = 1
    E1 = np.zeros((128, 128), np.float32); E1[:, 64:] = 1
    constf = np.concatenate([ident, ones], 1)
    constb = np.concatenate([ones, blk, R, E0, E1], 1)
    p = np.arange(128)[:, None]; f = np.arange(512)[None, :]
    masks = [(np.abs(128 * (j - 1) + p - f) <= 128).astype(np.float32) for j in range(6)]
    return constf, constb, np.concatenate(masks, 1)


_CACHE = {}


def kernel(**inp):
    nlayers = int(inp.pop("_nlayers", NLAYERS_DEFAULT))
    f32 = lambda a: np.ascontiguousarray(np.asarray(a, dtype=np.float32))
    x = f32(inp["x"])[0]; ctx = f32(inp["ctx"])[0]
    xpad = np.zeros((SEQ + 2 * HALO, D), np.float32); xpad[HALO:HALO + SEQ] = x
    cosf, sinf = _rope_tables()
    cpad = np.ones((SEQ + 2 * HALO, 64), np.float32); cpad[HALO:HALO + SEQ] = cosf
    spad = np.zeros((SEQ + 2 * HALO, 64), np.float32); spad[HALO:HALO + SEQ] = sinf
    vpad = np.zeros((SEQ + 2 * HALO,), np.float32); vpad[HALO:HALO + SEQ] = 1.0
    constf, constb, maskd = _consts()
    cvec = np.stack([f32(inp["c"])[0].reshape(8, 128).T, f32(inp["c_ctx"]).reshape(8, 128).T], -1)
    ada_w = f32(inp["ada_w"]); ada_b = f32(inp["ada_b"])
    normd = np.stack([f32(inp["norm1"]).reshape(4, 8, 128), f32(inp["norm2"]).reshape(4, 8, 128)], 1)
    normd = np.ascontiguousarray(normd.transpose(3, 0, 1, 2))
    w = f32(inp["diff_w_qkv"])[0]
    wq = w[:, :1024].reshape(D, 2, 8, 64).transpose(0, 2, 1, 3).reshape(D, 1024)
    wk = w[:, 1024:2048].reshape(D, 2, 8, 64).transpose(0, 2, 1, 3).reshape(D, 1024)
    l0_qkv = np.ascontiguousarray(np.concatenate([wq, wk, w[:, 2048:]], 1))
    l0_vec = np.zeros((128, 8), np.float32)
    l0_vec[:, 0] = np.tile(f32(inp["diff_q_norm"])[0], 2); l0_vec[:, 1] = np.tile(f32(inp["diff_k_norm"])[0], 2)
    l0_vec[:, 2] = f32(inp["diff_subln"])[0]
    for j, nm in enumerate(["diff_lam_q1", "diff_lam_k1", "diff_lam_q2", "diff_lam_k2"]):
        l0_vec[:64, 3 + j] = f32(inp[nm])[0]
    w2 = f32(inp["win_w_qkv"])[0]
    k2 = w2[:, 1024:1280].reshape(D, 4, 1, 64).repeat(2, axis=2).reshape(D, 512)
    l2_qkv = np.ascontiguousarray(np.concatenate([w2[:, :1024], k2, w2[:, 1280:]], 1))
    l2_vec = np.zeros((128, 16), np.float32)
    l2_vec[:, 0] = np.tile(f32(inp["win_q_norm"])[0], 2); l2_vec[:, 1] = np.tile(f32(inp["win_k_norm"])[0], 2)
    sink = f32(inp["win_sink"])[0]
    for c in range(8):
        l2_vec[:64, 2 + c] = sink[2 * c]; l2_vec[64:, 2 + c] = sink[2 * c + 1]
    l1_conv = np.ascontiguousarray(f32(inp["sc_conv_w"])[0].reshape(3, 8, 128).transpose(2, 1, 0))
    l3_vec = np.zeros((128, 8, 40), np.float32)
    pb = f32(inp["cf_b_pw1"])[0]
    l3_vec[:, :, 0] = pb[:1024].reshape(8, 128).T; l3_vec[:, :, 1] = pb[1024:].reshape(8, 128).T
    l3_vec[:, :, 2] = f32(inp["cf_dw_b"])[0].reshape(8, 128).T
    l3_vec[:, :, 3] = f32(inp["cf_ln_g"])[0].reshape(8, 128).T
    l3_vec[:, :, 4] = f32(inp["cf_ln_b"])[0].reshape(8, 128).T
    l3_vec[:, :, 5] = f32(inp["cf_b_pw2"])[0].reshape(8, 128).T
    l3_vec[:, :, 8:39] = f32(inp["cf_dw_w"])[0].reshape(31, 8, 128).transpose(2, 1, 0)
    wts = {"l0_qkv": l0_qkv, "l0_wo": f32(inp["diff_w_o"])[0],
           "l1_win": f32(inp["sc_w_in"])[0], "l1_wout": f32(inp["sc_w_out"])[0],
           "l2_qkv": l2_qkv, "l2_wo": f32(inp["win_w_o"])[0],
           "l3_pw1": f32(inp["cf_w_pw1"])[0], "l3_pw2": f32(inp["cf_w_pw2"])[0]}
    gu = f32(inp["ffn_w_gate_up"]); dn = f32(inp["ffn_w_down"])
    for l in range(4):
        wts[f"ffn_gu{l}"] = gu[l]
        dpad = np.zeros((24 * 128, D), np.float32); dpad[:DFF] = dn[l]
        wts[f"ffn_d{l}"] = dpad

    def pack(spec, r):
        parts = []
        for name, n, cpr in spec:
            parts.append(wts[name][r * cpr * 128:(r + 1) * cpr * 128, :].reshape(-1))
        return np.ascontiguousarray(np.concatenate(parts).reshape(-1, 512))

    shared = dict(ctxa=np.ascontiguousarray(ctx[:128]), ctxb=np.ascontiguousarray(ctx[128:]), cvec=f32(cvec), normd=normd,
                  constf=constf, constb=constb, maskda=np.ascontiguousarray(maskd[:, :1536]), maskdb=np.ascontiguousarray(maskd[:, 1536:]),
                  l0_vec=l0_vec, l1_conv=l1_conv, l2_vec=l2_vec, l3_vec=l3_vec)
    in_maps = []
    for i in range(NCORES):
        s0 = OWN * i
        m = dict(shared)
        m["xw"] = np.ascontiguousarray(xpad[s0:s0 + NW])
        m["cosd"] = np.ascontiguousarray(np.tile(cpad[s0:s0 + NW].T, (2, 1)))
        m["sind"] = np.ascontiguousarray(np.tile(spad[s0:s0 + NW].T, (2, 1)))
        m["validd"] = np.ascontiguousarray(np.broadcast_to(vpad[s0:s0 + NW][None, :], (128, NW)))
        m["wA"] = pack(WSPEC_A, i); m["wB"] = pack(WSPEC_B, i)
        m["adaw"] = np.ascontiguousarray(ada_w[:, :, i * 768:(i + 1) * 768])
        m["adab"] = np.ascontiguousarray(ada_b[:, i * 768:(i + 1) * 768].reshape(4, 6, 128).transpose(2, 0, 1).reshape(128, 24))
        in_maps.append(m)
    key = nlayers
    if key not in _CACHE:
        _CACHE[key] = build_program(nlayers)[0]
    nc = _CACHE[key]
    res = run_bass_kernel_spmd(nc, in_maps, core_ids=list(range(NCORES)))
    out = np.concatenate([np.asarray(r[f"out{i}"], dtype=np.float32) for r in res.results for i in range(16)], 0)
    return out.reshape(1, SEQ, D)
```

```python
import math
import numpy as np
import concourse.bass as bass
import concourse.mybir as mybir
from concourse.bass_utils import run_bass_kernel_spmd

F32 = mybir.dt.float32
BF16 = mybir.dt.bfloat16
AF = mybir.ActivationFunctionType
ALU = mybir.AluOpType

NCORES = 8
D = 1024
SEQ = 16384
OWN = SEQ // NCORES
HALO = 144
NW = OWN + 2 * HALO
NCTX = 256
NT = NW + NCTX
DFF = 2816
EPS = 1e-6
NLAYERS_DEFAULT = 4

ENG_NAMES = ("pe", "act", "dve", "pool", "sp")
EPOCH = 12000
NDSEM = 6


class Op:
    __slots__ = ("eng", "fn", "reads", "writes", "dma", "sig", "idx", "waits", "cnt", "dslot", "cc", "noself")

    def __init__(self, eng, fn, reads, writes, dma=False, sig=True, cc=False):
        self.eng = eng; self.fn = fn; self.reads = tuple(reads); self.writes = tuple(writes)
        self.dma = dma; self.sig = sig; self.waits = []; self.cnt = None; self.dslot = None; self.cc = cc
        self.noself = False


class Prog:
    def __init__(self, nc, selfsync=True):
        self.nc = nc
        self.ops = []
        self.selfsync = selfsync
        self.ncc = 0

    def add(self, eng, fn, reads=(), writes=(), dma=False, sig=True, cc=False):
        op = Op(eng, fn, list(reads) + ["__phase__"], writes, dma, sig, cc)
        op.idx = len(self.ops)
        self.ops.append(op)
        return op

    def barrier(self):
        nc = self.nc
        op = Op("dve", lambda e: e.memset(self._bar[:, :], 0.0), [], ["__phase__"])
        op.idx = len(self.ops)
        self.ops.append(op)

    def finalize(self, final_wait_ops=()):
        nc = self.nc
        ops = self.ops
        cnt = {e: 0 for e in ENG_NAMES}
        dcount = {}
        dn = {e: 0 for e in ENG_NAMES}
        for op in ops:
            if op.cc:
                self.ncc += 1
                op.dslot = (("cc", self.ncc), 1)
            elif op.dma:
                slot = dn[op.eng] % NDSEM
                dn[op.eng] += 1
                k = (op.eng, slot)
                dcount[k] = dcount.get(k, 0) + 1
                op.dslot = (k, dcount[k])
            elif op.sig:
                cnt[op.eng] += 1
                op.cnt = cnt[op.eng]
        nxt = {e: None for e in ENG_NAMES}
        for op in reversed(ops):
            if op.dma or op.cc:
                continue
            if op.sig:
                nxt[op.eng] = op
            else:
                op.cnt = ("fwd", nxt[op.eng])
        last_writer = {}
        readers = {}
        seen = {e: {} for e in ENG_NAMES}
        n_waits = 0
        for op in ops:
            deps = set()
            for r in op.reads:
                w = last_writer.get(r)
                if w is not None: deps.add(w)
            for wr in op.writes:
                w = last_writer.get(wr)
                if w is not None: deps.add(w)
                rl = readers.get(wr)
                if rl:
                    deps.update(rl.values() if isinstance(rl, dict) else rl)
            waits = {}
            if op.dma and not op.cc:
                k, c = op.dslot
                if c > 1:
                    waits[("d",) + k] = 16 * (c - 1)
            for d in deps:
                if d is op: continue
                if d.cc:
                    key = ("d",) + d.dslot[0]; val = 1
                elif d.dma:
                    k, c = d.dslot
                    key = ("d",) + k; val = 16 * c
                else:
                    tgt = d
                    if not d.sig:
                        tgt = d.cnt[1]
                        assert tgt is not None, f"no signalling op after {d.idx}"
                    if tgt.eng == op.eng and not op.dma and not op.cc:
                        if op.eng == "pe" or not self.selfsync or op.noself:
                            continue
                        if tgt.idx >= op.idx:
                            continue
                    assert tgt.idx < op.idx, f"signal op {tgt.idx} after waiter {op.idx} (dep {d.idx})"
                    ep, v = divmod(tgt.cnt - 1, EPOCH)
                    key = ("c", tgt.eng, ep); val = v + 1
                if waits.get(key, 0) < val: waits[key] = val
            for key, val in waits.items():
                if seen[op.eng].get(key, 0) >= val: continue
                seen[op.eng][key] = val
                op.waits.append((key, val))
                n_waits += 1
            for r in op.reads:
                if r == "__phase__":
                    readers.setdefault(r, {})
                    rk = op.dslot[0] if (op.dma or op.cc) else op.eng
                    if op.sig or op.dma or op.cc:
                        readers[r][rk] = op
                else:
                    readers.setdefault(r, []).append(op)
            for wr in op.writes:
                last_writer[wr] = op
                readers[wr] = {} if wr == "__phase__" else []
        self.n_waits = n_waits
        sems = {}

        def sem(key):
            if key not in sems:
                sems[key] = nc.alloc_semaphore("s_" + "_".join(str(k) for k in key))
            return sems[key]

        fin = []
        for op in final_wait_ops:
            k, c = op.dslot
            fin.append((("d",) + k, 16 * c))
        per_eng = {e: [op for op in ops if op.eng == e] for e in ENG_NAMES}
        with nc.cleanup_on_exit():
            for op in ops:
                for key, val in op.waits:
                    sem(key)
                if op.cc or op.dma:
                    sem(("d",) + op.dslot[0])
                elif op.sig:
                    sem(("c", op.eng, (op.cnt - 1) // EPOCH))
            for key, val in fin:
                sem(key)
            for h_ in sems.values():
                nc.gpsimd.sem_clear(h_)
            nc.all_engine_barrier()
            with nc.Block() as block:
                def emit(ename):
                    def body(eng):
                        for op in per_eng[ename]:
                            for key, val in op.waits:
                                eng.wait_ge(sem(key), val)
                            ins = op.fn(eng)
                            if op.cc:
                                ins.then_inc(sem(("d",) + op.dslot[0]))
                            elif op.dma:
                                k, c = op.dslot
                                ins.then_inc(sem(("d",) + k), 16)
                            elif op.sig:
                                ep = (op.cnt - 1) // EPOCH
                                ins.then_inc(sem(("c", ename, ep)), 1)
                        if ename == "sp":
                            for key, val in fin:
                                eng.wait_ge(sem(key), val)
                    return body
                block.tensor(emit("pe"))
                block.scalar(emit("act"))
                block.vector(emit("dve"))
                block.gpsimd(emit("pool"))
                block.sync(emit("sp"))
            nc.all_engine_barrier()
        return len(sems)


def _fix_phase_readers(readers_val):
    return readers_val.values() if isinstance(readers_val, dict) else readers_val


class Arena:
    def __init__(self, ap32, nbytes):
        self.ap32 = ap32
        self.nbytes = nbytes
        self.off = 0

    def reset(self, off=0):
        self.off = off

    def alloc(self, free_shape, dtype):
        n = int(np.prod(free_shape))
        esz = 4 if dtype == F32 else 2
        nb = (n * esz + 31) // 32 * 32
        assert self.off + nb <= self.nbytes, f"arena overflow {self.off}+{nb}>{self.nbytes}"
        a = self.ap32[:, self.off // 4:(self.off + nb) // 4]
        self.off += nb
        if dtype != F32:
            a = a.bitcast(dtype)
        a = a[:, 0:n]
        if len(free_shape) == 2:
            a = a.rearrange("p (a b) -> p a b", a=free_shape[0])
        elif len(free_shape) == 3:
            a = a.rearrange("p (a b c) -> p a b c", a=free_shape[0], b=free_shape[1])
        return a


WSPEC_A = [("l0_qkv", 3072, 1), ("l0_wo", 1024, 1), ("ffn_gu0", 5632, 1), ("ffn_d0", 1024, 3)]
WSPEC_B = [("l1_win", 3072, 1), ("l1_wout", 1024, 1), ("ffn_gu1", 5632, 1), ("ffn_d1", 1024, 3),
           ("l2_qkv", 1792, 1), ("l2_wo", 1024, 1), ("ffn_gu2", 5632, 1), ("ffn_d2", 1024, 3),
           ("l3_pw1", 2048, 1), ("l3_pw2", 1024, 1), ("ffn_gu3", 5632, 1), ("ffn_d3", 1024, 3)]


def _wlayout(spec):
    offs = {}
    off = 0
    for name, n, cpr in spec:
        offs[name] = (off, n, cpr)
        off += cpr * 128 * n
    assert off % 512 == 0
    return offs, off


WOFF_A, WSIZE_A = _wlayout(WSPEC_A)
WOFF_B, WSIZE_B = _wlayout(WSPEC_B)

CH_LAT = [(0, 512), (512, 512), (1024, 512), (1536, 512), (2048, 288)]
CH_CTX = (NW, NCTX)
CH_B = [(128, 512), (640, 512), (1152, 512), (1664, 512), (2176, 32)]


def build_program(nlayers=NLAYERS_DEFAULT, debug_h=False):
    nc = bass.Bass("TRN2", target_bir_lowering=False)
    P = Prog(nc)

    def din(name, shape, dt=F32):
        return nc.dram_tensor(name, list(shape), dt, kind="ExternalInput").ap()

    xw = din("xw", [NW, D]); ctxa = din("ctxa", [128, D]); ctxb = din("ctxb", [128, D])
    cvec = din("cvec", [128, 8, 2])
    adaw = din("adaw", [4, D, 768]); adab = din("adab", [128, 24])
    normd = din("normd", [128, 4, 2, 8])
    cosd = din("cosd", [128, NW]); sind = din("sind", [128, NW])
    validd = din("validd", [128, NW])
    constf = din("constf", [128, 256])
    constb = din("constb", [128, 5 * 128])
    maskda = din("maskda", [128, 3 * 512]); maskdb = din("maskdb", [128, 3 * 512])
    wA = din("wA", [WSIZE_A // 512, 512]); wB = din("wB", [WSIZE_B // 512, 512])
    l0_vec = din("l0_vec", [128, 8]); l1_conv = din("l1_conv", [128, 8, 3])
    l2_vec = din("l2_vec", [128, 16])
    hspill = nc.dram_tensor("hspill", [128, 8 * NT], F32).ap(); l3_vec = din("l3_vec", [128, 8, 40])
    wAb = nc.dram_tensor("wAb", [WSIZE_A // 512, 512], F32).ap()
    wBb = nc.dram_tensor("wBb", [WSIZE_B // 512, 512], F32).ap()
    gA = nc.dram_tensor("gA", [NCORES * WSIZE_A // 512, 512], F32).ap()
    gB = nc.dram_tensor("gB", [NCORES * WSIZE_B // 512, 512], F32).ap()
    gAf = gA.rearrange("a b -> (a b)"); gBf = gB.rearrange("a b -> (a b)")

    def wsrc(name, i):
        if name in WOFF_A:
            (off, n, cpr), flat, rs = WOFF_A[name], gAf, WSIZE_A
        else:
            (off, n, cpr), flat, rs = WOFF_B[name], gBf, WSIZE_B
        base = (i // cpr) * rs + off + (i % cpr) * 128 * n
        return flat[base:base + 128 * n].rearrange("(p n) -> p n", n=n), ("gA" if name in WOFF_A else "gB")
    outd = [nc.dram_tensor(f"out{i}", [128, D], F32, kind="ExternalOutput").ap() for i in range(16)]
    kvloc = nc.dram_tensor("kvloc", [2048, 2048], BF16).ap()
    kvall = nc.dram_tensor("kvall", [NCORES * 2048, 2048], BF16).ap()
    modloc = nc.dram_tensor("modloc", [128, 48], F32).ap()
    modall = nc.dram_tensor("modall", [NCORES * 128, 48], F32).ap()

    ident_ones = nc.alloc_sbuf_tensor("ident_ones", [128, 256], F32)
    ident = ident_ones[:, 0:128]; onesf = ident_ones[:, 128:256]
    cb = nc.alloc_sbuf_tensor("cb", [128, 5 * 128], BF16)
    onesb = cb[:, 0:128]; blk = cb[:, 128:256]; rotm = cb[:, 256:384]; E0 = cb[:, 384:512]; E1 = cb[:, 512:640]
    validb = nc.alloc_sbuf_tensor("validb", [128, NW], BF16)
    modsb = nc.alloc_sbuf_tensor("modsb", [128, 4, 48, 2], F32)
    normsb = nc.alloc_sbuf_tensor("normsb", [128, 4, 2, 8], F32)
    Gt = nc.alloc_sbuf_tensor("Gt", [128, 4, 2, 2, 8], F32)
    bar = nc.alloc_sbuf_tensor("bar", [128, 8], F32)
    P._bar = bar
    small = nc.alloc_sbuf_tensor("small", [128, 64], F32)
    l0v = nc.alloc_sbuf_tensor("l0v", [128, 8], F32)
    l1cv = nc.alloc_sbuf_tensor("l1cv", [128, 8, 3], F32)
    l2v = nc.alloc_sbuf_tensor("l2v", [128, 16], F32)
    l3v = nc.alloc_sbuf_tensor("l3v", [128, 8, 40], F32)
    H_BYTES = NT * 8 * 4
    hreg = nc.alloc_sbuf_tensor("hreg", [128, H_BYTES // 4], F32)
    T_BYTES = 115 * 1024
    treg = nc.alloc_sbuf_tensor("treg", [128, T_BYTES // 4], F32)
    hT = hreg[:, :].rearrange("p (c t) -> p c t", c=8)
    HA = Arena(hreg[:, :], H_BYTES)
    TA = Arena(treg[:, :], T_BYTES)
    ps = nc.alloc_psum_tensor("ps", [128, 8 * 512], F32)
    bankctr = [0]

    def bank():
        i = bankctr[0] % 8
        bankctr[0] += 1
        return i

    def bk(i, n=512):
        return ps[:, i * 512:i * 512 + n]

    def MM(out, lhsT, rhs, st, sp, R, W, sig=True, tp=None):
        if tp is None:
            P.add("pe", lambda e: e.matmul(out, lhsT=lhsT, rhs=rhs, start=st, stop=sp), R, W, sig=sig)
        else:
            P.add("pe", lambda e: e.matmul(out, lhsT=lhsT, rhs=rhs, start=st, stop=sp, tile_position=tp), R, W, sig=sig)

    def ACT(out, in_, func, R, W, bias=None, scale=None):
        kw = {}
        if bias is not None: kw["bias"] = bias
        if scale is not None: kw["scale"] = scale
        P.add("act", lambda e: e.activation(out=out, in_=in_, func=func, **kw), R, W)

    def TT(eng, out, in0, in1, op, R, W, noself=False):
        o_ = P.add(eng, lambda e: e.tensor_tensor(out=out, in0=in0, in1=in1, op=op), R, W)
        o_.noself = noself

    def STT(eng, out, in0, scalar, in1, op0, op1, R, W):
        P.add(eng, lambda e: e.scalar_tensor_tensor(out=out, in0=in0, scalar=scalar, in1=in1, op0=op0, op1=op1), R, W)

    def TS(eng, out, in0, s1, s2, op0, op1, R, W):
        if s2 is None:
            P.add(eng, lambda e: e.tensor_scalar(out=out, in0=in0, scalar1=s1, scalar2=None, op0=op0), R, W)
        else:
            P.add(eng, lambda e: e.tensor_scalar(out=out, in0=in0, scalar1=s1, scalar2=s2, op0=op0, op1=op1), R, W)

    def CP(eng, out, in_, R, W):
        if eng == "act":
            P.add("act", lambda e: e.copy(out=out, in_=in_), R, W)
        else:
            P.add(eng, lambda e: e.tensor_copy(out=out, in_=in_), R, W)

    def RECIP(out, in_, R, W):
        P.add("dve", lambda e: e.reciprocal(out=out, in_=in_), R, W)

    def DMA(out, in_, R, W, q="sp"):
        return P.add(q, lambda e: e.dma_start(out=out, in_=in_), R, W, dma=True)

    def MEMSET(eng, ap, val, W):
        P.add(eng, lambda e: e.memset(ap, val), [], W)

    def modap(l, q, s):
        return modsb[:, l, q, s:s + 1]

    def load_w(dst, wname, name, kc_n=8, c0=0, ncol=None):
        for kc in range(kc_n):
            src, g = wsrc(wname, kc)
            if ncol is not None:
                src = src[:, c0:c0 + ncol]
            DMA(dst[:, kc, :], src, [g], [f"{name}"], q="pool")

    evac_ctr = [0]

    def evac_eng():
        evac_ctr[0] += 1
        return "act" if evac_ctr[0] % 2 else "dve"

    DMA(ident_ones[:, :], constf[:, :], [], ["constf"])
    DMA(cb[:, :], constb[:, :], [], ["cb"], q="pool")
    DMA(validb[:, :], validd[:, :], [], ["validb"], q="pool")
    DMA(normsb[:, :, :, :], normd[:, :, :, :], [], ["normsb"])
    DMA(l0v[:, :], l0_vec[:, :], [], ["l0v"])
    DMA(l1cv[:, :, :], l1_conv[:, :, :], [], ["l1cv"])
    DMA(l2v[:, :], l2_vec[:, :], [], ["l2v"])
    DMA(l3v[:, :, :], l3_vec[:, :, :], [], ["l3v"])
    DMA(wAb[:, :], wA[:, :], [], ["wAb"])
    P.add("pool", lambda e: e.collective_compute("AllGather", ALU.bypass, replica_groups=[list(range(NCORES))],
                                                 ins=[wAb[:, :]], outs=[gA[:, :]]), ["wAb"], ["gA", "ccchain"], cc=True)
    DMA(wBb[:, :], wB[:, :], [], ["wBb"])

    TA.reset()
    cact = TA.alloc((8, 2), F32)
    csig = TA.alloc((8, 2), F32)
    adabs = TA.alloc((24,), F32)
    modl = TA.alloc((24, 2), F32)
    DMA(cact, cvec[:, :, :], [], ["cact"])
    DMA(adabs, adab[:, :], [], ["adabs"])
    ACT(csig, cact, AF.Silu, ["cact"], ["csig"])
    awb = [TA.alloc((8, 768), F32) for _ in range(2)]
    for l in range(4):
        wb_ = awb[l % 2]
        for kc in range(8):
            DMA(wb_[:, kc, :], adaw[l, kc * 128:(kc + 1) * 128, :], [], [f"awb{l % 2}"])
        for cc in range(6):
            b = bank()
            for kc in range(8):
                MM(ps[:, b * 512:b * 512 + 2], wb_[:, kc, cc * 128:(cc + 1) * 128], csig[:, kc, :], kc == 0, kc == 7,
                   [f"awb{l % 2}", "csig"], [f"B{b}"], sig=(kc == 7))
            TS("dve", modl[:, l * 6 + cc, :], ps[:, b * 512:b * 512 + 2], adabs[:, l * 6 + cc:l * 6 + cc + 1], None, ALU.add, None,
               [f"B{b}", "adabs"], ["modl"])
    DMA(modloc[:, :], modl.rearrange("p a b -> p (a b)"), ["modl"], ["modloc"])
    P.add("pool", lambda e: e.collective_compute("AllGather", ALU.bypass, replica_groups=[list(range(NCORES))],
                                                 ins=[modloc[:, :]], outs=[modall[:, :]]),
          ["modloc"], ["modall", "ccchain"], cc=True)
    for l in range(4):
        P.add("sp", lambda e, l=l: e.dma_start(
            out=modsb[:, l, :, :].rearrange("p (r c) s -> p r (c s)", r=8),
            in_=modall[:, l * 12:(l + 1) * 12].rearrange("(r p) cs -> p r cs", p=128)),
            ["modall"], ["modsb"], dma=True)
    for l in range(4):
        for sub in range(2):
            for s in range(2):
                q0 = 8 if sub == 0 else 32
                STT("dve", Gt[:, l, sub, s, :], modsb[:, l, q0:q0 + 8, s], 1.0, normsb[:, l, sub, :], ALU.add, ALU.mult,
                    ["modsb", "normsb"], ["Gt"])
    P.barrier()

    def load_xT(arena):
        stg = arena.alloc((4, 1024), F32)
        groups = [(xw, g * 512, min(512, NW - g * 512), g * 512) for g in range(5)] + [(None, 0, 256, NW)]
        for gi, (src, r0, n, t0) in enumerate(groups):
            ntile = (n + 127) // 128
            for i in range(ntile):
                rows = min(128, n - i * 128)
                if src is None:
                    sap = (ctxa, ctxb)[i][:, :]
                else:
                    sap = src[r0 + i * 128:r0 + i * 128 + rows, :]
                DMA(stg[0:rows, i, :], sap, [], ["xstg"])
            for fc in range(8):
                b = bank()
                for i in range(ntile):
                    rows = min(128, n - i * 128)
                    MM(ps[:, b * 512 + i * 128:b * 512 + i * 128 + rows], stg[0:rows, i, fc * 128:(fc + 1) * 128],
                       ident[0:rows, 0:rows], True, True, ["xstg", "constf"], [f"B{b}"], sig=(i == ntile - 1))
                CP(evac_eng(), hT[:, fc, t0:t0 + n], bk(b, n), [f"B{b}"], [f"h{gi}"])

    def chunk_id(t0):
        if t0 >= NW: return 5
        return t0 // 512

    def make_xn(xn, l, sub, chunks, tmpA):
        sq = tmpA.alloc((8, 512), BF16)
        rs = tmpA.alloc((512,), F32)
        tm = [tmpA.alloc((512,), F32) for _ in range(2)]
        shq = 0 if sub == 0 else 24
        for (t0, n) in chunks:
            s = 1 if t0 >= NW else 0
            hid = f"h{chunk_id(t0)}"
            ACT(sq[:, :, 0:n], hT[:, :, t0:t0 + n], AF.Square, [hid], ["sq"])
            b = bank()
            for fc in range(8):
                MM(bk(b, n), onesb, sq[:, fc, 0:n], fc == 0, fc == 7, ["sq", "cb"], [f"B{b}"], sig=(fc == 7))
            ACT(rs[:, 0:n], bk(b, n), AF.Ln, [f"B{b}"], ["rs"], bias=EPS, scale=1.0 / D)
            ACT(rs[:, 0:n], rs[:, 0:n], AF.Exp, ["rs"], ["rs"], scale=-0.5)
            for fc in range(8):
                t_ = tm[fc % 2]
                STT("dve", t_[:, 0:n], hT[:, fc, t0:t0 + n], Gt[:, l, sub, s, fc:fc + 1], rs[:, 0:n], ALU.mult, ALU.mult,
                    [hid, "Gt", "rs"], [f"tm{fc % 2}"])
                ACT(xn[:, fc, t0:t0 + n], t_[:, 0:n], AF.Identity, [f"tm{fc % 2}", "modsb"], [f"xn{chunk_id(t0)}"],
                    bias=modap(l, shq + fc, s))

    def proj_residual(w_sb, wname, inT, in_name, l, gq, chunks, kc_n=8, bias=None):
        for (t0, n) in chunks:
            s = 1 if t0 >= NW else 0
            cid = chunk_id(t0)
            for dc in range(8):
                b = bank()
                for kc in range(kc_n):
                    MM(bk(b, n), w_sb[:, kc, dc * 128:(dc + 1) * 128], inT[:, kc, t0:t0 + n], kc == 0, kc == kc_n - 1,
                       [wname, f"{in_name}{cid}"], [f"B{b}"], sig=(kc == kc_n - 1))
                if bias is None:
                    STT("dve", hT[:, dc, t0:t0 + n], bk(b, n), modap(l, gq + dc, s), hT[:, dc, t0:t0 + n], ALU.mult, ALU.add,
                        [f"B{b}", "modsb", f"h{cid}"], [f"h{cid}"])
                else:
                    raise NotImplementedError

    def ffn(l, chunks):
        TA.reset()
        xn2 = TA.alloc((8, NT), BF16)
        actb = TA.alloc((2, NT), BF16)
        gub = [TA.alloc((8, 2, 256), BF16) for _ in range(3)]
        wdb = [TA.alloc((2, 1024), BF16) for _ in range(3)]
        sgt = [TA.alloc((512,), BF16) for _ in range(2)]
        make_xn(xn2, l, 1, chunks, TA)
        for u in range(11):
            g_ = gub[u % 3]; wd_ = wdb[u % 3]
            c0 = u * 256
            for kc in range(8):
                src, g = wsrc(f"ffn_gu{l}", kc)
                DMA(g_[:, kc, 0, :], src[:, c0:c0 + 256], [g], [f"gub{u % 3}"], q="pool")
                DMA(g_[:, kc, 1, :], src[:, DFF + c0:DFF + c0 + 256], [g], [f"gub{u % 3}"], q="pool")
            for jj in range(2):
                src, g = wsrc(f"ffn_d{l}", 2 * u + jj)
                DMA(wd_[:, jj, :], src, [g], [f"wdb{u % 3}"], q="pool")
            for jj in range(2):
                for ci, (t0, n) in enumerate(chunks):
                    cid = chunk_id(t0)
                    bg = bank()
                    for kc in range(8):
                        MM(bk(bg, n), g_[:, kc, 0, jj * 128:(jj + 1) * 128], xn2[:, kc, t0:t0 + n], kc == 0, kc == 7,
                           [f"gub{u % 3}", f"xn{cid}"], [f"B{bg}"], sig=(kc == 7))
                    bu = bank()
                    for kc in range(8):
                        MM(bk(bu, n), g_[:, kc, 1, jj * 128:(jj + 1) * 128], xn2[:, kc, t0:t0 + n], kc == 0, kc == 7,
                           [f"gub{u % 3}", f"xn{cid}"], [f"B{bu}"], sig=(kc == 7))
                    st_ = sgt[ci % 2]
                    ACT(st_[:, 0:n], bk(bg, n), AF.Silu, [f"B{bg}"], [f"sgt{ci % 2}"])
                    TT("dve", actb[:, jj, t0:t0 + n], st_[:, 0:n], bk(bu, n), ALU.mult, [f"sgt{ci % 2}", f"B{bu}"], [f"act{jj}_{cid}"])
            for (t0, n) in chunks:
                s = 1 if t0 >= NW else 0
                cid = chunk_id(t0)
                for dc in range(8):
                    b = bank()
                    for jj in range(2):
                        MM(bk(b, n), wd_[:, jj, dc * 128:(dc + 1) * 128], actb[:, jj, t0:t0 + n], jj == 0, jj == 1,
                           [f"wdb{u % 3}", f"act{jj}_{cid}"], [f"B{b}"], sig=(jj == 1))
                    STT("dve", hT[:, dc, t0:t0 + n], bk(b, n), modap(l, 40 + dc, s), hT[:, dc, t0:t0 + n], ALU.mult, ALU.add,
                        [f"B{b}", "modsb", f"h{cid}"], [f"h{cid}"])
        P.barrier()

    def layer0():
        l = 0
        chunks = CH_LAT + [CH_CTX]
        TA.reset()
        xn = TA.alloc((8, NT), BF16)
        QO = TA.alloc((8, NT), BF16)
        ctxK = TA.alloc((8, NCTX), BF16)
        ctxV = TA.alloc((2, 1024), BF16)
        tsave = TA.off
        make_xn(xn, l, 0, chunks, TA)
        P.barrier()
        HA.reset()
        wq = HA.alloc((8, 3072), BF16)
        cos_ = HA.alloc((NW,), F32); sin_ = HA.alloc((NW,), F32)
        load_w(wq, "l0_qkv", "wq")
        DMA(cos_, cosd[:, :], [], ["cos"]); DMA(sin_, sind[:, :], [], ["sin"])
        TA.reset(tsave)
        sqb = [TA.alloc((512,), BF16) for _ in range(2)]
        rr = [TA.alloc((512,), F32) for _ in range(2)]
        qn = [TA.alloc((512,), BF16) for _ in range(2)]
        t1 = [TA.alloc((512,), F32) for _ in range(2)]
        t2 = [TA.alloc((512,), F32) for _ in range(2)]
        kst = [TA.alloc((512,), BF16) for _ in range(2)]
        vst = [TA.alloc((512,), BF16) for _ in range(2)]
        it = 0
        for (t0, n) in chunks:
            isctx = t0 >= NW
            cid = chunk_id(t0)
            for oc in range(16):
                isk = oc >= 8
                hh = oc % 8
                i2 = it % 2; it += 1
                b = bank()
                for kc in range(8):
                    MM(bk(b, n), wq[:, kc, oc * 128:(oc + 1) * 128], xn[:, kc, t0:t0 + n], kc == 0, kc == 7,
                       ["wq", f"xn{cid}"], [f"B{b}"], sig=(kc == 7))
                ACT(sqb[i2][:, 0:n], bk(b, n), AF.Square, [f"B{b}"], [f"sqb{i2}"])
                b2 = bank()
                MM(bk(b2, n), blk, sqb[i2][:, 0:n], True, True, [f"sqb{i2}", "cb"], [f"B{b2}"])
                ACT(rr[i2][:, 0:n], bk(b2, n), AF.Ln, [f"B{b2}"], [f"rr{i2}"], bias=EPS, scale=1.0 / 64)
                ACT(rr[i2][:, 0:n], rr[i2][:, 0:n], AF.Exp, [f"rr{i2}"], [f"rr{i2}"], scale=-0.5)
                gcol = l0v[:, 1:2] if isk else l0v[:, 0:1]
                if isctx:
                    dst = ctxK[:, hh, :] if isk else QO[:, hh, t0:t0 + n]
                    dname = "ctxK" if isk else f"QO{hh}_{cid}"
                    STT("dve", dst, bk(b, n), gcol, rr[i2][:, 0:n], ALU.mult, ALU.mult, [f"B{b}", f"rr{i2}", "l0v"], [dname])
                    continue
                STT("dve", qn[i2][:, 0:n], bk(b, n), gcol, rr[i2][:, 0:n], ALU.mult, ALU.mult, [f"B{b}", f"rr{i2}", "l0v"], [f"qn{i2}"])
                b3 = bank()
                MM(bk(b3, n), rotm, qn[i2][:, 0:n], True, True, [f"qn{i2}", "cb"], [f"B{b3}"])
                TT("dve", t1[i2][:, 0:n], qn[i2][:, 0:n], cos_[:, t0:t0 + n], ALU.mult, [f"qn{i2}", "cos"], [f"t1{i2}"])
                TT("dve", t2[i2][:, 0:n], bk(b3, n), sin_[:, t0:t0 + n], ALU.mult, [f"B{b3}", "sin"], [f"t2{i2}"])
                if not isk:
                    TT("pool", QO[:, hh, t0:t0 + n], t1[i2][:, 0:n], t2[i2][:, 0:n], ALU.add, [f"t1{i2}", f"t2{i2}"], [f"QO{hh}_{cid}"])
                else:
                    TT("pool", kst[i2][:, 0:n], t1[i2][:, 0:n], t2[i2][:, 0:n], ALU.add, [f"t1{i2}", f"t2{i2}"], [f"kst{i2}"])
                    a = max(t0, HALO); e_ = min(t0 + n, HALO + OWN)
                    DMA(kvloc[hh * 256:hh * 256 + 128, a - HALO:e_ - HALO], kst[i2][:, a - t0:e_ - t0], [f"kst{i2}"], ["kvloc"])
        vt = 0
        for i in range(16 + 2):
            isctx = i >= 16
            tok0 = (HALO + 128 * i) if not isctx else (NW + 128 * (i - 16))
            xr = [f"xn{c}" for c in sorted({chunk_id(tok0), chunk_id(tok0 + 127)})]
            for hb in range(2):
                b = bank()
                for kc in range(8):
                    MM(bk(b), xn[:, kc, tok0:tok0 + 128], wq[:, kc, 2048 + hb * 512:2048 + (hb + 1) * 512], kc == 0, kc == 7,
                       ["wq"] + xr, [f"B{b}"], sig=(kc == 7))
                if isctx:
                    CP(evac_eng(), ctxV[:, i - 16, hb * 512:(hb + 1) * 512], bk(b), [f"B{b}"], ["ctxV"])
                else:
                    i2 = vt % 2; vt += 1
                    CP(evac_eng(), vst[i2][:, :], bk(b), [f"B{b}"], [f"vst{i2}"])
                    r0 = hb * 4 * 256
                    DMA(kvloc[r0:r0 + 1024, i * 128:(i + 1) * 128].rearrange("(h k p) d -> p h k d", k=2, p=128)[:, :, 1, :],
                        vst[i2][:, :].rearrange("p (h d) -> p h d", h=4), [f"vst{i2}"], ["kvloc"])
        P.add("pool", lambda e: e.collective_compute("AllGather", ALU.bypass, replica_groups=[list(range(NCORES))],
                                                     ins=[kvloc[:, :]], outs=[kvall[:, :]]),
              ["kvloc"], ["kvall", "ccchain"], cc=True)
        P.add("pool", lambda e: e.collective_compute("AllGather", ALU.bypass, replica_groups=[list(range(NCORES))],
                                                     ins=[wBb[:, :]], outs=[gB[:, :]]), ["wBb"], ["gB", "ccchain"], cc=True)
        lamv = small[:, 0:4]
        TT("dve", small[0:64, 8:9], l0v[0:64, 3:4], l0v[0:64, 4:5], ALU.mult, ["l0v"], ["lamp"])
        TT("dve", small[0:64, 9:10], l0v[0:64, 5:6], l0v[0:64, 6:7], ALU.mult, ["l0v", "lamp"], ["lamp"])
        b = bank()
        MM(ps[:, b * 512:b * 512 + 2], onesf[0:64, :], small[0:64, 8:10], True, True, ["lamp", "constf"], [f"B{b}"])
        ACT(small[:, 10:12], ps[:, b * 512:b * 512 + 2], AF.Exp, [f"B{b}"], ["lame"])
        TT("dve", small[:, 12:13], small[:, 10:11], small[:, 11:12], ALU.subtract, ["lame"], ["lam"])
        lam_init = 0.8 - 0.6 * math.exp(-0.3 * 0)
        TS("dve", small[:, 13:14], small[:, 12:13], lam_init, -1.0, ALU.add, ALU.mult, ["lam"], ["neglam"])
        TS("dve", small[:, 14:15], l0v[:, 2:3], 1.0 - lam_init, None, ALU.mult, None, ["l0v"], ["sgain"])
        neglam = small[:, 13:14]; sgain = small[:, 14:15]
        P.barrier()
        HA.reset()
        Kh = HA.alloc((SEQ,), BF16)
        Vh = HA.alloc((128, 128), BF16)
        r0_ = HA.alloc((512,), F32); r1_ = HA.alloc((512,), F32)
        u0_ = HA.alloc((512,), F32); u1_ = HA.alloc((512,), F32)
        o_ = HA.alloc((512,), F32); lr_ = HA.alloc((512,), F32)
        osq = HA.alloc((512,), BF16)
        accS = HA.alloc((2, 512), F32)
        TA.reset(tsave)
        pt = [TA.alloc((1024,), BF16) for _ in range(3)]
        pctr = 0
        sctr = 0
        scale = 1.0 / 8.0
        for h in range(8):
            for r in range(NCORES):
                DMA(Kh[:, r * 2048:(r + 1) * 2048], kvall[r * 2048 + h * 256:r * 2048 + h * 256 + 128, :], ["kvall"], [f"Kh{r}"])
                DMA(Vh[:, r * 16:(r + 1) * 16, :].rearrange("p a b -> p (a b)"),
                    kvall[r * 2048 + h * 256 + 128:r * 2048 + h * 256 + 256, :], ["kvall"], [f"Vh{r}"])
            for (t0, n) in chunks:
                isctx = t0 >= NW
                cid = chunk_id(t0)
                qname = f"QO{h}_{cid}"
                keys = ([] if isctx else [("g", kt) for kt in range(128)]) + [("c", 0), ("c", 1)]
                nk = len(keys)

                def emit_scores(ki):
                    kind, kt = keys[ki]
                    sb = (ki % 2) * 2
                    if kind == "g":
                        kl0 = Kh[0:64, kt * 128:(kt + 1) * 128]; kl1 = Kh[64:128, kt * 128:(kt + 1) * 128]
                        kr = [f"Kh{kt // 16}"]
                    else:
                        kl0 = ctxK[0:64, h, kt * 128:(kt + 1) * 128]; kl1 = ctxK[64:128, h, kt * 128:(kt + 1) * 128]
                        kr = ["ctxK"]
                    MM(bk(sb, n), kl0, QO[0:64, h, t0:t0 + n], True, True, kr + [qname], [f"B{sb}"], sig=False, tp=(0, 0))
                    MM(bk(sb + 1, n), kl1, QO[64:128, h, t0:t0 + n], True, True, kr + [qname], [f"B{sb + 1}"], tp=(64, 0))

                emit_scores(0)
                for ki, (kind, kt) in enumerate(keys):
                    sb = (ki % 2) * 2
                    if ki + 1 < nk:
                        emit_scores(ki + 1)
                    if kind == "g":
                        vl = Vh[:, kt, :]; vr = [f"Vh{kt // 16}"]
                    else:
                        vl = ctxV[:, kt, h * 128:(h + 1) * 128]; vr = ["ctxV"]
                    p_ = pt[pctr % 3]; pn = f"pt{pctr % 3}"; pctr += 1
                    if n == 512:
                        ACT(p_[:, 0:1024], ps[:, sb * 512:sb * 512 + 1024], AF.Exp, [f"B{sb}", f"B{sb + 1}"], [pn], scale=scale)
                    else:
                        ACT(p_[:, :].rearrange("p (a b) -> p a b", a=2)[:, :, 0:n],
                            ps[:, sb * 512:sb * 512 + 1024].rearrange("p (a b) -> p a b", a=2)[:, :, 0:n],
                            AF.Exp, [f"B{sb}", f"B{sb + 1}"], [pn], scale=scale)
                    first = ki == 0; last = ki == nk - 1
                    for m in range(2):
                        MM(bk(4 + m, n), vl, p_[:, m * 512:m * 512 + n], first, last, vr + [pn], [f"B{4 + m}"], sig=(m == 1))
                    av = accS[:, :, :].rearrange("p a b -> p (a b)") if n == 512 else accS[:, :, 0:n]
                    pv_ = p_[:, 0:1024] if n == 512 else p_[:, :].rearrange("p (a b) -> p a b", a=2)[:, :, 0:n]
                    if first:
                        CP("dve", av, pv_, [pn], ["accS"])
                    else:
                        TT("dve", av, av, pv_, ALU.add, [pn, "accS"], ["accS"], noself=True)
                for m in range(2):
                    MM(bk(6 + m, n), onesf, accS[:, m, 0:n], True, True, ["accS", "constf"], [f"B{6 + m}"])
                RECIP(r0_[:, 0:n], bk(6, n), ["B6"], ["r0"])
                RECIP(r1_[:, 0:n], bk(7, n), ["B7"], ["r1"])
                TT("dve", u0_[:, 0:n], bk(4, n), r0_[:, 0:n], ALU.mult, ["B4", "r0"], ["u0"])
                TT("dve", u1_[:, 0:n], bk(5, n), r1_[:, 0:n], ALU.mult, ["B5", "r1"], ["u1"])
                STT("dve", o_[:, 0:n], u1_[:, 0:n], neglam, u0_[:, 0:n], ALU.mult, ALU.add, ["u0", "u1", "neglam"], ["o_"])
                ACT(osq[:, 0:n], o_[:, 0:n], AF.Square, ["o_"], ["osq"])
                sb = 0
                MM(bk(sb, n), onesb, osq[:, 0:n], True, True, ["osq", "cb"], [f"B{sb}"])
                ACT(lr_[:, 0:n], bk(sb, n), AF.Ln, [f"B{sb}"], ["lr"], bias=EPS, scale=1.0 / 128)
                ACT(lr_[:, 0:n], lr_[:, 0:n], AF.Exp, ["lr"], ["lr"], scale=-0.5)
                STT("dve", QO[:, h, t0:t0 + n], o_[:, 0:n], sgain, lr_[:, 0:n], ALU.mult, ALU.mult, ["o_", "lr", "sgain"], [qname])
        P.barrier()
        HA.reset()
        TA.reset(tsave)
        load_xT(TA)
        P.barrier()
        TA.reset(tsave)
        wo = TA.alloc((8, 1024), BF16)
        load_w(wo, "l0_wo", "wo")
        for h in range(8):
            pass
        for (t0, n) in chunks:
            s = 1 if t0 >= NW else 0
            cid = chunk_id(t0)
            for dc in range(8):
                b = bank()
                for kc in range(8):
                    MM(bk(b, n), wo[:, kc, dc * 128:(dc + 1) * 128], QO[:, kc, t0:t0 + n], kc == 0, kc == 7,
                       ["wo", f"QO{kc}_{cid}"], [f"B{b}"], sig=(kc == 7))
                STT("dve", hT[:, dc, t0:t0 + n], bk(b, n), modap(l, 16 + dc, s), hT[:, dc, t0:t0 + n], ALU.mult, ALU.add,
                    [f"B{b}", "modsb", f"h{cid}"], [f"h{cid}"])
        P.barrier()
        ffn(l, chunks)


    def layer1():
        l = 1
        chunks = CH_LAT + [CH_CTX]
        TA.reset()
        xn = TA.alloc((8, NT), BF16)
        z = TA.alloc((8, NT), BF16)
        tsave = TA.off
        make_xn(xn, l, 0, chunks, TA)
        P.barrier()
        TA.reset(tsave)
        cu = TA.alloc((NT,), F32)
        bS = TA.alloc((NT,), BF16)
        wst = [TA.alloc((8, 3, 128), BF16) for _ in range(2)]
        cS = TA.alloc((512,), F32)
        yt = TA.alloc((512,), F32)
        segs = [(0, NW), (NW, NT)]
        for fc in range(8):
            w_ = wst[fc % 2]; wn = f"wst{fc % 2}"
            for kc in range(8):
                src, g = wsrc("l1_win", kc)
                for j in range(3):
                    DMA(w_[:, kc, j, :], src[:, j * 1024 + fc * 128:j * 1024 + (fc + 1) * 128], [g], [wn], q="pool")
            for (t0, n) in chunks:
                cid = chunk_id(t0)
                bb = [bank(), bank(), bank()]
                for j in range(3):
                    for kc in range(8):
                        MM(bk(bb[j], n), w_[:, kc, j, :], xn[:, kc, t0:t0 + n], kc == 0, kc == 7, [wn, f"xn{cid}"], [f"B{bb[j]}"], sig=(kc == 7))
                CP("act", cS[:, 0:n], bk(bb[1], n), [f"B{bb[1]}"], ["cS"])
                TT("dve", cu[:, t0:t0 + n], cS[:, 0:n], bk(bb[2], n), ALU.mult, ["cS", f"B{bb[2]}"], ["cu"])
                CP("act", bS[:, t0:t0 + n], bk(bb[0], n), [f"B{bb[0]}"], ["bS"])
            TT("dve", cu[:, 0:NW], cu[:, 0:NW], validb[:, :], ALU.mult, ["cu", "validb"], ["cu"])
            for (t0, n) in chunks:
                cid = chunk_id(t0)
                s0, s1 = segs[1] if t0 >= NW else segs[0]
                ACT(yt[:, 0:n], cu[:, t0:t0 + n], AF.Identity, ["cu"], ["yt"], scale=l1cv[:, fc, 1:2])
                a = max(t0, s0 + 1)
                STT("dve", yt[:, a - t0:n], cu[:, a - 1:t0 + n - 1], l1cv[:, fc, 0:1], yt[:, a - t0:n], ALU.mult, ALU.add, ["cu", "yt", "l1cv"], ["yt"])
                e_ = min(t0 + n, s1 - 1)
                STT("dve", yt[:, 0:e_ - t0], cu[:, t0 + 1:e_ + 1], l1cv[:, fc, 2:3], yt[:, 0:e_ - t0], ALU.mult, ALU.add, ["cu", "yt", "l1cv"], ["yt"])
                TT("dve", z[:, fc, t0:t0 + n], yt[:, 0:n], bS[:, t0:t0 + n], ALU.mult, ["yt", "bS"], [f"z{cid}"])
        P.barrier()
        TA.reset(tsave)
        wo = TA.alloc((8, 1024), BF16)
        load_w(wo, "l1_wout", "wo1")
        proj_residual(wo, "wo1", z, "z", l, 16, chunks)
        P.barrier()
        ffn(l, chunks)


    def layer2():
        l = 2
        chunks_kv = CH_LAT + [CH_CTX]
        TA.reset()
        xn = TA.alloc((8, NT), BF16)
        QO = TA.alloc((8, NW), BF16)
        K2 = TA.alloc((4, NT), BF16)
        tsave = TA.off
        make_xn(xn, l, 0, chunks_kv, TA)
        for fc in range(8):
            DMA(hspill[:, fc * NT:(fc + 1) * NT], hreg[:, fc * NT:(fc + 1) * NT], [f"h{c}" for c in range(6)], ["hspill"])
        P.barrier()
        HA.reset()
        VP = HA.alloc((21, 576), BF16)
        vsave = HA.off
        wq = HA.alloc((8, 1792), BF16)
        cos_ = HA.alloc((NW,), F32); sin_ = HA.alloc((NW,), F32)
        load_w(wq, "l2_qkv", "wq2")
        DMA(cos_, cosd[:, :], [], ["cos"]); DMA(sin_, sind[:, :], [], ["sin"])
        MEMSET("pool", VP[:, :, :], 0.0, ["VP"])
        TA.reset(tsave)
        sqb = [TA.alloc((512,), BF16) for _ in range(2)]
        rr = [TA.alloc((512,), F32) for _ in range(2)]
        qn = [TA.alloc((512,), BF16) for _ in range(2)]
        t1 = [TA.alloc((512,), F32) for _ in range(2)]
        t2 = [TA.alloc((512,), F32) for _ in range(2)]
        it = 0
        for (t0, n) in chunks_kv:
            isctx = t0 >= NW
            cid = chunk_id(t0)
            for oc in range(12):
                isk = oc >= 8
                if isctx and not isk:
                    continue
                i2 = it % 2; it += 1
                b = bank()
                for kc in range(8):
                    MM(bk(b, n), wq[:, kc, oc * 128:(oc + 1) * 128], xn[:, kc, t0:t0 + n], kc == 0, kc == 7,
                       ["wq2", f"xn{cid}"], [f"B{b}"], sig=(kc == 7))
                ACT(sqb[i2][:, 0:n], bk(b, n), AF.Square, [f"B{b}"], [f"sqb{i2}"])
                b2 = bank()
                MM(bk(b2, n), blk, sqb[i2][:, 0:n], True, True, [f"sqb{i2}", "cb"], [f"B{b2}"])
                ACT(rr[i2][:, 0:n], bk(b2, n), AF.Ln, [f"B{b2}"], [f"rr{i2}"], bias=EPS, scale=1.0 / 64)
                ACT(rr[i2][:, 0:n], rr[i2][:, 0:n], AF.Exp, [f"rr{i2}"], [f"rr{i2}"], scale=-0.5)
                gcol = l2v[:, 1:2] if isk else l2v[:, 0:1]
                dst = K2[:, oc - 8, t0:t0 + n] if isk else QO[:, oc, t0:t0 + n]
                dname = f"K2_{cid}" if isk else f"QO{oc}_{cid}"
                if isctx:
                    STT("dve", dst, bk(b, n), gcol, rr[i2][:, 0:n], ALU.mult, ALU.mult, [f"B{b}", f"rr{i2}", "l2v"], [dname])
                    continue
                STT("dve", qn[i2][:, 0:n], bk(b, n), gcol, rr[i2][:, 0:n], ALU.mult, ALU.mult, [f"B{b}", f"rr{i2}", "l2v"], [f"qn{i2}"])
                b3 = bank()
                MM(bk(b3, n), rotm, qn[i2][:, 0:n], True, True, [f"qn{i2}", "cb"], [f"B{b3}"])
                TT("dve", t1[i2][:, 0:n], qn[i2][:, 0:n], cos_[:, t0:t0 + n], ALU.mult, [f"qn{i2}", "cos"], [f"t1{i2}"])
                TT("dve", t2[i2][:, 0:n], bk(b3, n), sin_[:, t0:t0 + n], ALU.mult, [f"B{b3}", "sin"], [f"t2{i2}"])
                TT("pool", dst, t1[i2][:, 0:n], t2[i2][:, 0:n], ALU.add, [f"t1{i2}", f"t2{i2}"], [dname])
        for i in range(21):
            isctx = i >= 19
            tok0 = 128 * i if not isctx else NW + 128 * (i - 19)
            rows = min(128, NW - tok0) if not isctx else 128
            xr = [f"xn{c}" for c in sorted({chunk_id(tok0), chunk_id(tok0 + rows - 1)})]
            b = bank()
            for kc in range(8):
                MM(ps[0:rows, b * 512:b * 512 + 256], xn[:, kc, tok0:tok0 + rows], wq[:, kc, 1536:1792], kc == 0, kc == 7,
                   ["wq2"] + xr, [f"B{b}"], sig=(kc == 7))
            CP(evac_eng(), VP[0:rows, i, 64:576].rearrange("p (k c) -> p k c", c=128)[:, :, 0:64],
               ps[0:rows, b * 512:b * 512 + 256].rearrange("p (k c) -> p k c", c=64), [f"B{b}"], ["VP"])
        P.barrier()
        HA.reset(vsave)
        mk = HA.alloc((6, 512), BF16)
        esk = HA.alloc((8,), F32)
        vcol = HA.alloc((19,), F32)
        den = HA.alloc((512,), F32)
        pt = [HA.alloc((1024,), BF16) for _ in range(3)]
        DMA(mk[:, 0:3, :].rearrange("p a b -> p (a b)"), maskda[:, :], [], ["mk"], q="pool")
        DMA(mk[:, 3:6, :].rearrange("p a b -> p (a b)"), maskdb[:, :], [], ["mk"], q="pool")
        ACT(esk[:, :], l2v[:, 2:10], AF.Exp, ["l2v"], ["esk"])
        for i in range(19):
            rows = min(128, NW - 128 * i)
            DMA(vcol[0:rows, i:i + 1], validd[0:1, 128 * i:128 * i + rows].rearrange("o (p x) -> (o p) x", x=1), [], ["vcol"])
        pctr = 0; sctr = 0
        scale = 1.0 / 8.0
        for c in range(8):
            kvh = c // 2
            for (t0, n) in CH_B:
                cid = chunk_id(t0)
                qname = f"QO{c}_{cid}"
                k0 = t0 - 128
                keys = []
                for j in range(6):
                    ks = k0 + 128 * j
                    if ks >= NW or 128 * (j - 1) - (n - 1) > 128:
                        continue
                    keys.append(("w", ks // 128, j))
                keys += [("c", 19, None), ("c", 20, None)]
                nk = len(keys)

                def tile_geom(ki):
                    kind, ti, j = keys[ki]
                    if kind == "w":
                        return min(128, NW - 128 * ti), 128 * ti
                    return 128, NW + 128 * (ti - 19)

                def emit_scores2(ki):
                    rows, kcol = tile_geom(ki)
                    sb = (ki % 2) * 2
                    kn = f"K2_{chunk_id(kcol)}"
                    MM(ps[0:rows, sb * 512:sb * 512 + n], K2[0:64, kvh, kcol:kcol + rows], QO[0:64, c, t0:t0 + n], True, True,
                       [kn, qname], [f"B{sb}"], sig=False, tp=(0, 0))
                    MM(ps[0:rows, (sb + 1) * 512:(sb + 1) * 512 + n], K2[64:128, kvh, kcol:kcol + rows], QO[64:128, c, t0:t0 + n], True, True,
                       [kn, qname], [f"B{sb + 1}"], tp=(64, 0))

                emit_scores2(0)
                for ki, (kind, ti, j) in enumerate(keys):
                    sb = (ki % 2) * 2
                    rows, kcol = tile_geom(ki)
                    if ki + 1 < nk:
                        emit_scores2(ki + 1)
                    p_ = pt[pctr % 3]; pn = f"pt{pctr % 3}"; pctr += 1
                    pv = p_[0:rows, :].rearrange("p (a b) -> p a b", a=2)[:, :, 0:n]
                    ACT(pv, ps[0:rows, sb * 512:sb * 512 + 1024].rearrange("p (a b) -> p a b", a=2)[:, :, 0:n],
                        AF.Exp, [f"B{sb}", f"B{sb + 1}"], [pn], scale=scale)
                    if kind == "w":
                        for m in range(2):
                            STT("dve", p_[0:rows, m * 512:m * 512 + n], p_[0:rows, m * 512:m * 512 + n], vcol[0:rows, ti:ti + 1],
                                mk[0:rows, j, 0:n], ALU.mult, ALU.mult, [pn, "vcol", "mk"], [pn])
                    first = ki == 0; last = ki == nk - 1
                    va = VP[0:rows, ti, 64 + 128 * kvh:64 + 128 * kvh + 128]
                    vb = VP[0:rows, ti, 128 * kvh:128 * kvh + 128]
                    MM(bk(4, n), va, p_[0:rows, 0:n], first, False, ["VP", pn], ["B4"], sig=False)
                    MM(bk(4, n), vb, p_[0:rows, 512:512 + n], False, last, ["VP", pn], ["B4"], sig=False)
                    MM(bk(6, n), E0[0:rows, :], p_[0:rows, 0:n], first, False, [pn, "cb"], ["B6"], sig=False)
                    MM(bk(6, n), E1[0:rows, :], p_[0:rows, 512:512 + n], False, last, [pn, "cb"], ["B6"])
                TS("dve", den[:, 0:n], bk(6, n), esk[:, c:c + 1], None, ALU.add, None, ["B6", "esk"], ["den"])
                RECIP(den[:, 0:n], den[:, 0:n], ["den"], ["den"])
                TT("dve", QO[:, c, t0:t0 + n], bk(4, n), den[:, 0:n], ALU.mult, ["B4", "den"], [qname])
        P.barrier()
        for fc in range(8):
            DMA(hreg[:, fc * NT:(fc + 1) * NT], hspill[:, fc * NT:(fc + 1) * NT], ["hspill"], [f"h{c}" for c in range(6)])
        TA.reset(tsave)
        wo = TA.alloc((8, 1024), BF16)
        load_w(wo, "l2_wo", "wo2")
        P.barrier()
        for (t0, n) in CH_B:
            cid = chunk_id(t0)
            for dc in range(8):
                b = bank()
                for kc in range(8):
                    MM(bk(b, n), wo[:, kc, dc * 128:(dc + 1) * 128], QO[:, kc, t0:t0 + n], kc == 0, kc == 7,
                       ["wo2", f"QO{kc}_{cid}"], [f"B{b}"], sig=(kc == 7))
                STT("dve", hT[:, dc, t0:t0 + n], bk(b, n), modap(l, 16 + dc, 0), hT[:, dc, t0:t0 + n], ALU.mult, ALU.add,
                    [f"B{b}", "modsb", f"h{cid}"], [f"h{cid}"])
        P.barrier()
        ffn(l, CH_B)


    CH_OWN = [(HALO + 512 * i, 512) for i in range(4)]

    def layer3():
        l = 3
        TA.reset()
        xn = TA.alloc((8, NW), BF16)
        U = TA.alloc((8, NW), BF16)
        tsave = TA.off
        make_xn(xn, l, 0, CH_B, TA)
        P.barrier()
        TA.reset(tsave)
        glu = TA.alloc((NW,), F32)
        accA = TA.alloc((NW,), F32)
        accB = TA.alloc((NW,), F32)
        wst = [TA.alloc((8, 2, 128), BF16) for _ in range(2)]
        sg = TA.alloc((512,), F32)
        o0, o1 = HALO, HALO + OWN
        for fc in range(8):
            w_ = wst[fc % 2]; wn = f"wst{fc % 2}"
            for kc in range(8):
                src, g = wsrc("l3_pw1", kc)
                for j in range(2):
                    DMA(w_[:, kc, j, :], src[:, j * 1024 + fc * 128:j * 1024 + (fc + 1) * 128], [g], [wn], q="pool")
            for (t0, n) in CH_B:
                cid = chunk_id(t0)
                ba = bank(); bg = bank()
                for j, b in ((0, ba), (1, bg)):
                    for kc in range(8):
                        MM(bk(b, n), w_[:, kc, j, :], xn[:, kc, t0:t0 + n], kc == 0, kc == 7, [wn, f"xn{cid}"], [f"B{b}"], sig=(kc == 7))
                ACT(sg[:, 0:n], bk(bg, n), AF.Sigmoid, [f"B{bg}", "l3v"], ["sg"], bias=l3v[:, fc, 1:2])
                STT("dve", glu[:, t0:t0 + n], bk(ba, n), l3v[:, fc, 0:1], sg[:, 0:n], ALU.add, ALU.mult, [f"B{ba}", "sg", "l3v"], ["glu"])
            TT("dve", glu[:, 128:2208], glu[:, 128:2208], validb[:, 128:2208], ALU.mult, ["glu", "validb"], ["glu"])
            for k in range(31):
                src_ = glu[:, o0 + k - 15:o1 + k - 15]
                wk = l3v[:, fc, 8 + k:9 + k]
                if k == 0:
                    TS("dve", accA[:, o0:o1], src_, wk, None, ALU.mult, None, ["glu", "l3v"], ["accA"])
                else:
                    STT("dve", accA[:, o0:o1], src_, wk, accA[:, o0:o1], ALU.mult, ALU.add, ["glu", "l3v", "accA"], ["accA"])
            TS("dve", U[:, fc, o0:o1], accA[:, o0:o1], l3v[:, fc, 2:3], None, ALU.add, None, ["accA", "l3v"], ["U"])
        P.barrier()
        TA.reset(tsave)
        usq = TA.alloc((8, 512), BF16)
        mean = TA.alloc((512,), F32); msq = TA.alloc((512,), F32); rstd = TA.alloc((512,), F32)
        tmp = [TA.alloc((512,), F32) for _ in range(2)]
        tmpb = [TA.alloc((512,), F32) for _ in range(2)]
        wo = TA.alloc((8, 1024), BF16)
        load_w(wo, "l3_pw2", "wo3")
        for (t0, n) in CH_OWN:
            cid = chunk_id(t0)
            ACT(usq[:, :, 0:n], U[:, :, t0:t0 + n], AF.Square, ["U"], ["usq"])
            b1 = bank()
            for fc in range(8):
                MM(bk(b1, n), onesb, U[:, fc, t0:t0 + n], fc == 0, fc == 7, ["U", "cb"], [f"B{b1}"], sig=(fc == 7))
            b2 = bank()
            for fc in range(8):
                MM(bk(b2, n), onesb, usq[:, fc, 0:n], fc == 0, fc == 7, ["usq", "cb"], [f"B{b2}"], sig=(fc == 7))
            ACT(mean[:, 0:n], bk(b1, n), AF.Identity, [f"B{b1}"], ["mean"], scale=1.0 / D)
            TT("dve", msq[:, 0:n], mean[:, 0:n], mean[:, 0:n], ALU.mult, ["mean"], ["msq"])
            STT("dve", rstd[:, 0:n], bk(b2, n), 1.0 / D, msq[:, 0:n], ALU.mult, ALU.subtract, [f"B{b2}", "msq"], ["rstd"])
            ACT(rstd[:, 0:n], rstd[:, 0:n], AF.Ln, ["rstd"], ["rstd"], bias=EPS)
            ACT(rstd[:, 0:n], rstd[:, 0:n], AF.Exp, ["rstd"], ["rstd"], scale=-0.5)
            for fc in range(8):
                t_ = tmp[fc % 2]; tn = f"tmp{fc % 2}"
                TT("dve", t_[:, 0:n], U[:, fc, t0:t0 + n], mean[:, 0:n], ALU.subtract, ["U", "mean"], [tn])
                TT("dve", t_[:, 0:n], t_[:, 0:n], rstd[:, 0:n], ALU.mult, [tn, "rstd"], [tn])
                ACT(U[:, fc, t0:t0 + n], t_[:, 0:n], AF.Silu, [tn, "l3v"], [f"S{cid}"], bias=l3v[:, fc, 4:5], scale=l3v[:, fc, 3:4])
        for (t0, n) in CH_OWN:
            cid = chunk_id(t0)
            for dc in range(8):
                b = bank()
                for kc in range(8):
                    MM(bk(b, n), wo[:, kc, dc * 128:(dc + 1) * 128], U[:, kc, t0:t0 + n], kc == 0, kc == 7,
                       ["wo3", f"S{cid}", "U"], [f"B{b}"], sig=(kc == 7))
                tb = tmpb[dc % 2]; tbn = f"tmpb{dc % 2}"
                ACT(tb[:, 0:n], bk(b, n), AF.Identity, [f"B{b}", "l3v"], [tbn], bias=l3v[:, dc, 5:6])
                STT("dve", hT[:, dc, t0:t0 + n], tb[:, 0:n], modap(l, 16 + dc, 0), hT[:, dc, t0:t0 + n], ALU.mult, ALU.add,
                    [tbn, "modsb", f"h{cid}"], [f"h{cid}"])
        P.barrier()
        ffn(l, CH_OWN)

    TA.reset()
    load_xT(TA)
    P.barrier()
    if nlayers >= 1:
        layer0()
    if nlayers >= 2:
        layer1()
    if nlayers >= 3:
        layer2()
    if nlayers >= 4:
        layer3()

    P.barrier()
    TA.reset()
    ot = [TA.alloc((1024,), F32) for _ in range(2)]
    outs = []
    for i in range(16):
        tok0 = HALO + 128 * i
        o2 = ot[i % 2]
        hr = [f"h{c}" for c in sorted({chunk_id(tok0), chunk_id(tok0 + 127)})]
        for half in range(2):
            b = bank()
            for f4 in range(4):
                fc = half * 4 + f4
                MM(ps[:, b * 512 + f4 * 128:b * 512 + (f4 + 1) * 128], hT[:, fc, tok0:tok0 + 128], ident, True, True,
                   hr + ["constf"], [f"B{b}"], sig=(f4 == 3))
            CP(evac_eng(), o2[:, half * 512:(half + 1) * 512], bk(b), [f"B{b}"], [f"ot{i % 2}"])
        outs.append(DMA(outd[i][:, :], o2[:, :], [f"ot{i % 2}"], ["out"]))
    nsem = P.finalize(final_wait_ops=outs)
    return nc, P, nsem


def _rope_tables():
    n_freq = 16
    inv = (10000.0 ** (-np.arange(n_freq, dtype=np.float32) / np.float32(n_freq))).astype(np.float32)
    t = np.arange(SEQ)
    rows = (t // 64).astype(np.float32); cols = (t % 64).astype(np.float32)
    ang_r = rows[:, None] * inv[None, :]; ang_c = cols[:, None] * inv[None, :]
    ang = np.concatenate([ang_r, ang_r, ang_c, ang_c], axis=-1).astype(np.float32)
    return np.cos(ang).astype(np.float32), np.sin(ang).astype(np.float32)


def _consts():
    ident = np.eye(128, dtype=np.float32)
    ones = np.ones((128, 128), np.float32)
    blk = np.zeros((128, 128), np.float32); blk[:64, :64] = 1; blk[64:, 64:] = 1
    R = np.zeros((128, 128), np.float32)
    for base in (0, 64):
        for j in range(16):
            R[base + 16 + j, base + j] = -1.0
            R[base + j, base + 16 + j] = 1.0
            R[base + 48 + j, base + 32 + j] = -1.0
            R[base + 32 + j, base + 48 + j] = 1.0
    E0 = np.zeros((128, 128), np.float32); E0[:, :64] = 1
    E1 = np.zeros((128, 128), np.float32); E1[:, 64:] = 1
    constf = np.concatenate([ident, ones], 1)
    constb = np.concatenate([ones, blk, R, E0, E1], 1)
    p = np.arange(128)[:, None]; f = np.arange(512)[None, :]
    masks = [(np.abs(128 * (j - 1) + p - f) <= 128).astype(np.float32) for j in range(6)]
    return constf, constb, np.concatenate(masks, 1)


_CACHE = {}


def kernel(**inp):
    nlayers = int(inp.pop("_nlayers", NLAYERS_DEFAULT))
    f32 = lambda a: np.ascontiguousarray(np.asarray(a, dtype=np.float32))
    x = f32(inp["x"])[0]; ctx = f32(inp["ctx"])[0]
    xpad = np.zeros((SEQ + 2 * HALO, D), np.float32); xpad[HALO:HALO + SEQ] = x
    cosf, sinf = _rope_tables()
    cpad = np.ones((SEQ + 2 * HALO, 64), np.float32); cpad[HALO:HALO + SEQ] = cosf
    spad = np.zeros((SEQ + 2 * HALO, 64), np.float32); spad[HALO:HALO + SEQ] = sinf
    vpad = np.zeros((SEQ + 2 * HALO,), np.float32); vpad[HALO:HALO + SEQ] = 1.0
    constf, constb, maskd = _consts()
    cvec = np.stack([f32(inp["c"])[0].reshape(8, 128).T, f32(inp["c_ctx"]).reshape(8, 128).T], -1)
    ada_w = f32(inp["ada_w"]); ada_b = f32(inp["ada_b"])
    normd = np.stack([f32(inp["norm1"]).reshape(4, 8, 128), f32(inp["norm2"]).reshape(4, 8, 128)], 1)
    normd = np.ascontiguousarray(normd.transpose(3, 0, 1, 2))
    w = f32(inp["diff_w_qkv"])[0]
    wq = w[:, :1024].reshape(D, 2, 8, 64).transpose(0, 2, 1, 3).reshape(D, 1024)
    wk = w[:, 1024:2048].reshape(D, 2, 8, 64).transpose(0, 2, 1, 3).reshape(D, 1024)
    l0_qkv = np.ascontiguousarray(np.concatenate([wq, wk, w[:, 2048:]], 1))
    l0_vec = np.zeros((128, 8), np.float32)
    l0_vec[:, 0] = np.tile(f32(inp["diff_q_norm"])[0], 2); l0_vec[:, 1] = np.tile(f32(inp["diff_k_norm"])[0], 2)
    l0_vec[:, 2] = f32(inp["diff_subln"])[0]
    for j, nm in enumerate(["diff_lam_q1", "diff_lam_k1", "diff_lam_q2", "diff_lam_k2"]):
        l0_vec[:64, 3 + j] = f32(inp[nm])[0]
    w2 = f32(inp["win_w_qkv"])[0]
    k2 = w2[:, 1024:1280].reshape(D, 4, 1, 64).repeat(2, axis=2).reshape(D, 512)
    l2_qkv = np.ascontiguousarray(np.concatenate([w2[:, :1024], k2, w2[:, 1280:]], 1))
    l2_vec = np.zeros((128, 16), np.float32)
    l2_vec[:, 0] = np.tile(f32(inp["win_q_norm"])[0], 2); l2_vec[:, 1] = np.tile(f32(inp["win_k_norm"])[0], 2)
    sink = f32(inp["win_sink"])[0]
    for c in range(8):
        l2_vec[:64, 2 + c] = sink[2 * c]; l2_vec[64:, 2 + c] = sink[2 * c + 1]
    l1_conv = np.ascontiguousarray(f32(inp["sc_conv_w"])[0].reshape(3, 8, 128).transpose(2, 1, 0))
    l3_vec = np.zeros((128, 8, 40), np.float32)
    pb = f32(inp["cf_b_pw1"])[0]
    l3_vec[:, :, 0] = pb[:1024].reshape(8, 128).T; l3_vec[:, :, 1] = pb[1024:].reshape(8, 128).T
    l3_vec[:, :, 2] = f32(inp["cf_dw_b"])[0].reshape(8, 128).T
    l3_vec[:, :, 3] = f32(inp["cf_ln_g"])[0].reshape(8, 128).T
    l3_vec[:, :, 4] = f32(inp["cf_ln_b"])[0].reshape(8, 128).T
    l3_vec[:, :, 5] = f32(inp["cf_b_pw2"])[0].reshape(8, 128).T
    l3_vec[:, :, 8:39] = f32(inp["cf_dw_w"])[0].reshape(31, 8, 128).transpose(2, 1, 0)
    wts = {"l0_qkv": l0_qkv, "l0_wo": f32(inp["diff_w_o"])[0],
           "l1_win": f32(inp["sc_w_in"])[0], "l1_wout": f32(inp["sc_w_out"])[0],
           "l2_qkv": l2_qkv, "l2_wo": f32(inp["win_w_o"])[0],
           "l3_pw1": f32(inp["cf_w_pw1"])[0], "l3_pw2": f32(inp["cf_w_pw2"])[0]}
    gu = f32(inp["ffn_w_gate_up"]); dn = f32(inp["ffn_w_down"])
    for l in range(4):
        wts[f"ffn_gu{l}"] = gu[l]
        dpad = np.zeros((24 * 128, D), np.float32); dpad[:DFF] = dn[l]
        wts[f"ffn_d{l}"] = dpad

    def pack(spec, r):
        parts = []
        for name, n, cpr in spec:
            parts.append(wts[name][r * cpr * 128:(r + 1) * cpr * 128, :].reshape(-1))
        return np.ascontiguousarray(np.concatenate(parts).reshape(-1, 512))

    shared = dict(ctxa=np.ascontiguousarray(ctx[:128]), ctxb=np.ascontiguousarray(ctx[128:]), cvec=f32(cvec), normd=normd,
                  constf=constf, constb=constb, maskda=np.ascontiguousarray(maskd[:, :1536]), maskdb=np.ascontiguousarray(maskd[:, 1536:]),
                  l0_vec=l0_vec, l1_conv=l1_conv, l2_vec=l2_vec, l3_vec=l3_vec)
    in_maps = []
    for i in range(NCORES):
        s0 = OWN * i
        m = dict(shared)
        m["xw"] = np.ascontiguousarray(xpad[s0:s0 + NW])
        m["cosd"] = np.ascontiguousarray(np.tile(cpad[s0:s0 + NW].T, (2, 1)))
        m["sind"] = np.ascontiguousarray(np.tile(spad[s0:s0 + NW].T, (2, 1)))
        m["validd"] = np.ascontiguousarray(np.broadcast_to(vpad[s0:s0 + NW][None, :], (128, NW)))
        m["wA"] = pack(WSPEC_A, i); m["wB"] = pack(WSPEC_B, i)
        m["adaw"] = np.ascontiguousarray(ada_w[:, :, i * 768:(i + 1) * 768])
        m["adab"] = np.ascontiguousarray(ada_b[:, i * 768:(i + 1) * 768].reshape(4, 6, 128).transpose(2, 0, 1).reshape(128, 24))
        in_maps.append(m)
    key = nlayers
    if key not in _CACHE:
        _CACHE[key] = build_program(nlayers)[0]
    nc = _CACHE[key]
    res = run_bass_kernel_spmd(nc, in_maps, core_ids=list(range(NCORES)))
    out = np.concatenate([np.asarray(r[f"out{i}"], dtype=np.float32) for r in res.results for i in range(16)], 0)
    return out.reshape(1, SEQ, D)
```
